# Optimizing a Trainium2 kernel written in Bass

```python
import math
import jax, jax.numpy as jnp
from jax import lax
import numpy as np

D_MODEL = 1024
BATCH = 32
SEQ = 256
DEPTH = 2
DEC_BATCH = 8
DEC_SEQ = 2048
PAST_LEN = 256

GRID_W = 64
N_EVEN = (DEPTH + 1) // 2
N_ODD = DEPTH // 2
MIX_W = D_MODEL
ATTN_W = MIX_W // 2
CONV_W = MIX_W - ATTN_W
N_HEADS_A = 4
V_HEAD = ATTN_W // N_HEADS_A
HALF_HEAD = V_HEAD // 2
QK_HEAD = 2 * HALF_HEAD
ROPE_AXIS = HALF_HEAD // 2
ROPE_BASE = 10000.0
Q_BLOCK = 128
EVEN_IN_W = 3 * ATTN_W + 3 * CONV_W
SPLIT_EVEN = (ATTN_W, 2 * ATTN_W, 3 * ATTN_W, 3 * ATTN_W + CONV_W, 3 * ATTN_W + 2 * CONV_W)
POOL_W = MIX_W // 2
FOURIER_W = MIX_W - POOL_W
POOL_WINDOWS = (2, 4, 8, 16)
N_POOL_GROUPS = len(POOL_WINDOWS)
POOL_GROUP = POOL_W // N_POOL_GROUPS
N_FOURIER_GROUPS = 4
FOURIER_GROUP = FOURIER_W // N_FOURIER_GROUPS
D_FF = -(-8 * D_MODEL // (3 * 256)) * 256
EPS = 1e-6

kernel_name = 'hybrid_diffattn_shortconv_pool_fourier_prefix_step'


def rmsnorm(x, g):
    xf = x.astype(jnp.float32)
    y = xf * lax.rsqrt(jnp.mean(xf * xf, axis=-1, keepdims=True) + EPS)
    return (y * g.astype(jnp.float32)).astype(x.dtype)


def modulate(x, g, shift, scale):
    return rmsnorm(x, g) * (1 + scale) + shift


def gated_residual(x, y, g, gate):
    return x + gate * rmsnorm(y, g)


def swiglu(h, w_gate, w_up, w_down):
    return (jax.nn.silu(h @ w_gate) * (h @ w_up)) @ w_down


def split_mod(mod):
    return jnp.split(mod, 6, axis=-1)


def axial_rope_tables(n_tok):
    rows = n_tok // GRID_W
    row = jnp.repeat(jnp.arange(rows), GRID_W).astype(jnp.float32)
    col = jnp.tile(jnp.arange(GRID_W), rows).astype(jnp.float32)
    inv = ROPE_BASE ** (-jnp.arange(0, ROPE_AXIS, 2, dtype=jnp.float32) / ROPE_AXIS)
    ang_r = row[:, None] * inv[None, :]
    ang_c = col[:, None] * inv[None, :]
    ang = jnp.concatenate([ang_r, ang_r, ang_c, ang_c], axis=-1)
    return jnp.cos(ang), jnp.sin(ang)


def rotate_axial(x):
    half = ROPE_AXIS // 2
    xr, xc = x[..., :ROPE_AXIS], x[..., ROPE_AXIS:]
    return jnp.concatenate([-xr[..., half:], xr[..., :half], -xc[..., half:], xc[..., :half]], axis=-1)


def apply_rope(x, cos, sin):
    xf = x.astype(jnp.float32)
    cb = cos[:, None, None, :]
    sb = sin[:, None, None, :]
    return (xf * cb + rotate_axial(xf) * sb).astype(x.dtype)


def diff_lambda(lp, lam_init):
    lp = lp.astype(jnp.float32)
    return jnp.exp(jnp.sum(lp[0] * lp[1])) - jnp.exp(jnp.sum(lp[2] * lp[3])) + lam_init


def diff_attention(q, k, v, lam):
    bsz, nh, lq = q.shape[0], q.shape[1], q.shape[2]
    nb = lq // Q_BLOCK
    scale = HALF_HEAD ** -0.5
    qb = q.reshape(bsz, nh, nb, Q_BLOCK, 2, HALF_HEAD).transpose(2, 0, 1, 3, 4, 5)

    def one_block(qblk):
        s = jnp.einsum('bhqcd,bhkcd->cbhqk', qblk, k).astype(jnp.float32) * scale
        p = jax.nn.softmax(s, axis=-1)
        w = (p[0] - lam * p[1]).astype(v.dtype)
        return jnp.einsum('bhqk,bhkd->bhqd', w, v)

    o = lax.map(one_block, qb)
    return o.transpose(1, 2, 0, 3, 4).reshape(bsz, nh, lq, V_HEAD)


def even_mixer(h, w_in, conv_w, w_out, sub_g, lam, lam_init, rope, ctx_k, ctx_v):
    bsz, n, _ = h.shape
    q, k, v, gate_b, gate_c, xin = jnp.split(h @ w_in, SPLIT_EVEN, axis=-1)
    q = q.reshape(bsz, n, N_HEADS_A, 2, HALF_HEAD)
    k = k.reshape(bsz, n, N_HEADS_A, 2, HALF_HEAD)
    if rope is not None:
        cos, sin = rope
        q = apply_rope(q, cos, sin)
        k = apply_rope(k, cos, sin)
    q = q.transpose(0, 2, 1, 3, 4)
    k = k.transpose(0, 2, 1, 3, 4)
    v = v.reshape(bsz, n, N_HEADS_A, V_HEAD).transpose(0, 2, 1, 3)
    if ctx_k is None:
        k_all, v_all = k, v
    else:
        ck = ctx_k.reshape(bsz, N_HEADS_A, ctx_k.shape[2], 2, HALF_HEAD)
        k_all = jnp.concatenate([ck, k], axis=2)
        v_all = jnp.concatenate([ctx_v, v], axis=2)
    o = diff_attention(q, k_all, v_all, lam)
    o = rmsnorm(o, sub_g) * (1.0 - lam_init)
    o = o.transpose(0, 2, 1, 3).reshape(bsz, n, ATTN_W)
    z = gate_c * xin
    zp = jnp.pad(z, ((0, 0), (1, 1), (0, 0)))
    conv = conv_w[0] * zp[:, :-2] + conv_w[1] * zp[:, 1:-1] + conv_w[2] * zp[:, 2:]
    y = jnp.concatenate([o, gate_b * conv], axis=-1) @ w_out
    return y, k.reshape(bsz, N_HEADS_A, n, QK_HEAD), v


def multi_scale_pool(x):
    bsz, n, ch = x.shape
    xf = x.astype(jnp.float32)
    csum = jnp.concatenate([jnp.zeros((bsz, 1, ch), jnp.float32), jnp.cumsum(xf, axis=1)], axis=1)
    t = jnp.arange(n)
    outs = []
    for g, win in enumerate(POOL_WINDOWS):
        lo = jnp.clip(t - win // 2, 0, n)
        hi = jnp.clip(t - win // 2 + win, 0, n)
        sg = csum[..., g * POOL_GROUP:(g + 1) * POOL_GROUP]
        cnt = (hi - lo).astype(jnp.float32)[:, None]
        outs.append((sg[:, hi] - sg[:, lo]) / cnt)
    return jnp.concatenate(outs, axis=-1).astype(x.dtype)


def odd_mixer(h, w_in, w_pool, pool_scale, w_four, w_out):
    bsz, n, _ = h.shape
    u = h @ w_in
    up, uf = u[..., :POOL_W], u[..., POOL_W:]
    pooled = (multi_scale_pool(up) - up).reshape(bsz, n, N_POOL_GROUPS, POOL_GROUP)
    pc = jnp.einsum('blgc,gcd->blgd', pooled, w_pool).reshape(bsz, n, POOL_W) * pool_scale
    fg = uf.reshape(bsz, n, N_FOURIER_GROUPS, FOURIER_GROUP).astype(jnp.float32)
    four = jnp.fft.fft2(fg, axes=(1, 3), norm='ortho').real.astype(h.dtype)
    fc = jnp.einsum('blgc,gcd->blgd', four, w_four).reshape(bsz, n, FOURIER_W)
    return jnp.concatenate([pc, fc], axis=-1) @ w_out


def setup_inputs(seed: int = 0) -> dict:
    key = jax.random.key(seed)
    ks = jax.random.split(key, 24)
    f32 = jnp.float32
    D = D_MODEL

    def nrm(k, shape, scale):
        return jax.random.normal(k, shape, f32) * scale

    return {
        'x_prompt': nrm(ks[0], (BATCH, SEQ, D), 1.0),
        'x_sample': nrm(ks[1], (DEC_BATCH, DEC_SEQ, D), 1.0),
        'cache_k': nrm(ks[2], (DEC_BATCH, N_EVEN, N_HEADS_A, PAST_LEN, QK_HEAD), 1.0),
        'cache_v': nrm(ks[3], (DEC_BATCH, N_EVEN, N_HEADS_A, PAST_LEN, V_HEAD), 1.0),
        'c': nrm(ks[4], (DEC_BATCH, D), 1.0),
        'c_ctx': nrm(ks[5], (D,), 1.0),
        'w_mod': nrm(ks[6], (DEPTH, D, 6 * D), 0.5 * D ** -0.5),
        'b_mod': nrm(ks[7], (DEPTH, 6 * D), 0.02),
        'norm_g': 1.0 + nrm(ks[8], (DEPTH, 4, D), 0.02),
        'w_in_even': nrm(ks[9], (N_EVEN, D, EVEN_IN_W), D ** -0.5),
        'lam_params': nrm(ks[10], (N_EVEN, 4, HALF_HEAD), 0.1),
        'subln_g': 1.0 + nrm(ks[11], (N_EVEN, V_HEAD), 0.02),
        'conv_w': nrm(ks[12], (N_EVEN, 3, CONV_W), 3 ** -0.5),
        'w_out_even': nrm(ks[13], (N_EVEN, MIX_W, D), MIX_W ** -0.5),
        'w_in_odd': nrm(ks[14], (N_ODD, D, MIX_W), D ** -0.5),
        'w_pool': nrm(ks[15], (N_ODD, N_POOL_GROUPS, POOL_GROUP, POOL_GROUP), POOL_GROUP ** -0.5),
        'pool_scale': 1.0 + nrm(ks[16], (N_ODD, POOL_W), 0.02),
        'w_fourier': nrm(ks[17], (N_ODD, N_FOURIER_GROUPS, FOURIER_GROUP, FOURIER_GROUP), FOURIER_GROUP ** -0.5),
        'w_out_odd': nrm(ks[18], (N_ODD, MIX_W, D), MIX_W ** -0.5),
        'w_gate': nrm(ks[19], (DEPTH, D, D_FF), D ** -0.5),
        'w_up': nrm(ks[20], (DEPTH, D, D_FF), D ** -0.5),
        'w_down': nrm(ks[21], (DEPTH, D_FF, D), D_FF ** -0.5),
    }


def reference(x_prompt, x_sample, cache_k, cache_v, c, c_ctx, w_mod, b_mod, norm_g,
              w_in_even, lam_params, subln_g, conv_w, w_out_even,
              w_in_odd, w_pool, pool_scale, w_fourier, w_out_odd,
              w_gate, w_up, w_down):
    xp, xs = x_prompt, x_sample
    rope = axial_rope_tables(xs.shape[1])
    silu_c = jax.nn.silu(c)
    silu_ctx = jax.nn.silu(c_ctx)
    new_k, new_v = [], []
    for l in range(DEPTH):
        mp = split_mod(silu_ctx @ w_mod[l] + b_mod[l])
        ms = split_mod((silu_c @ w_mod[l] + b_mod[l])[:, None, :])
        g = norm_g[l]
        hp = modulate(xp, g[0], mp[0], mp[1])
        hs = modulate(xs, g[0], ms[0], ms[1])
        if l % 2 == 0:
            i = l // 2
            lam_init = 0.8 - 0.6 * math.exp(-0.3 * l)
            lam = diff_lambda(lam_params[i], lam_init)
            yp, kp, vp = even_mixer(hp, w_in_even[i], conv_w[i], w_out_even[i], subln_g[i],
                                    lam, lam_init, None, None, None)
            ys, _, _ = even_mixer(hs, w_in_even[i], conv_w[i], w_out_even[i], subln_g[i],
                                  lam, lam_init, rope, cache_k[:, i], cache_v[:, i])
            new_k.append(kp)
            new_v.append(vp)
        else:
            i = l // 2
            yp = odd_mixer(hp, w_in_odd[i], w_pool[i], pool_scale[i], w_fourier[i], w_out_odd[i])
            ys = odd_mixer(hs, w_in_odd[i], w_pool[i], pool_scale[i], w_fourier[i], w_out_odd[i])
        xp = gated_residual(xp, yp, g[1], mp[2])
        xs = gated_residual(xs, ys, g[1], ms[2])
        hp = modulate(xp, g[2], mp[3], mp[4])
        hs = modulate(xs, g[2], ms[3], ms[4])
        xp = gated_residual(xp, swiglu(hp, w_gate[l], w_up[l], w_down[l]), g[3], mp[5])
        xs = gated_residual(xs, swiglu(hs, w_gate[l], w_up[l], w_down[l]), g[3], ms[5])
    new_k_arr = jnp.stack(new_k, axis=1)
    new_v_arr = jnp.stack(new_v, axis=1)
    return (xp, xs, new_k_arr, new_v_arr)
```

```python
import math
from contextlib import ExitStack

import numpy as np
import ml_dtypes

import concourse.bass as bass
import concourse.mybir as mybir
from concourse.bass_utils import run_bass_kernel_spmd

F32 = mybir.dt.float32
BF16 = mybir.dt.bfloat16
U8 = mybir.dt.uint8
AF = mybir.ActivationFunctionType
ALU = mybir.AluOpType
AX = mybir.AxisListType

D = 1024
KC = 8
DFF = 2816
FC = 22
NCORES = 8
EPS = 1e-6
LAM_INIT = 0.8 - 0.6 * math.exp(-0.3 * 0)
ARENA_BYTES = 212000

import os
SKIP = set(os.environ.get('K_SKIP', '').split(','))
DEBUG = False
CASTMODE = os.environ.get('K_CAST', 'dma')
FULLBAR = os.environ.get('K_FULLBAR', '') == '1'
SAMEENG = False
STAGE = 99


class Buf:
    __slots__ = ("name", "w", "w_eng", "r", "pend", "excl")

    def __init__(self, name=""):
        self.name = name
        self.excl = False
        self.w = None
        self.w_eng = None
        self.r = []
        self.pend = None


class Tile:
    def __init__(self, ap, name):
        self.ap = ap
        self.name = name
        self.subs = {}
        self.inh = []
        self.abs_lo = None

    def b(self, key=None):
        if key not in self.subs:
            bf = Buf("%s/%s" % (self.name, key))
            bf.r = list(self.inh)
            self.subs[key] = bf
        return self.subs[key]


class Prog:
    ENGS = ("pe", "act", "dve", "pool", "sp")
    NDMA = 16

    def __init__(self, nc, es):
        self.nc = nc
        self.streams = {e: [] for e in self.ENGS}
        self.sem = {}
        self.cnt = {}
        for e in ("pe", "act", "dve", "pool"):
            self.sem[e] = es.enter_context(nc.semaphore("s_" + e))
            self.cnt[e] = 0
        for i in range(self.NDMA):
            for pre in ("dmaH", "dmaS"):
                n = "%s%d" % (pre, i)
                self.sem[n] = es.enter_context(nc.semaphore("s_" + n))
                self.cnt[n] = 0
        self.dma_rr = {"dmaH": 0, "dmaS": 0}
        self.waited = {e: {} for e in self.ENGS}
        self.pending = {e: [] for e in self.ENGS}
        self.nops = 0
        self.marks = []

    def _need(self, eng, need, tok):
        s, v = tok
        if self.waited[eng].get(s, 0) >= v:
            return
        if need.get(s, 0) < v:
            need[s] = v

    def op(self, eng, fn, r=(), w=(), sig=True, dma=False):
        need = {}
        for b in r:
            if b.pend is not None and b.pend != eng:
                raise RuntimeError("read of buffer with pending unsignalled write: " + b.name)
            if b.w is not None:
                if not (b.pend == eng):
                    self._need(eng, need, b.w)
            if b.excl:
                for (t, e2) in b.r:
                    if e2 != eng:
                        self._need(eng, need, t)
        for b in w:
            if b.pend is not None and (b.pend != eng or dma):
                raise RuntimeError("write to buffer with pending unsignalled access: " + b.name)
            if b.w is not None and (b.w_eng != eng or dma or b.w_eng == "dma" or (SAMEENG and eng != "pe")):
                self._need(eng, need, b.w)
            for (t, e2) in b.r:
                if e2 != eng or dma or (SAMEENG and eng != "pe"):
                    self._need(eng, need, t)
        signal = None
        if dma:
            pre = "dmaS" if eng == "pool" else "dmaH"
            sname = "%s%d" % (pre, self.dma_rr[pre])
            self.dma_rr[pre] = (self.dma_rr[pre] + 1) % self.NDMA
            if self.cnt[sname] > 0:
                self._need(eng, need, (sname, self.cnt[sname]))
            self.cnt[sname] += 16
            tok = (sname, self.cnt[sname])
            signal = (sname, 16)
            peng = "dma"
        elif sig:
            self.cnt[eng] += 1
            tok = (eng, self.cnt[eng])
            signal = (eng, 1)
            peng = eng
        else:
            tok = None
            peng = eng
        waits = sorted(need.items())
        for s, v in waits:
            self.waited[eng][s] = v
        self.streams[eng].append((fn, waits, signal))
        self.nops += 1
        if tok is None:
            for b in r:
                self.pending[eng].append((b, "r"))
            for b in w:
                self.pending[eng].append((b, "w"))
                b.pend = eng
            return None
        allr = [(b, "r") for b in r]
        allw = [(b, "w") for b in w]
        if not dma:
            allr += [(b, k) for (b, k) in self.pending[eng] if k == "r"]
            allw += [(b, k) for (b, k) in self.pending[eng] if k == "w"]
            self.pending[eng] = []
        for b, _ in allw:
            b.w = tok
            b.w_eng = peng
            b.r = []
            b.pend = None
        for b, _ in allr:
            b.r.append((tok, peng))
        return tok

    def barrier(self):
        import inspect
        fr = inspect.stack()[1]
        self.marks.append((fr.function, fr.lineno, sum(1 for o in self.streams["pe"] if o[0] is not None)))
        toks = [(s, c) for s, c in self.cnt.items() if c > 0]
        for e in self.ENGS:
            need = {}
            for t in toks:
                self._need(e, need, t)
            waits = sorted(need.items())
            if waits:
                for s, v in waits:
                    self.waited[e][s] = v
                self.streams[e].append((None, waits, None))

    def replay(self, eng, e):
        for fn, waits, signal in self.streams[eng]:
            for s, v in waits:
                e.wait_ge(self.sem[s], v)
            if fn is None:
                continue
            ins = fn(e)
            if signal is not None:
                ins.then_inc(self.sem[signal[0]], signal[1])


class Arena:
    def __init__(self, ar, nbytes):
        self.ar = ar
        self.top = 0
        self.size = nbytes
        self.peak = 0
        self.reg = []
        self.prog = None

    def register(self, lo, hi, tile):
        if self.prog is not None:
            for e_, lst in self.prog.pending.items():
                if lst:
                    raise RuntimeError("tile allocation with unsignalled ops pending on " + e_)
        inh = {}
        for (l2, h2, t2) in self.reg:
            if l2 < hi and lo < h2:
                toks = list(t2.inh)
                for b in t2.subs.values():
                    if b.w is not None:
                        toks.append((b.w, b.w_eng))
                    toks.extend(b.r)
                for (tk, e2) in toks:
                    sname, v = tk
                    if sname not in inh or inh[sname][0][1] < v:
                        inh[sname] = ((sname, v), e2)
        tile.inh = list(inh.values())
        tile.abs_lo = lo
        self.reg = [(l2, h2, t2) for (l2, h2, t2) in self.reg if not (lo <= l2 and h2 <= hi)]
        self.reg.append((lo, hi, tile))

    def carve(self, parent, lo_el, hi_el, c, name, esz=2):
        ap = parent.ap[:, lo_el:hi_el].rearrange("p (c t) -> p c t", c=c)
        t_ = Tile(ap, name)
        self.register(parent.abs_lo + esz * lo_el, parent.abs_lo + esz * hi_el, t_)
        return t_

    def mark(self):
        return self.top

    def release(self, m):
        self.top = m

    def t(self, name, shape, dtype, parts=128):
        esz = 4 if dtype == F32 else 2
        n = 1
        for s in shape:
            n *= s
        nb = (n * esz + 63) // 64 * 64
        if self.top + nb > self.size:
            raise RuntimeError("arena overflow at %s: top=%d need=%d" % (name, self.top, nb))
        ap = self.ar[0:parts, self.top:self.top + n * esz].bitcast(dtype)
        lo_ = self.top
        self.top += nb
        self.peak = max(self.peak, self.top)
        if len(shape) == 2:
            ap = ap.rearrange("p (a b) -> p a b", a=shape[0])
        elif len(shape) == 3:
            ap = ap.rearrange("p (a b c) -> p a b c", a=shape[0], b=shape[1])
        elif len(shape) == 4:
            ap = ap.rearrange("p (a b c d) -> p a b c d", a=shape[0], b=shape[1], c=shape[2])
        t_ = Tile(ap, name)
        self.register(lo_, lo_ + nb, t_)
        return t_


def _bf(a):
    return np.ascontiguousarray(a.astype(np.float32)).astype(ml_dtypes.bfloat16)


def make_constants():
    c = {}
    c["ident"] = np.eye(128, dtype=np.float32)
    P = np.zeros((128, 128), np.float32)
    for blk in range(4):
        o = blk * 32
        for j in range(16):
            P[o + j + 16, o + j] = -1.0
            P[o + j, o + j + 16] = 1.0
    c["prot"] = _bf(P)
    n_tok = 2048
    rows = np.repeat(np.arange(n_tok // 64), 64).astype(np.float32)
    cols = np.tile(np.arange(64), n_tok // 64).astype(np.float32)
    inv = (10000.0 ** (-np.arange(0, 32, 2, dtype=np.float32) / 32)).astype(np.float32)
    ang_r = rows[:, None] * inv[None, :]
    ang_c = cols[:, None] * inv[None, :]
    ang = np.concatenate([ang_r, ang_r, ang_c, ang_c], axis=-1).astype(np.float32)
    cos = np.cos(ang).astype(np.float32).T
    sin = np.sin(ang).astype(np.float32).T
    c["rcos"] = np.ascontiguousarray(np.concatenate([cos, cos], axis=0))
    c["rsin"] = np.ascontiguousarray(np.concatenate([sin, sin], axis=0))
    wins = (2, 4, 8, 16)
    n = 384
    band = np.zeros((4, 5, 128, 128), np.float32)
    for g, win in enumerate(wins):
        def full(nn):
            M = np.zeros((nn, nn), np.float64)
            for t in range(nn):
                lo = min(max(t - win // 2, 0), nn)
                hi = min(max(t - win // 2 + win, 0), nn)
                M[lo:hi, t] = 1.0 / (hi - lo)
                M[t, t] -= 1.0
            return M
        M = full(n)
        band[g, 0] = M[0:128, 0:128]
        band[g, 1] = M[128:256, 128:256]
        band[g, 2] = M[256:384, 256:384]
        band[g, 3] = M[0:128, 128:256]
        band[g, 4] = M[256:384, 128:256]
    c["band"] = _bf(band.transpose(2, 0, 1, 3).reshape(128, 20 * 128))
    def dft(nn):
        k = np.arange(nn, dtype=np.int64)
        ph = (np.outer(k, k) % nn).astype(np.float64) * (2.0 * np.pi / nn)
        s = 1.0 / math.sqrt(nn)
        return np.cos(ph) * s, np.sin(ph) * s
    cc, sc = dft(128)
    c["dftc"] = _bf(np.concatenate([cc, -sc], axis=1))
    c2, s2 = dft(256)
    c["dft256"] = _bf(np.concatenate([c2.reshape(2, 128, 256).transpose(1, 0, 2),
                                      s2.reshape(2, 128, 256).transpose(1, 0, 2)], axis=1).reshape(128, 4 * 256))
    c3, s3 = dft(2048)
    def tiles(Mx):
        return np.ascontiguousarray(
            Mx.reshape(16, 128, 4, 512).transpose(2, 1, 0, 3).reshape(4, 128, 16 * 512))
    c["dft2kc"] = _bf(tiles(c3))
    c["dft2ks"] = _bf(tiles(s3))
    return c


_CONST = None


def build_program(debug=False, stage=99):
    nc = bass.Bass("TRN2", target_bir_lowering=False)
    dt = {}

    def din(name, shape, dtype=F32):
        dt[name] = nc.dram_tensor(name, list(shape), dtype, kind="ExternalInput").ap()
        return dt[name]

    def dout(name, shape, dtype=F32):
        dt[name] = nc.dram_tensor(name, list(shape), dtype, kind="ExternalOutput").ap()
        return dt[name]

    xp_d = din("xp", [1024, D])
    xs_d = din("xs", [2048, D])
    ck_d = din("ck", [4, 256, 128])
    cv_d = din("cv", [4, 256, 128])
    vpack_d = din("vpack", [128, 128])
    bpack_d = din("bpack", [128, 128])
    lam_d = din("lamp", [1, 256])
    wmod_d = din("w_mod", [2, 12, 128, 8 * 512])
    wine_d = din("w_in_even", [24, 128, 8 * 128])
    woute_d = din("w_out_even", [4, 128, 8 * 256])
    wino_d = din("w_in_odd", [4, 128, 8 * 256])
    wouto_d = din("w_out_odd", [4, 128, 8 * 256])
    wpool_d = din("w_pool", [4, 128, 128])
    wfour_d = din("w_fourier", [4, 128, 128])
    wgate_d = din("w_gate", [2, 11, 128, 8 * 256])
    wup_d = din("w_up", [2, 11, 128, 8 * 256])
    wdown_d = din("w_down", [2, 8, 128, 22 * 128])
    ident_d = din("ident", [128, 128])
    prot_d = din("prot", [128, 128], BF16)
    rcos_d = din("rcos", [128, 2048])
    rsin_d = din("rsin", [128, 2048])
    band_d = din("band", [128, 20 * 128], BF16)
    dftc_d = din("dftc", [128, 256], BF16)
    dft256_d = din("dft256", [128, 4 * 256], BF16)
    dft2kc_d = din("dft2kc", [4, 128, 16 * 512], BF16)
    dft2ks_d = din("dft2ks", [4, 128, 16 * 512], BF16)

    yp_d = dout("yp", [1024, D])
    ys_d = dout("ys", [2048, D])
    nk_d = dout("nk", [4, 4, 256, 128])
    nv_d = dout("nv", [4, 4, 256, 128])
    dbg = {}

    es = ExitStack()
    arena_t = es.enter_context(nc.sbuf_tensor("arena", [128, ARENA_BYTES], U8))
    psum_t = es.enter_context(nc.psum_tensor("psum", [128, 8, 512], F32))
    P = Prog(nc, es)
    A = Arena(arena_t, ARENA_BYTES)
    A.prog = P
    PS = [Tile(psum_t[:, i, :], "ps%d" % i) for i in range(8)]
    for p_ in PS:
        p_.b().excl = True

    def phase_sync():
        if debug or FULLBAR:
            P.barrier()
        else:
            import inspect
            fr = inspect.stack()[1]
            P.marks.append((fr.function, fr.lineno, sum(1 for o in P.streams["pe"] if o[0] is not None)))

    def dbg_dump(name, src_ap, bufs, shape):
        if not debug:
            return
        d = dout("dbg_" + name, shape, src_ap.dtype)
        dbg[name] = shape
        P.op("sp", lambda e: e.dma_start(out=d, in_=src_ap), r=bufs, dma=True)

    rr = {"ev": 0, "dq": 0}

    def ev_eng():
        rr["ev"] ^= 1
        return "act" if rr["ev"] else "dve"

    def copy_op(eng, out, in_, r, w, scale=None):
        if eng == "act":
            if scale is None:
                return P.op("act", lambda e: e.activation(out=out, in_=in_, func=AF.Copy), r=r, w=w)
            return P.op("act", lambda e: e.activation(out=out, in_=in_, func=AF.Copy, scale=scale), r=r, w=w)
        if scale is None:
            return P.op(eng, lambda e: e.tensor_copy(out=out, in_=in_), r=r, w=w)
        return P.op(eng, lambda e: e.tensor_scalar(out=out, in0=in_, scalar1=scale, scalar2=None,
                                                   op0=ALU.mult), r=r, w=w)

    def dma(eng, out, in_, r=(), w=(), mld=None):
        if mld is None:
            return P.op(eng, lambda e: e.dma_start(out=out, in_=in_), r=r, w=w, dma=True)
        return P.op(eng, lambda e: e.dma_start(out=out, in_=in_, max_dma_last_dim=mld), r=r, w=w, dma=True)

    ident = A.t("ident", [128], F32)
    ones_bf = A.t("ones_bf", [128], BF16)
    ones_f = A.t("ones_f", [128], F32)
    vT = A.t("vT", [128], F32)
    bT = A.t("bT", [128], F32)
    sc = A.t("sc", [8, 2], F32)
    scb = A.t("scb", [8, 2], BF16)
    modsb = A.t("modsb", [2, 48, 2], F32)
    mv = A.t("mv", [2, 2, 6, 8], F32)
    neglam = A.t("neglam", [1], F32)
    sublns = A.t("sublns", [1], F32)
    prot = A.t("prot", [128], BF16)
    band = A.t("band", [20, 128], BF16)
    dftc = A.t("dftc", [256], BF16)
    dft256 = A.t("dft256", [4, 256], BF16)
    wpool = A.t("wpool", [4, 128], BF16)
    wfour = A.t("wfour", [4, 128], BF16)
    epsb = A.t("epsb", [1], F32)
    small = A.t("small", [64], F32)

    dma("sp", ident.ap, ident_d, w=[ident.b()])
    dma("sp", prot.ap, prot_d, w=[prot.b()])
    dma("sp", band.ap, band_d.rearrange("p (a b) -> p a b", a=20), w=[band.b()])
    dma("sp", dftc.ap, dftc_d, w=[dftc.b()])
    dma("sp", dft256.ap, dft256_d.rearrange("p (a b) -> p a b", a=4), w=[dft256.b()])
    P.op("dve", lambda e: e.memset(ones_bf.ap, 1.0), w=[ones_bf.b()])
    P.op("dve", lambda e: e.memset(ones_f.ap, 1.0), w=[ones_f.b()])
    P.op("dve", lambda e: e.memset(epsb.ap, EPS), w=[epsb.b()])

    m0 = A.mark()
    stg = A.t("stg", [2, 128], F32)
    dma("sp", stg.ap[:, 0, :], vpack_d, w=[stg.b(0)])
    dma("sp", stg.ap[:, 1, :], bpack_d, w=[stg.b(1)])
    P.op("pe", lambda e: e.transpose(out=PS[0].ap[:, 0:128], in_=stg.ap[:, 0, :], identity=ident.ap),
         r=[stg.b(0), ident.b()], w=[PS[0].b()])
    P.op("pe", lambda e: e.transpose(out=PS[1].ap[:, 0:128], in_=stg.ap[:, 1, :], identity=ident.ap),
         r=[stg.b(1), ident.b()], w=[PS[1].b()])
    copy_op("dve", vT.ap, PS[0].ap[:, 0:128], [PS[0].b()], [vT.b()])
    copy_op("dve", bT.ap, PS[1].ap[:, 0:128], [PS[1].b()], [bT.b()])
    P.op("act", lambda e: e.activation(out=sc.ap[:, :, 0], in_=vT.ap[:, 89:97], func=AF.Silu),
         r=[vT.b()], w=[sc.b()])
    P.op("act", lambda e: e.activation(out=sc.ap[:, :, 1], in_=vT.ap[:, 81:89], func=AF.Silu),
         r=[vT.b(), sc.b()], w=[sc.b()])
    P.op("dve", lambda e: e.tensor_scalar(out=sublns.ap, in0=vT.ap[:, 80:81], scalar1=1.0 - LAM_INIT,
                                          scalar2=None, op0=ALU.mult), r=[vT.b()], w=[sublns.b()])
    wst = A.t("wst8", [8, 128], F32)
    dma("sp", wst.ap[:, 0:4, :], wpool_d.rearrange("g c d -> c g d"), w=[wst.b()])
    dma("sp", wst.ap[:, 4:8, :], wfour_d.rearrange("g c d -> c g d"), w=[wst.b()])
    copy_op("dve", wpool.ap, wst.ap[:, 0:4, :], [wst.b()], [wpool.b()])
    copy_op("dve", wfour.ap, wst.ap[:, 4:8, :], [wst.b()], [wfour.b()])
    lp = A.t("lp", [4, 64], F32, parts=1)
    dma("sp", lp.ap, lam_d.rearrange("o (a b) -> o a b", a=4), w=[lp.b()])
    lprod = A.t("lprod", [2, 64], F32, parts=1)
    lsum = A.t("lsum", [4], F32, parts=1)
    P.op("dve", lambda e: e.tensor_tensor(out=lprod.ap, in0=lp.ap[:, 0:4:2, :], in1=lp.ap[:, 1:4:2, :],
                                          op=ALU.mult), r=[lp.b()], w=[lprod.b()])
    P.op("dve", lambda e: e.reduce_sum(out=lsum.ap[:, 0:2], in_=lprod.ap, axis=AX.X),
         r=[lprod.b()], w=[lsum.b()])
    P.op("act", lambda e: e.activation(out=lsum.ap[:, 2:4], in_=lsum.ap[:, 0:2], func=AF.Exp),
         r=[lsum.b()], w=[lsum.b()])
    P.op("dve", lambda e: e.tensor_tensor(out=lsum.ap[:, 0:1], in0=lsum.ap[:, 3:4], in1=lsum.ap[:, 2:3],
                                          op=ALU.subtract), r=[lsum.b()], w=[lsum.b()])
    P.op("dve", lambda e: e.tensor_scalar(out=lsum.ap[:, 1:2], in0=lsum.ap[:, 0:1],
                                          scalar1=-LAM_INIT, scalar2=None, op0=ALU.add),
         r=[lsum.b()], w=[lsum.b()])
    P.op("pe", lambda e: e.matmul(PS[2].ap[:, 0:1], ones_f.ap[0:1, :], lsum.ap[:, 1:2], start=True, stop=True),
         r=[ones_f.b(), lsum.b()], w=[PS[2].b()])
    copy_op("dve", neglam.ap, PS[2].ap[:, 0:1], [PS[2].b()], [neglam.b()])

    P.op("act", lambda e: e.activation(out=scb.ap, in_=sc.ap, func=AF.Copy), r=[sc.b()], w=[scb.b()])
    bg = []

    def bg_step():
        if bg:
            bg.pop(0)()

    def bg_flush():
        while bg:
            bg.pop(0)()

    def mod_steps(l, wms, rowblk):
        def issue(blk):
            ws = wms[blk % 3]
            dma("pool", ws.ap, wmod_d[l, blk].rearrange("p (k n) -> p k n", k=8), w=[ws.b()], mld=4096)

        def step(blk):
            if blk + 2 < 12:
                issue(blk + 2)
            ws = wms[blk % 3]
            psX, psY = PS[2], PS[3]
            for k in range(8):
                P.op("pe", lambda e, k=k: e.matmul(psX.ap[0:2, :], scb.ap[:, k, :], ws.ap[:, k, :],
                                                  start=(k == 0), stop=(k == 7)),
                     r=[ws.b(), scb.b()], w=[psX.b()], sig=(k == 7))
            copy_op("dve", rowblk.ap, psX.ap[0:2, :], [psX.b()], [rowblk.b()])
            for j in range(4):
                P.op("pe", lambda e, j=j: e.transpose(out=psY.ap[:, j * 2:(j + 1) * 2],
                                                      in_=rowblk.ap[:, j * 128:(j + 1) * 128],
                                                      identity=ident.ap[0:2, 0:2]),
                     r=[rowblk.b(), ident.b()], w=[psY.b()], sig=(j == 3))
            for w_ in range(2):
                P.op("dve", lambda e, w_=w_: e.tensor_tensor(
                    out=modsb.ap[:, l, blk * 4:(blk + 1) * 4, w_],
                    in0=psY.ap[:, 0:8].rearrange("p (c w) -> p c w", w=2)[:, :, w_],
                    in1=bT.ap[:, l * 48 + blk * 4: l * 48 + blk * 4 + 4], op=ALU.add),
                    r=[psY.b(), bT.b()], w=[modsb.b()])

        issue(0)
        issue(1)
        return [lambda blk=blk: step(blk) for blk in range(12)] + [lambda: mod_final(l)]

    def mod_final(l):
        for w_ in range(2):
            g = lambda j, l=l: vT.ap[:, l * 32 + j * 8: l * 32 + j * 8 + 8]
            md = lambda i, l=l, w_=w_: modsb.ap[:, l, i * 8:(i + 1) * 8, w_]
            P.op("dve", lambda e, l=l, w_=w_, g=g, md=md: e.scalar_tensor_tensor(
                out=mv.ap[:, l, w_, 0, :], in0=md(1), scalar=1.0, in1=g(0), op0=ALU.add, op1=ALU.mult),
                r=[modsb.b(), vT.b()], w=[mv.b()])
            P.op("dve", lambda e, l=l, w_=w_, md=md: e.tensor_copy(out=mv.ap[:, l, w_, 1, :], in_=md(0)),
                 r=[modsb.b()], w=[mv.b()])
            P.op("dve", lambda e, l=l, w_=w_, g=g, md=md: e.tensor_tensor(
                out=mv.ap[:, l, w_, 2, :], in0=md(2), in1=g(1), op=ALU.mult),
                r=[modsb.b(), vT.b()], w=[mv.b()])
            P.op("dve", lambda e, l=l, w_=w_, g=g, md=md: e.scalar_tensor_tensor(
                out=mv.ap[:, l, w_, 3, :], in0=md(4), scalar=1.0, in1=g(2), op0=ALU.add, op1=ALU.mult),
                r=[modsb.b(), vT.b()], w=[mv.b()])
            P.op("dve", lambda e, l=l, w_=w_, md=md: e.tensor_copy(out=mv.ap[:, l, w_, 4, :], in_=md(3)),
                 r=[modsb.b()], w=[mv.b()])
            P.op("dve", lambda e, l=l, w_=w_, g=g, md=md: e.tensor_tensor(
                out=mv.ap[:, l, w_, 5, :], in0=md(5), in1=g(3), op=ALU.mult),
                r=[modsb.b(), vT.b()], w=[mv.b()])
    wms0 = [A.t("wmsA%d" % i, [8, 512], BF16) for i in range(3)]
    rowblk0 = A.t("rowblkA", [512], F32, parts=2)
    for st_ in mod_steps(0, wms0, rowblk0):
        st_()
    dbg_dump("mv", mv.ap, [mv.b()], [128, 2, 2, 6, 8])
    dbg_dump("neglam", neglam.ap, [neglam.b()], [128, 1])
    dbg_dump("sc", sc.ap, [sc.b()], [128, 8, 2])
    dbg_dump("vT", vT.ap, [vT.b()], [128, 128])
    dbg_dump("bT", bT.ap, [bT.b()], [128, 128])
    dbg_dump("modsb", modsb.ap, [modsb.b()], [128, 2, 48, 2])
    phase_sync()
    A.release(m0)

    def load_x(xd, x, T):
        m = A.mark()
        xtm = [A.t("xtm%d" % i, [1024], F32) for i in range(8)]
        for ck in range(T // 128):
            xt = xtm[ck % 8]
            dma("sp", xt.ap, xd[ck * 128:(ck + 1) * 128, :], w=[xt.b()])
            for half in range(2):
                ps = PS[(ck * 2 + half) % 8]
                for j in range(4):
                    c = half * 4 + j
                    P.op("pe", lambda e, ps=ps, xt=xt, c=c, j=j: e.transpose(
                        out=ps.ap[:, j * 128:(j + 1) * 128], in_=xt.ap[:, c * 128:(c + 1) * 128],
                        identity=ident.ap), r=[xt.b(), ident.b()], w=[ps.b()], sig=(j == 3))
                copy_op(ev_eng(), x.ap[:, half * 4:half * 4 + 4, ck * 128:(ck + 1) * 128],
                        ps.ap.rearrange("p (a b) -> p a b", a=4), [ps.b()], [x.b(ck // 4)])
        phase_sync()
        A.release(m)

    def store_x(yd, x, T):
        m = A.mark()
        otm = [A.t("otm%d" % i, [1024], F32) for i in range(8)]
        for ck in range(T // 128):
            ot = otm[ck % 8]
            for half in range(2):
                ps = PS[(ck * 2 + half) % 8]
                for j in range(4):
                    c = half * 4 + j
                    P.op("pe", lambda e, ps=ps, c=c, j=j, ck=ck: e.transpose(
                        out=ps.ap[:, j * 128:(j + 1) * 128], in_=x.ap[:, c, ck * 128:(ck + 1) * 128],
                        identity=ident.ap), r=[x.b(ck // 4), ident.b()], w=[ps.b()], sig=(j == 3))
                copy_op(ev_eng(), ot.ap[:, half * 512:(half + 1) * 512], ps.ap, [ps.b()], [ot.b()])
            dma("sp", yd[ck * 128:(ck + 1) * 128, :], ot.ap, r=[ot.b()])
        phase_sync()
        A.release(m)

    def rstd_from_ps(ps, rs, n, width):
        P.op("act", lambda e: e.activation(out=rs.ap[:, 0:width], in_=ps.ap[:, 0:width], func=AF.Ln,
                                           bias=epsb.ap, scale=1.0 / n), r=[ps.b(), epsb.b()], w=[rs.b()])
        P.op("act", lambda e: e.activation(out=rs.ap[:, 0:width], in_=rs.ap[:, 0:width], func=AF.Exp,
                                           scale=-0.5), r=[rs.b()], w=[rs.b()])

    def norm_mod(x, h, tiles, l, w_, kind_gs, kind_sh, tmp):
        sq, rs, t1 = tmp
        for ti, tl in enumerate(tiles):
            cols = slice(tl * 512, (tl + 1) * 512)
            ps = PS[ti % 2]
            for c in range(8):
                P.op("act", lambda e, c=c, cols=cols: e.activation(out=sq.ap[:, c, :], in_=x.ap[:, c, cols],
                                                                  func=AF.Square),
                     r=[x.b(tl)], w=[sq.b(c)])
                P.op("pe", lambda e, c=c, ps=ps: e.matmul(ps.ap, ones_bf.ap, sq.ap[:, c, :],
                                                         start=(c == 0), stop=(c == 7)),
                     r=[sq.b(c), ones_bf.b()], w=[ps.b()], sig=True)
            rstd_from_ps(ps, rs, D, 512)
            hc = slice(ti * 512, (ti + 1) * 512)
            for c in range(8):
                t = t1[c % 2]
                P.op("dve", lambda e, c=c, cols=cols, t=t: e.scalar_tensor_tensor(
                    out=t.ap, in0=x.ap[:, c, cols], scalar=mv.ap[:, l, w_, kind_gs, c:c + 1], in1=rs.ap,
                    op0=ALU.mult, op1=ALU.mult), r=[x.b(tl), mv.b(), rs.b()], w=[t.b()])
                P.op("act", lambda e, c=c, hc=hc, t=t: e.activation(
                    out=h.ap[:, c, hc], in_=t.ap, func=AF.Identity, bias=mv.ap[:, l, w_, kind_sh, c:c + 1],
                    scale=1.0), r=[t.b(), mv.b()], w=[h.b(ti)])

    def gated_residual(x, tl, ysb, ssq_ps, l, w_, kind_gg, tmp):
        sq, rs, t1 = tmp
        cols = slice(tl * 512, (tl + 1) * 512)
        rstd_from_ps(ssq_ps, rs, D, 512)
        for c in range(8):
            t = t1[c % 2]
            P.op("dve", lambda e, c=c, t=t: e.scalar_tensor_tensor(
                out=t.ap, in0=ysb.ap[:, c, :], scalar=mv.ap[:, l, w_, kind_gg, c:c + 1], in1=rs.ap,
                op0=ALU.mult, op1=ALU.mult), r=[ysb.b(c), mv.b(), rs.b()], w=[t.b()])
            P.op("dve", lambda e, c=c, t=t, cols=cols: e.tensor_tensor(
                out=x.ap[:, c, cols], in0=x.ap[:, c, cols], in1=t.ap, op=ALU.add),
                r=[t.b(), x.b(tl)], w=[x.b(tl)])

    def norm_mod_pieces(x, h, tl, ti, l, w_, kind_gs, kind_sh, tmp, ps):
        sq, rs, t1 = tmp
        cols = slice(tl * 512, (tl + 1) * 512)
        hc = slice(ti * 512, (ti + 1) * 512)

        def stats():
            for c in range(8):
                P.op("act", lambda e, c=c: e.activation(out=sq.ap[:, c % 4, :], in_=x.ap[:, c, cols],
                                                        func=AF.Square), r=[x.b(tl)], w=[sq.b(c % 4)])
                P.op("pe", lambda e, c=c: e.matmul(ps.ap, ones_bf.ap, sq.ap[:, c % 4, :],
                                                   start=(c == 0), stop=(c == 7)),
                     r=[sq.b(c % 4), ones_bf.b()], w=[ps.b()], sig=True)

        def half(c0):
            if c0 == 0:
                rstd_from_ps(ps, rs, D, 512)
            for c in range(c0, c0 + 4):
                t = t1[c % 2]
                P.op("dve", lambda e, c=c, t=t: e.scalar_tensor_tensor(
                    out=t.ap, in0=x.ap[:, c, cols], scalar=mv.ap[:, l, w_, kind_gs, c:c + 1], in1=rs.ap,
                    op0=ALU.mult, op1=ALU.mult), r=[x.b(tl), mv.b(), rs.b()], w=[t.b()])
                P.op("act", lambda e, c=c, t=t: e.activation(
                    out=h.ap[:, c, hc], in_=t.ap, func=AF.Identity, bias=mv.ap[:, l, w_, kind_sh, c:c + 1],
                    scale=1.0), r=[t.b(), mv.b()], w=[h.b(ti)])
        return [stats, lambda: half(0), lambda: half(4)]

    def gated_residual_pieces(x, tl, ysb, ssq_ps, l, w_, kind_gg, tmp):
        sq, rs, t1 = tmp
        cols = slice(tl * 512, (tl + 1) * 512)

        def piece(c):
            if c == 0:
                rstd_from_ps(ssq_ps, rs, D, 512)
            t = t1[c % 2]
            P.op("dve", lambda e: e.scalar_tensor_tensor(
                out=t.ap, in0=ysb.ap[:, c, :], scalar=mv.ap[:, l, w_, kind_gg, c:c + 1], in1=rs.ap,
                op0=ALU.mult, op1=ALU.mult), r=[ysb.b(c), mv.b(), rs.b()], w=[t.b()])
            P.op("dve", lambda e: e.tensor_tensor(
                out=x.ap[:, c, cols], in0=x.ap[:, c, cols], in1=t.ap, op=ALU.add),
                r=[t.b(), x.b(tl)], w=[x.b(tl)])
        return [lambda c=c: piece(c) for c in range(8)]

    class WStream:
        def __init__(self, kc, maxcols, nstage=2, name="ws"):
            self.kc = kc
            self.maxcols = maxcols
            self.st = [] if CASTMODE == "dma" else [A.t("%s_st%d" % (name, i), [kc, maxcols], F32) for i in range(nstage)]
            self.i = 0

        def load(self, dst_tile, dst_ap, slab_ap, kc, ncols, dst_buf):
            dma("pool", dst_ap, slab_ap.rearrange("p (k n) -> p k n", k=kc), w=[dst_buf], mld=4096)

    def ffn(x, tiles, l, w_, scr, h2_pre=None, next_tiles=None):
        m = A.mark()
        nt = len(tiles)
        h2 = h2_pre if h2_pre is not None else A.carve(scr, 0, 8 * 512 * nt, 8, "h2")
        a = A.carve(scr, 8 * 512 * nt, 8 * 512 * nt + FC * 512 * nt, FC, "a")
        sq = A.t("f_sq", [8, 512], BF16)
        rs = A.t("f_rs", [512], F32)
        t1 = [A.t("f_t%d" % i, [512], F32) for i in range(2)]
        tmp = (sq, rs, t1)
        if h2_pre is None:
            norm_mod(x, h2, tiles, l, w_, 3, 4, tmp)
        ws = WStream(8, 256, 2, "wgu")
        NR = 3
        wg = [A.t("wg%d" % i, [8, 256], BF16) for i in range(NR)]
        wu = [A.t("wu%d" % i, [8, 256], BF16) for i in range(NR)]
        sg = [A.t("sg%d" % i, [512], F32) for i in range(2)]
        k = 0
        for jp in range(FC // 2):
            bg_step()
            g_ = wg[jp % NR]
            u_ = wu[jp % NR]
            ws.load(g_, g_.ap, wgate_d[l, jp], 8, 256, g_.b())
            ws.load(u_, u_.ap, wup_d[l, jp], 8, 256, u_.b())
            for jj in range(2):
                j = jp * 2 + jj
                for ti in range(nt):
                    hc = slice(ti * 512, (ti + 1) * 512)
                    pg = PS[(k * 2) % 8]
                    pu = PS[(k * 2 + 1) % 8]
                    k += 1
                    for c in range(8):
                        P.op("pe", lambda e, c=c, pg=pg, g_=g_, jj=jj, hc=hc: e.matmul(
                            pg.ap, g_.ap[:, c, jj * 128:(jj + 1) * 128], h2.ap[:, c, hc],
                            start=(c == 0), stop=(c == 7)), r=[g_.b(), h2.b(ti)], w=[pg.b()], sig=(c == 7))
                    for c in range(8):
                        P.op("pe", lambda e, c=c, pu=pu, u_=u_, jj=jj, hc=hc: e.matmul(
                            pu.ap, u_.ap[:, c, jj * 128:(jj + 1) * 128], h2.ap[:, c, hc],
                            start=(c == 0), stop=(c == 7)), r=[u_.b(), h2.b(ti)], w=[pu.b()], sig=(c == 7))
                    s_ = sg[k % 2]
                    P.op("act", lambda e, s_=s_, pg=pg: e.activation(out=s_.ap, in_=pg.ap, func=AF.Silu),
                         r=[pg.b()], w=[s_.b()])
                    P.op("dve", lambda e, s_=s_, pu=pu, j=j, hc=hc: e.tensor_tensor(
                        out=a.ap[:, j, hc], in0=s_.ap, in1=pu.ap, op=ALU.mult),
                        r=[s_.b(), pu.b()], w=[a.b((j, ti))])
        phase_sync()
        A.release(m)
        rs = A.t("f_rs", [512], F32)
        t1 = [A.t("f_t%d" % i, [512], F32) for i in range(2)]
        tmp = (None, rs, t1)
        wsd = WStream(FC // 2, 128, 2, "wd")
        NWD = 2 if next_tiles else 3
        wd = [A.t("wdn%d" % i, [FC, 128], BF16) for i in range(NWD)]
        ysb = [A.t("ysb%d" % i, [8, 512], F32) for i in range(nt)]
        sqd = [A.t("sqd%d" % i, [512], BF16) for i in range(3)]
        ssq = [PS[6 + i] for i in range(nt)]
        k = 0
        pend = []
        todo = []
        h2n = None
        if next_tiles:
            nsq = A.t("f_nsq", [4, 512], BF16)
            nrs = A.t("f_nrs", [512], F32)
            nt1 = [A.t("f_nt%d" % i, [512], F32) for i in range(2)]
            h2n = A.carve(scr, 0, 8 * 512 * len(next_tiles), 8, "h2n")
            for ti2, tl2 in enumerate(next_tiles):
                todo.extend(norm_mod_pieces(x, h2n, tl2, ti2, l, w_, 3, 4, (nsq, nrs, nt1), PS[5]))
        for d in range(8):
            w_d = wd[d % NWD]
            for hf in range(2):
                wsd.load(w_d, w_d.ap[:, hf * 11:(hf + 1) * 11, :], wdown_d[l, d][:, hf * 1408:(hf + 1) * 1408],
                         11, 128, w_d.b(hf))
            for ti in range(nt):
                hc = slice(ti * 512, (ti + 1) * 512)
                ps = PS[k % 5]
                k += 1
                for j in range(FC):
                    P.op("pe", lambda e, j=j, ps=ps, w_d=w_d, hc=hc: e.matmul(
                        ps.ap, w_d.ap[:, j, :], a.ap[:, j, hc], start=(j == 0), stop=(j == FC - 1)),
                        r=[w_d.b(j // 11), a.b((j, ti))], w=[ps.b()], sig=(j == FC - 1))
                if pend:
                    pend.pop()()
                s_ = sqd[k % 3]
                P.op("act", lambda e, ps=ps, s_=s_: e.activation(out=s_.ap, in_=ps.ap, func=AF.Square),
                     r=[ps.b()], w=[s_.b()])
                copy_op("dve", ysb[ti].ap[:, d, :], ps.ap, [ps.b()], [ysb[ti].b(d)])
                if todo and k >= 2:
                    todo.pop(0)()

                def fin(ti=ti, s_=s_, d=d):
                    P.op("pe", lambda e: e.matmul(ssq[ti].ap, ones_bf.ap, s_.ap, start=(d == 0), stop=(d == 7)),
                         r=[s_.b(), ones_bf.b()], w=[ssq[ti].b()], sig=True)
                pend.append(fin)
        if pend:
            pend.pop()()
        while todo:
            todo.pop(0)()
        for ti, tl in enumerate(tiles):
            gated_residual(x, tl, ysb[ti], ssq[ti], l, w_, 5, tmp)
        phase_sync()
        A.release(m)
        return h2n

    def even_mixer(x, scr, NT, nseq, L, w_, is_sample, nk_out=None, nv_out=None):
        T = NT * 512
        nchunk = T // 128
        nctx = 2 if is_sample else 0
        m = A.mark()
        h = A.carve(scr, 0, 8 * T, 8, "h")
        cat = A.carve(scr, 8 * T, 16 * T, 8, "cat")
        sq = A.t("m_sq", [8, 512], BF16)
        rs = A.t("m_rs", [512], F32)
        t1 = [A.t("m_t%d" % i, [512], F32) for i in range(2)]
        tmp = (sq, rs, t1)
        norm_mod(x, h, list(range(NT)), 0, w_, 0, 1, tmp)
        phase_sync()
        A.release(m)
        if debug and not is_sample:
            dbg_dump("h0p", h.ap, [h.b(i) for i in range(NT)], [128, 8, T])
        if stage <= 1:
            phase_sync(); A.release(m); return
        ws = WStream(8, 128, 2, "wie")
        wsl = [A.t("wsl%d" % i, [8, 128], BF16) for i in range(5)]
        wcnt = [0]

        def slab(col0):
            s_ = wsl[wcnt[0] % 5]
            wcnt[0] += 1
            ws.load(s_, s_.ap, wine_d[col0 // 128], 8, 128, s_.b())
            return s_

        mc = A.mark()
        Lp = L + 2
        zl = [A.t("z%d" % i, [nseq, Lp], F32) for i in range(2)]
        accl = [A.t("acc%d" % i, [nseq, L], F32) for i in range(2)]
        gbl = [A.t("gb%d" % i, [T], BF16) for i in range(2)]
        gcs = [A.t("gcs%d" % i, [512], F32) for i in range(2)]
        spt = 512 // L if L < 512 else 1
        for z in zl:
            P.op("dve", lambda e, z=z: e.memset(z.ap, 0.0), w=[z.b()])
        kk = 0
        for c in range(4):
            bg_step()
            z, acc, gb = zl[c % 2], accl[c % 2], gbl[c % 2]
            s_gb = slab(1536 + c * 128)
            s_gc = slab(2048 + c * 128)
            s_xi = slab(2560 + c * 128)
            for tl in range(NT):
                cols = slice(tl * 512, (tl + 1) * 512)
                p_gb, p_gc, p_xi = PS[kk % 8], PS[(kk + 1) % 8], PS[(kk + 2) % 8]
                kk += 3
                for (pp, ss) in ((p_gb, s_gb), (p_gc, s_gc), (p_xi, s_xi)):
                    for k in range(8):
                        P.op("pe", lambda e, pp=pp, ss=ss, k=k, cols=cols: e.matmul(
                            pp.ap, ss.ap[:, k, :], h.ap[:, k, cols], start=(k == 0), stop=(k == 7)),
                            r=[ss.b(), h.b(tl)], w=[pp.b()], sig=(k == 7))
                copy_op("act", gb.ap[:, cols], p_gb.ap, [p_gb.b()], [gb.b()])
                g_ = gcs[tl % 2]
                copy_op("act", g_.ap, p_gc.ap, [p_gc.b()], [g_.b()])
                if L >= 512:
                    zo = z.ap[:, 0, 1 + tl * 512: 1 + (tl + 1) * 512]
                    xi = p_xi.ap
                    gi = g_.ap
                else:
                    zo = z.ap[:, tl * spt:(tl + 1) * spt, 1:1 + L]
                    xi = p_xi.ap.rearrange("p (s t) -> p s t", s=spt)
                    gi = g_.ap.rearrange("p (s t) -> p s t", s=spt)
                P.op("dve", lambda e, zo=zo, xi=xi, gi=gi: e.tensor_tensor(out=zo, in0=xi, in1=gi, op=ALU.mult),
                     r=[p_xi.b(), g_.b()], w=[z.b()])
            cw = lambda tap, c=c: vT.ap[:, 64 + tap * 4 + c: 64 + tap * 4 + c + 1]
            P.op("act", lambda e, cw=cw, z=z, acc=acc: e.activation(out=acc.ap, in_=z.ap[:, :, 1:1 + L], func=AF.Copy,
                                                      scale=cw(1)), r=[z.b(), vT.b()], w=[acc.b()])
            P.op("dve", lambda e, cw=cw, z=z, acc=acc: e.scalar_tensor_tensor(
                out=acc.ap, in0=z.ap[:, :, 0:L], scalar=cw(0), in1=acc.ap, op0=ALU.mult, op1=ALU.add),
                r=[z.b(), vT.b(), acc.b()], w=[acc.b()])
            P.op("dve", lambda e, cw=cw, z=z, acc=acc: e.scalar_tensor_tensor(
                out=acc.ap, in0=z.ap[:, :, 2:2 + L], scalar=cw(2), in1=acc.ap, op0=ALU.mult, op1=ALU.add),
                r=[z.b(), vT.b(), acc.b()], w=[acc.b()])
            P.op("dve", lambda e, c=c, acc=acc, gb=gb: e.tensor_tensor(
                out=cat.ap[:, 4 + c, :], in0=acc.ap.rearrange("p s t -> p (s t)"), in1=gb.ap, op=ALU.mult),
                r=[acc.b(), gb.b()], w=[cat.b(("cv", c))])
        phase_sync()
        A.release(mc)
        if debug and not is_sample:
            dbg_dump("catconv_p", cat.ap[:, 4:8, :], [cat.b(("cv", c)) for c in range(4)], [128, 4, T])
        if stage <= 2:
            phase_sync(); A.release(m); return

        nkeys = nctx * 128 + (T if is_sample else L)
        qh = A.t("qh", [T], BF16)
        kTh = A.t("kTh", [nctx * 128 + T], BF16)
        vh = A.t("vh", [nctx + nchunk, 128], BF16)
        if is_sample:
            rcos = A.t("rcos", [2048], F32)
            rsin = A.t("rsin", [2048], F32)
            dma("sp", rcos.ap, rcos_d, w=[rcos.b()])
            dma("sp", rsin.ap, rsin_d, w=[rsin.b()])
            qb = [A.t("qb%d" % i, [512], BF16) for i in range(1)] * 2
            rt1 = [A.t("rt1%d" % i, [512], F32) for i in range(1)] * 2
            rt2 = [A.t("rt2%d" % i, [512], F32) for i in range(1)] * 2
            ckst = A.t("ckst", [2, 128], F32)
            cvst = A.t("cvst", [2, 128], F32)
        else:
            kvo = [A.t("kvo%d" % i, [512], F32) for i in range(2)]
        E = [[A.t("E%d%d" % (i, j), [512], BF16) for j in range(2)] for i in range(2)]
        r01 = [A.t("r01%d" % i, [512], F32) for i in range(2)]
        osb = A.t("osb", [512], F32)
        o2 = A.t("o2", [512], F32)
        osq = A.t("osq", [512], BF16)
        scale = 64 ** -0.5
        kq = [0]
        deferred = []

        def proj_fm(s_, dst, dst_off, rope, nm):
            for tl in range(NT):
                cols = slice(tl * 512, (tl + 1) * 512)
                dcols = slice(dst_off + tl * 512, dst_off + (tl + 1) * 512)
                ps = PS[kq[0] % 4]
                kq[0] += 1
                for k in range(8):
                    P.op("pe", lambda e, ps=ps, s_=s_, k=k, cols=cols: e.matmul(
                        ps.ap, s_.ap[:, k, :], h.ap[:, k, cols], start=(k == 0), stop=(k == 7)),
                        r=[s_.b(), h.b(tl)], w=[ps.b()], sig=(k == 7))
                if not rope:
                    copy_op(ev_eng(), dst.ap[:, dcols], ps.ap, [ps.b()], [dst.b(tl)])
                else:
                    q_ = qb[tl % 2]
                    a_ = rt1[tl % 2]
                    b_ = rt2[tl % 2]
                    ps2 = PS[4 + (kq[0] % 2)]
                    copy_op("act", q_.ap, ps.ap, [ps.b()], [q_.b()])
                    P.op("dve", lambda e, a_=a_, ps=ps, cols=cols: e.tensor_tensor(
                        out=a_.ap, in0=ps.ap, in1=rcos.ap[:, cols], op=ALU.mult),
                        r=[ps.b(), rcos.b()], w=[a_.b()])
                    P.op("pe", lambda e, ps2=ps2, q_=q_: e.matmul(ps2.ap, prot.ap, q_.ap, start=True, stop=True),
                         r=[prot.b(), q_.b()], w=[ps2.b()])
                    P.op("dve", lambda e, b_=b_, ps2=ps2, cols=cols: e.tensor_tensor(
                        out=b_.ap, in0=ps2.ap, in1=rsin.ap[:, cols], op=ALU.mult),
                        r=[ps2.b(), rsin.b()], w=[b_.b()])
                    P.op("dve", lambda e, a_=a_, b_=b_, dcols=dcols: e.tensor_tensor(
                        out=dst.ap[:, dcols], in0=a_.ap, in1=b_.ap, op=ALU.add),
                        r=[a_.b(), b_.b()], w=[dst.b(tl)])

        def proj_tm(s_, hd, to_v, out_d):
            for tl in range(NT):
                ps = PS[kq[0] % 4]
                kq[0] += 1
                for cq in range(4):
                    ck = tl * 4 + cq
                    for k in range(8):
                        P.op("pe", lambda e, ps=ps, s_=s_, k=k, ck=ck, cq=cq: e.matmul(
                            ps.ap[:, cq * 128:(cq + 1) * 128], h.ap[:, k, ck * 128:(ck + 1) * 128], s_.ap[:, k, :],
                            start=(k == 0), stop=(k == 7)),
                            r=[s_.b(), h.b(tl)], w=[ps.b()], sig=(k == 7 and cq == 3))
                if to_v:
                    copy_op("act", vh.ap[:, nctx + tl * 4: nctx + tl * 4 + 4, :],
                            ps.ap.rearrange("p (a b) -> p a b", a=4), [ps.b()], [vh.b(tl)])
                if out_d is not None:
                    ko = kvo[tl % 2]
                    copy_op("dve", ko.ap, ps.ap, [ps.b()], [ko.b()])
                    for sq_ in range(2):
                        for c2 in range(2):
                            dma("sp", out_d[2 * tl + sq_, hd, c2 * 128:(c2 + 1) * 128, :],
                                ko.ap[:, (sq_ * 2 + c2) * 128:(sq_ * 2 + c2 + 1) * 128], r=[ko.b()])

        for hd in range(4):
            bg_step()
            s_q = slab(hd * 128)
            proj_fm(s_q, qh, 0, is_sample, "q")
            s_k = slab(512 + hd * 128)
            proj_fm(s_k, kTh, nctx * 128, is_sample, "k")
            if not is_sample and 'tmk' not in SKIP:
                proj_tm(s_k, hd, False, nk_out if 'kvout' not in SKIP else None)
            s_v = slab(1024 + hd * 128)
            if 'tmv' not in SKIP:
                proj_tm(s_v, hd, True, (nv_out if 'kvout' not in SKIP else None) if not is_sample else None)
            if is_sample:
                dma("sp", ckst.ap, ck_d[hd].rearrange("(c t) d -> t c d", c=2), w=[ckst.b()])
                dma("sp", cvst.ap, cv_d[hd].rearrange("(c t) d -> t c d", c=2), w=[cvst.b()])
                ps = PS[4]
                for cq in range(2):
                    P.op("pe", lambda e, ps=ps, cq=cq: e.transpose(
                        out=ps.ap[:, cq * 128:(cq + 1) * 128], in_=ckst.ap[:, cq, :], identity=ident.ap),
                        r=[ckst.b(), ident.b()], w=[ps.b()], sig=(cq == 1))
                copy_op("dve", kTh.ap[:, 0:256], ps.ap[:, 0:256], [ps.b()], [kTh.b("ctx")])
                copy_op("act", vh.ap[:, 0:2, :], cvst.ap, [cvst.b()], [vh.b("ctx")])
            if debug and hd == 0 and 'qdump' not in SKIP:
                tag = "s" if is_sample else "p"
                dbg_dump("qh_" + tag, qh.ap, [qh.b(i) for i in range(NT)], [128, T])
                dbg_dump("kTh_" + tag, kTh.ap, [kTh.b(i) for i in range(NT)] + ([kTh.b("ctx")] if is_sample else []),
                         [128, nctx * 128 + T])
                dbg_dump("vh_" + tag, vh.ap, [vh.b(i) for i in range(NT)] + ([vh.b("ctx")] if is_sample else []),
                         [128, nctx + nchunk, 128])
            if is_sample:
                jobs = [(tl * 512, 512, list(range(nctx + nchunk)), tl) for tl in range(NT)]
            else:
                jobs = [(s * L, L, [s * (L // 128) + i for i in range(L // 128)], (s * L) // 512)
                        for s in range(nseq)]
            if stage < 3:
                jobs = []
            for (q0, nq, kcs, tl) in jobs:
                qc = slice(q0, q0 + nq)
                pO = [PS[4], PS[5]]
                pZ = [PS[6], PS[7]]
                kbufs = [kTh.b(i) for i in range(NT)] + ([kTh.b("ctx")] if is_sample else [])
                vbufs = [vh.b(i) for i in range(NT)] + ([vh.b("ctx")] if is_sample else [])
                nk_ = len(kcs)

                def emit_S(ki):
                    kc = kcs[ki]
                    kcol = slice(kc * 128, (kc + 1) * 128)
                    for comp in range(2):
                        pS = PS[(ki % 2) * 2 + comp]
                        pr = slice(comp * 64, (comp + 1) * 64)
                        P.op("pe", lambda e, pS=pS, pr=pr, kcol=kcol, qc=qc, nq=nq: e.matmul(
                            pS.ap[:, 0:nq], kTh.ap[pr, kcol], qh.ap[pr, qc], start=True, stop=True),
                            r=kbufs + [qh.b(tl)], w=[pS.b()])
                        E_ = E[ki % 2][comp]
                        P.op("act", lambda e, pS=pS, E_=E_, nq=nq: e.activation(
                            out=E_.ap[:, 0:nq], in_=pS.ap[:, 0:nq], func=AF.Exp, scale=scale),
                            r=[pS.b()], w=[E_.b()])

                def emit_PV(ki):
                    kc = kcs[ki]
                    first = (ki == 0)
                    last = (ki == nk_ - 1)
                    for comp in range(2):
                        E_ = E[ki % 2][comp]
                        P.op("pe", lambda e, comp=comp, E_=E_, kc=kc, nq=nq, first=first, last=last: e.matmul(
                            pO[comp].ap[:, 0:nq], vh.ap[:, kc, :], E_.ap[:, 0:nq], start=first, stop=last),
                            r=vbufs + [E_.b()], w=[pO[comp].b()], sig=False)
                        P.op("pe", lambda e, comp=comp, E_=E_, nq=nq, first=first, last=last: e.matmul(
                            pZ[comp].ap[:, 0:nq], ones_bf.ap, E_.ap[:, 0:nq], start=first, stop=last),
                            r=[ones_bf.b(), E_.b()], w=[pZ[comp].b()], sig=True)

                emit_S(0)
                if nk_ > 1:
                    emit_S(1)
                for ki in range(nk_):
                    emit_PV(ki)
                    if ki + 2 < nk_:
                        emit_S(ki + 2)
                if deferred:
                    deferred.pop()()
                P.op("act", lambda e, nq=nq: e.activation(out=osb.ap[:, 0:nq], in_=pO[0].ap[:, 0:nq], func=AF.Copy),
                     r=[pO[0].b()], w=[osb.b()])
                P.op("act", lambda e, nq=nq: e.activation(out=o2.ap[:, 0:nq], in_=pO[1].ap[:, 0:nq], func=AF.Copy),
                     r=[pO[1].b()], w=[o2.b()])
                for comp in range(2):
                    P.op("dve", lambda e, comp=comp, nq=nq: e.tensor_copy(out=r01[comp].ap[:, 0:nq],
                                                                          in_=pZ[comp].ap[:, 0:nq]),
                         r=[pZ[comp].b()], w=[r01[comp].b()])
                for comp in range(2):
                    P.op("dve", lambda e, comp=comp, nq=nq: e.reciprocal(out=r01[comp].ap[:, 0:nq],
                                                                        in_=r01[comp].ap[:, 0:nq]),
                         r=[r01[comp].b()], w=[r01[comp].b()])
                P.op("dve", lambda e, nq=nq: e.tensor_tensor(out=osb.ap[:, 0:nq], in0=osb.ap[:, 0:nq],
                                                             in1=r01[0].ap[:, 0:nq], op=ALU.mult),
                     r=[osb.b(), r01[0].b()], w=[osb.b()])
                P.op("dve", lambda e, nq=nq: e.tensor_tensor(out=o2.ap[:, 0:nq], in0=o2.ap[:, 0:nq],
                                                             in1=r01[1].ap[:, 0:nq], op=ALU.mult),
                     r=[o2.b(), r01[1].b()], w=[o2.b()])
                P.op("dve", lambda e, nq=nq: e.scalar_tensor_tensor(
                    out=osb.ap[:, 0:nq], in0=o2.ap[:, 0:nq], scalar=neglam.ap[:, 0:1], in1=osb.ap[:, 0:nq],
                    op0=ALU.mult, op1=ALU.add), r=[o2.b(), neglam.b(), osb.b()], w=[osb.b()])
                def tail(nq=nq, hd=hd, qc=qc, q0=q0):
                    pN = PS[0]
                    P.op("act", lambda e: e.activation(out=osq.ap[:, 0:nq], in_=osb.ap[:, 0:nq], func=AF.Square),
                         r=[osb.b()], w=[osq.b()])
                    P.op("pe", lambda e: e.matmul(pN.ap[:, 0:nq], ones_bf.ap, osq.ap[:, 0:nq],
                                                  start=True, stop=True),
                         r=[ones_bf.b(), osq.b()], w=[pN.b()])
                    rs_o = r01[0]
                    P.op("act", lambda e: e.activation(
                        out=rs_o.ap[:, 0:nq], in_=pN.ap[:, 0:nq], func=AF.Ln, bias=epsb.ap, scale=1.0 / 128),
                        r=[pN.b(), epsb.b()], w=[rs_o.b()])
                    P.op("act", lambda e: e.activation(out=rs_o.ap[:, 0:nq], in_=rs_o.ap[:, 0:nq], func=AF.Exp,
                                                       scale=-0.5), r=[rs_o.b()], w=[rs_o.b()])
                    P.op("dve", lambda e: e.scalar_tensor_tensor(
                        out=cat.ap[:, hd, qc], in0=osb.ap[:, 0:nq], scalar=sublns.ap[:, 0:1], in1=rs_o.ap[:, 0:nq],
                        op0=ALU.mult, op1=ALU.mult), r=[osb.b(), sublns.b(), rs_o.b()], w=[cat.b(("o", hd, q0))])
                deferred.append(tail)
        if deferred:
            deferred.pop()()
        phase_sync()
        A.release(m)
        if debug:
            tag = "s" if is_sample else "p"
            dbg_dump("cat0_" + tag, cat.ap, [], [128, 8, T])
            phase_sync()
        if stage <= 3:
            return
        out_proj_residual(x, cat, NT, woute_d, 0, w_)

    def out_proj_residual(x, cat, NT, wd, l, w_):
        m = A.mark()
        wo = A.t("wo", [8, 1024], BF16)
        ws = WStream(8, 256, 2, "wos")
        for i in range(4):
            ws.load(wo, wo.ap[:, :, i * 256:(i + 1) * 256], wd[i], 8, 256, wo.b(i))
        rsl = [A.t("o_rs%d" % i, [512], F32) for i in range(2)]
        t1 = [A.t("o_t%d" % i, [512], F32) for i in range(2)]
        ysbl = [A.t("o_ysb%d" % i, [8, 512], F32) for i in range(2)]
        sqd = [A.t("o_sqd%d" % i, [512], BF16) for i in range(3)]
        k = 0
        pend = []
        todo = []
        for tl in range(NT):
            cols = slice(tl * 512, (tl + 1) * 512)
            ssq = PS[6 + tl % 2]
            ysb = ysbl[tl % 2]
            for d in range(8):
                ps = PS[k % 6]
                k += 1
                for c in range(8):
                    P.op("pe", lambda e, ps=ps, c=c, d=d, cols=cols: e.matmul(
                        ps.ap, wo.ap[:, c, d * 128:(d + 1) * 128], cat.ap[:, c, cols],
                        start=(c == 0), stop=(c == 7)), r=[wo.b(d // 2)], w=[ps.b()], sig=(c == 7))
                if pend:
                    pend.pop()()
                s_ = sqd[k % 3]
                P.op("act", lambda e, ps=ps, s_=s_: e.activation(out=s_.ap, in_=ps.ap, func=AF.Square),
                     r=[ps.b()], w=[s_.b()])
                copy_op("dve", ysb.ap[:, d, :], ps.ap, [ps.b()], [ysb.b(d)])
                if todo:
                    todo.pop(0)()

                def fin(ssq=ssq, s_=s_, d=d, tl=tl, ysb=ysb):
                    P.op("pe", lambda e: e.matmul(ssq.ap, ones_bf.ap, s_.ap, start=(d == 0), stop=(d == 7)),
                         r=[s_.b(), ones_bf.b()], w=[ssq.b()], sig=True)
                    if d == 7:
                        while todo:
                            todo.pop(0)()
                        todo.extend(gated_residual_pieces(x, tl, ysb, ssq, l, w_, 2, (None, rsl[tl % 2], t1)))
                pend.append(fin)
        if pend:
            pend.pop()()
        while todo:
            todo.pop(0)()
        phase_sync()
        A.release(m)

    def odd_mixer(x, scr, NT, nseq, L, w_, is_sample):
        T = NT * 512
        nchunk = T // 128
        cps = L // 128
        m = A.mark()
        hA = A.carve(scr, 0, 8 * 512, 8, "h1")
        cat = A.carve(scr, 8 * T, 16 * T, 8, "cat1")
        u = A.t("u", [nchunk, 1024], BF16)
        m1 = A.mark()
        if is_sample:
            wi = A.carve(scr, 4096, 4096 + 8192, 8, "wi")
        else:
            wi = A.t("wi", [8, 1024], BF16)
        ws = WStream(8, 128 if is_sample else 256, 2, "wio")
        for i in range(4):
            ws.load(wi, wi.ap[:, :, i * 256:(i + 1) * 256], wino_d[i], 8, 256, wi.b(i))
        sq = A.t("m_sq", [8, 512], BF16)
        rs = A.t("m_rs", [512], F32)
        t1 = [A.t("m_t%d" % i, [512], F32) for i in range(2)]
        tmp = (sq, rs, t1)
        k = 0
        if is_sample:
            hB = A.carve(scr, 12288, 16384, 8, "h1b")
        else:
            hB = A.carve(scr, 4096, 8192, 8, "h1b")
        hAB = [hA, hB]
        norm_mod(x, hAB[0], [0], 1, w_, 0, 1, tmp)
        for tl in range(NT):
            hcur = hAB[tl % 2]
            if tl + 1 < NT:
                norm_mod(x, hAB[(tl + 1) % 2], [tl + 1], 1, w_, 0, 1, tmp)
            for cq in range(4):
                ck = tl * 4 + cq
                for half in range(2):
                    ps = PS[2 + (k % 6)]
                    k += 1
                    for c in range(8):
                        P.op("pe", lambda e, ps=ps, c=c, cq=cq, half=half, hcur=hcur: e.matmul(
                            ps.ap, hcur.ap[:, c, cq * 128:(cq + 1) * 128], wi.ap[:, c, half * 512:(half + 1) * 512],
                            start=(c == 0), stop=(c == 7)),
                            r=[hcur.b(0), wi.b(half * 2), wi.b(half * 2 + 1)], w=[ps.b()], sig=(c == 7))
                    copy_op(ev_eng(), u.ap[:, ck, half * 512:(half + 1) * 512], ps.ap, [ps.b()], [u.b(ck)])
        phase_sync()
        A.release(m1)
        if debug and not is_sample:
            dbg_dump("u_p", u.ap, [], [128, nchunk, 1024])
            phase_sync()
        if is_sample:
            tabc = A.carve(scr, 0, 8192, 16, "tabc")
            tabs = A.carve(scr, 8192, 16384, 16, "tabs")
        pd = [A.t("pd%d" % i, [512], BF16) for i in range(2)]
        cu = [A.t("cu%d" % i, [512], BF16) for i in range(2)]
        su = [A.t("su%d" % i, [512], BF16) for i in range(2)]
        fr = [A.t("fr%d" % i, [512], BF16) for i in range(2)]
        ubufs = [u.b(i) for i in range(nchunk)]
        items = [(tl, g) for tl in range(NT) for g in range(4)]
        nit = len(items)

        def stage_A(i):
            tl, g = items[i]
            if g == 0 and is_sample:
                dma("sp", tabc.ap, dft2kc_d[tl].rearrange("p (a b) -> p a b", a=16), w=[tabc.b()])
                dma("sp", tabs.ap, dft2ks_d[tl].rearrange("p (a b) -> p a b", a=16), w=[tabs.b()])
            ps = PS[0 + i % 2]
            pc = PS[2 + i % 2]
            pz = PS[4 + i % 2]
            for bq in range(4):
                ck = tl * 4 + bq
                pos = ck % cps
                kind = 0 if pos == 0 else (2 if pos == cps - 1 else 1)
                terms = [(ck, kind)]
                if pos > 0:
                    terms.append((ck - 1, 3))
                if pos < cps - 1:
                    terms.append((ck + 1, 4))
                for j_, (cs, kd) in enumerate(terms):
                    P.op("pe", lambda e, ps=ps, bq=bq, cs=cs, kd=kd, g=g, j_=j_, n=len(terms): e.matmul(
                        ps.ap[:, bq * 128:(bq + 1) * 128], u.ap[:, cs, g * 128:(g + 1) * 128],
                        band.ap[:, g * 5 + kd, :], start=(j_ == 0), stop=(j_ == n - 1)),
                        r=ubufs + [band.b()], w=[ps.b()], sig=(bq == 3 and j_ == len(terms) - 1))
            pd_ = pd[i % 2]
            copy_op("act", pd_.ap, ps.ap, [ps.b()], [pd_.b()])
            if is_sample:
                for a_ in range(16):
                    P.op("pe", lambda e, pc=pc, a_=a_, g=g: e.matmul(
                        pc.ap, u.ap[:, a_, 512 + g * 128: 512 + (g + 1) * 128], tabc.ap[:, a_, :],
                        start=(a_ == 0), stop=(a_ == 15)), r=ubufs + [tabc.b()], w=[pc.b()], sig=(a_ == 15))
                for a_ in range(16):
                    P.op("pe", lambda e, pz=pz, a_=a_, g=g: e.matmul(
                        pz.ap, u.ap[:, a_, 512 + g * 128: 512 + (g + 1) * 128], tabs.ap[:, a_, :],
                        start=(a_ == 0), stop=(a_ == 15)), r=ubufs + [tabs.b()], w=[pz.b()], sig=(a_ == 15))
            else:
                for s_i in range(2):
                    for a_ in range(2):
                        ck = tl * 4 + s_i * 2 + a_
                        P.op("pe", lambda e, pc=pc, a_=a_, g=g, s_i=s_i, ck=ck: e.matmul(
                            pc.ap[:, s_i * 256:(s_i + 1) * 256], u.ap[:, ck, 512 + g * 128: 512 + (g + 1) * 128],
                            dft256.ap[:, a_, :], start=(a_ == 0), stop=(a_ == 1)),
                            r=ubufs + [dft256.b()], w=[pc.b()], sig=(a_ == 1 and s_i == 1))
                for s_i in range(2):
                    for a_ in range(2):
                        ck = tl * 4 + s_i * 2 + a_
                        P.op("pe", lambda e, pz=pz, a_=a_, g=g, s_i=s_i, ck=ck: e.matmul(
                            pz.ap[:, s_i * 256:(s_i + 1) * 256], u.ap[:, ck, 512 + g * 128: 512 + (g + 1) * 128],
                            dft256.ap[:, 2 + a_, :], start=(a_ == 0), stop=(a_ == 1)),
                            r=ubufs + [dft256.b()], w=[pz.b()], sig=(a_ == 1 and s_i == 1))
            copy_op("act", cu[i % 2].ap, pc.ap, [pc.b()], [cu[i % 2].b()])
            copy_op("dve", su[i % 2].ap, pz.ap, [pz.b()], [su[i % 2].b()])

        def stage_B(i):
            tl, g = items[i]
            cols = slice(tl * 512, (tl + 1) * 512)
            cu_, su_, pd_, fr_ = cu[i % 2], su[i % 2], pd[i % 2], fr[i % 2]
            pf = PS[7]
            P.op("pe", lambda e: e.matmul(pf.ap, dftc.ap[:, 0:128], cu_.ap, start=True, stop=False),
                 r=[dftc.b(), cu_.b()], w=[pf.b()], sig=False)
            P.op("pe", lambda e: e.matmul(pf.ap, dftc.ap[:, 128:256], su_.ap, start=False, stop=True),
                 r=[dftc.b(), su_.b()], w=[pf.b()])
            copy_op("act", fr_.ap, pf.ap, [pf.b()], [fr_.b()])
            ps2 = PS[6]
            P.op("pe", lambda e: e.matmul(ps2.ap, wpool.ap[:, g, :], pd_.ap, start=True, stop=True),
                 r=[wpool.b(), pd_.b()], w=[ps2.b()])
            copy_op("dve", cat.ap[:, g, cols], ps2.ap, [ps2.b()], [cat.b((tl, g))],
                    scale=vT.ap[:, 76 + g:77 + g])

        def stage_C(i):
            tl, g = items[i]
            cols = slice(tl * 512, (tl + 1) * 512)
            fr_ = fr[i % 2]
            pw = PS[6]
            P.op("pe", lambda e: e.matmul(pw.ap, wfour.ap[:, g, :], fr_.ap, start=True, stop=True),
                 r=[wfour.b(), fr_.b()], w=[pw.b()])
            copy_op("dve", cat.ap[:, 4 + g, cols], pw.ap, [pw.b()], [cat.b((tl, 4 + g))])

        for idx in range(nit + 2):
            if idx < nit:
                stage_A(idx)
            if idx >= 2:
                stage_C(idx - 2)
            if 1 <= idx <= nit:
                stage_B(idx - 1)
        phase_sync()
        A.release(m)
        if debug and not is_sample:
            dbg_dump("cat1_p", cat.ap, [], [128, 8, T])
            phase_sync()
        out_proj_residual(x, cat, NT, wouto_d, 1, w_)

    def run_group(xd, yd, NT, nseq, L, w_, is_sample):
        T = NT * 512
        mg = A.mark()
        if not is_sample:
            wms1 = [A.t("wmsB%d" % i, [8, 512], BF16) for i in range(3)]
            rowblk1 = A.t("rowblkB", [512], F32, parts=2)
            bg.extend(mod_steps(1, wms1, rowblk1))
        scr = A.t("scr", [max(16 * T, 30 * 1024)], BF16)
        x = A.t("x", [8, T], F32)
        load_x(xd, x, T)
        if debug and not is_sample:
            dbg_dump("x0p", x.ap, [x.b(i) for i in range(NT)], [128, 8, T])
        for l in range(2):
            if l == 0:
                even_mixer(x, scr, NT, nseq, L, w_, is_sample,
                           nk_out=None if is_sample else nk_d, nv_out=None if is_sample else nv_d)
            else:
                bg_flush()
                odd_mixer(x, scr, NT, nseq, L, w_, is_sample)
            if stage <= 3 + 3 * l:
                break
            if debug and not is_sample:
                dbg_dump("xmid%d_p" % l, x.ap, [x.b(i) for i in range(NT)], [128, 8, T])
            h2p = None
            for h0 in range(0, NT, 2):
                tiles = list(range(h0, min(h0 + 2, NT)))
                nxt = list(range(h0 + 2, min(h0 + 4, NT)))
                h2p = ffn(x, tiles, l, w_, scr, h2_pre=h2p, next_tiles=nxt or None)
            if debug and not is_sample:
                dbg_dump("xl%d_p" % l, x.ap, [x.b(i) for i in range(NT)], [128, 8, T])
            if stage <= 4 + 3 * l:
                break
        store_x(yd, x, T)
        phase_sync()
        A.release(mg)

    if stage >= 1:
        run_group(xp_d, yp_d, 2, 4, 256, 0, False)
    if stage >= 50:
        run_group(xs_d, ys_d, 4, 1, 2048, 1, True)

    P.barrier()

    with nc.Block() as block:
        @block.tensor
        def _(e):
            P.replay("pe", e)

        @block.scalar
        def _(e):
            P.replay("act", e)

        @block.vector
        def _(e):
            P.replay("dve", e)

        @block.gpsimd
        def _(e):
            P.replay("pool", e)

        @block.sync
        def _(e):
            P.replay("sp", e)
    es.close()
    return nc, dbg, {"nops": P.nops, "peak": A.peak, "marks": P.marks}


_PROG = {}


def _get_prog(debug, stage):
    key = (debug, stage)
    if key not in _PROG:
        _PROG[key] = build_program(debug, stage)
    return _PROG[key]


def kernel(x_prompt, x_sample, cache_k, cache_v, c, c_ctx, w_mod, b_mod, norm_g,
           w_in_even, lam_params, subln_g, conv_w, w_out_even,
           w_in_odd, w_pool, pool_scale, w_fourier, w_out_odd,
           w_gate, w_up, w_down, _debug=False, _stage=99):
    global _CONST
    if _CONST is None:
        _CONST = make_constants()
    f = lambda a: np.ascontiguousarray(np.asarray(a, dtype=np.float32))
    x_prompt, x_sample, cache_k, cache_v = f(x_prompt), f(x_sample), f(cache_k), f(cache_v)
    c, c_ctx, b_mod, norm_g = f(c), f(c_ctx), f(b_mod), f(norm_g)
    nc, dbg, info = _get_prog(_debug, _stage)
    def slabs(W, ncols):
        K, N = W.shape
        return np.ascontiguousarray(
            W.reshape(K // 128, 128, N // ncols, ncols).transpose(2, 1, 0, 3).reshape(N // ncols, 128, (K // 128) * ncols))

    shared = {
        "w_mod": np.stack([slabs(f(w_mod)[l], 512) for l in range(2)]),
        "w_in_even": slabs(f(w_in_even)[0], 128), "w_out_even": slabs(f(w_out_even)[0], 256),
        "w_in_odd": slabs(f(w_in_odd)[0], 256), "w_out_odd": slabs(f(w_out_odd)[0], 256),
        "w_pool": f(w_pool)[0], "w_fourier": f(w_fourier)[0],
        "w_gate": np.stack([slabs(f(w_gate)[l], 256) for l in range(2)]),
        "w_up": np.stack([slabs(f(w_up)[l], 256) for l in range(2)]),
        "w_down": np.stack([slabs(f(w_down)[l], 128) for l in range(2)]),
        "lamp": f(lam_params).reshape(1, 256),
    }
    shared.update(_CONST)
    bpack = np.zeros((128, 128), np.float32)
    bpack[0:96] = b_mod.reshape(96, 128)
    in_maps = []
    for i in range(NCORES):
        vpack = np.zeros((128, 128), np.float32)
        vpack[0:64] = norm_g.reshape(64, 128)
        vpack[64:76] = f(conv_w)[0].reshape(12, 128)
        vpack[76:80] = f(pool_scale)[0].reshape(4, 128)
        vpack[80] = f(subln_g)[0]
        vpack[81:89] = c[i].reshape(8, 128)
        vpack[89:97] = c_ctx.reshape(8, 128)
        mp = dict(shared)
        mp.update({
            "xp": x_prompt[4 * i:4 * i + 4].reshape(1024, D),
            "xs": x_sample[i],
            "ck": cache_k[i, 0], "cv": cache_v[i, 0],
            "vpack": vpack, "bpack": bpack,
        })
        in_maps.append(mp)
    res = run_bass_kernel_spmd(nc, in_maps, core_ids=list(range(NCORES)))
    rs = res.results
    yp = np.concatenate([r["yp"].reshape(4, 256, D) for r in rs], axis=0)
    ys = np.stack([r["ys"] for r in rs], axis=0)
    nk = np.concatenate([r["nk"] for r in rs], axis=0)[:, None]
    nv = np.concatenate([r["nv"] for r in rs], axis=0)[:, None]
    if _debug:
        return (yp, ys, nk, nv), [{k: r["dbg_" + k] for k in dbg} for r in rs]
    return (yp.astype(np.float32), ys.astype(np.float32), nk.astype(np.float32), nv.astype(np.float32))
```

```python
import math
from contextlib import ExitStack

import numpy as np
import ml_dtypes

import concourse.bass as bass
import concourse.mybir as mybir
from concourse.bass_utils import run_bass_kernel_spmd

F32 = mybir.dt.float32
BF16 = mybir.dt.bfloat16
U8 = mybir.dt.uint8
AF = mybir.ActivationFunctionType
ALU = mybir.AluOpType
AX = mybir.AxisListType

D = 1024
KC = 8
DFF = 2816
FC = 22
NCORES = 8
EPS = 1e-6
LAM_INIT = 0.8 - 0.6 * math.exp(-0.3 * 0)
ARENA_BYTES = 212000

import os
SKIP = set(os.environ.get('K_SKIP', '').split(','))
DEBUG = False
CASTMODE = os.environ.get('K_CAST', 'dma')
FULLBAR = os.environ.get('K_FULLBAR', '') == '1'
SAMEENG = False
STAGE = 99


class Buf:
    __slots__ = ("name", "w", "w_eng", "r", "pend", "excl")

    def __init__(self, name=""):
        self.name = name
        self.excl = False
        self.w = None
        self.w_eng = None
        self.r = []
        self.pend = None


class Tile:
    def __init__(self, ap, name):
        self.ap = ap
        self.name = name
        self.subs = {}
        self.inh = []
        self.abs_lo = None

    def b(self, key=None):
        if key not in self.subs:
            bf = Buf("%s/%s" % (self.name, key))
            bf.r = list(self.inh)
            self.subs[key] = bf
        return self.subs[key]


class Prog:
    ENGS = ("pe", "act", "dve", "pool", "sp")
    NDMA = 16

    def __init__(self, nc, es):
        self.nc = nc
        self.streams = {e: [] for e in self.ENGS}
        self.sem = {}
        self.cnt = {}
        for e in ("pe", "act", "dve", "pool"):
            self.sem[e] = es.enter_context(nc.semaphore("s_" + e))
            self.cnt[e] = 0
        for i in range(self.NDMA):
            for pre in ("dmaH", "dmaS"):
                n = "%s%d" % (pre, i)
                self.sem[n] = es.enter_context(nc.semaphore("s_" + n))
                self.cnt[n] = 0
        self.dma_rr = {"dmaH": 0, "dmaS": 0}
        self.waited = {e: {} for e in self.ENGS}
        self.pending = {e: [] for e in self.ENGS}
        self.nops = 0
        self.marks = []

    def _need(self, eng, need, tok):
        s, v = tok
        if self.waited[eng].get(s, 0) >= v:
            return
        if need.get(s, 0) < v:
            need[s] = v

    def op(self, eng, fn, r=(), w=(), sig=True, dma=False):
        need = {}
        for b in r:
            if b.pend is not None and b.pend != eng:
                raise RuntimeError("read of buffer with pending unsignalled write: " + b.name)
            if b.w is not None:
                if not (b.pend == eng):
                    self._need(eng, need, b.w)
            if b.excl:
                for (t, e2) in b.r:
                    if e2 != eng:
                        self._need(eng, need, t)
        for b in w:
            if b.pend is not None and (b.pend != eng or dma):
                raise RuntimeError("write to buffer with pending unsignalled access: " + b.name)
            if b.w is not None and (b.w_eng != eng or dma or b.w_eng == "dma" or (SAMEENG and eng != "pe")):
                self._need(eng, need, b.w)
            for (t, e2) in b.r:
                if e2 != eng or dma or (SAMEENG and eng != "pe"):
                    self._need(eng, need, t)
        signal = None
        if dma:
            pre = "dmaS" if eng == "pool" else "dmaH"
            sname = "%s%d" % (pre, self.dma_rr[pre])
            self.dma_rr[pre] = (self.dma_rr[pre] + 1) % self.NDMA
            if self.cnt[sname] > 0:
                self._need(eng, need, (sname, self.cnt[sname]))
            self.cnt[sname] += 16
            tok = (sname, self.cnt[sname])
            signal = (sname, 16)
            peng = "dma"
        elif sig:
            self.cnt[eng] += 1
            tok = (eng, self.cnt[eng])
            signal = (eng, 1)
            peng = eng
        else:
            tok = None
            peng = eng
        waits = sorted(need.items())
        for s, v in waits:
            self.waited[eng][s] = v
        self.streams[eng].append((fn, waits, signal))
        self.nops += 1
        if tok is None:
            for b in r:
                self.pending[eng].append((b, "r"))
            for b in w:
                self.pending[eng].append((b, "w"))
                b.pend = eng
            return None
        allr = [(b, "r") for b in r]
        allw = [(b, "w") for b in w]
        if not dma:
            allr += [(b, k) for (b, k) in self.pending[eng] if k == "r"]
            allw += [(b, k) for (b, k) in self.pending[eng] if k == "w"]
            self.pending[eng] = []
        for b, _ in allw:
            b.w = tok
            b.w_eng = peng
            b.r = []
            b.pend = None
        for b, _ in allr:
            b.r.append((tok, peng))
        return tok

    def barrier(self):
        import inspect
        fr = inspect.stack()[1]
        self.marks.append((fr.function, fr.lineno, sum(1 for o in self.streams["pe"] if o[0] is not None)))
        toks = [(s, c) for s, c in self.cnt.items() if c > 0]
        for e in self.ENGS:
            need = {}
            for t in toks:
                self._need(e, need, t)
            waits = sorted(need.items())
            if waits:
                for s, v in waits:
                    self.waited[e][s] = v
                self.streams[e].append((None, waits, None))

    def replay(self, eng, e):
        for fn, waits, signal in self.streams[eng]:
            for s, v in waits:
                e.wait_ge(self.sem[s], v)
            if fn is None:
                continue
            ins = fn(e)
            if signal is not None:
                ins.then_inc(self.sem[signal[0]], signal[1])


class Arena:
    def __init__(self, ar, nbytes):
        self.ar = ar
        self.top = 0
        self.size = nbytes
        self.peak = 0
        self.reg = []
        self.prog = None

    def register(self, lo, hi, tile):
        if self.prog is not None:
            for e_, lst in self.prog.pending.items():
                if lst:
                    raise RuntimeError("tile allocation with unsignalled ops pending on " + e_)
        inh = {}
        for (l2, h2, t2) in self.reg:
            if l2 < hi and lo < h2:
                toks = list(t2.inh)
                for b in t2.subs.values():
                    if b.w is not None:
                        toks.append((b.w, b.w_eng))
                    toks.extend(b.r)
                for (tk, e2) in toks:
                    sname, v = tk
                    if sname not in inh or inh[sname][0][1] < v:
                        inh[sname] = ((sname, v), e2)
        tile.inh = list(inh.values())
        tile.abs_lo = lo
        self.reg = [(l2, h2, t2) for (l2, h2, t2) in self.reg if not (lo <= l2 and h2 <= hi)]
        self.reg.append((lo, hi, tile))

    def carve(self, parent, lo_el, hi_el, c, name, esz=2):
        ap = parent.ap[:, lo_el:hi_el].rearrange("p (c t) -> p c t", c=c)
        t_ = Tile(ap, name)
        self.register(parent.abs_lo + esz * lo_el, parent.abs_lo + esz * hi_el, t_)
        return t_

    def mark(self):
        return self.top

    def release(self, m):
        self.top = m

    def t(self, name, shape, dtype, parts=128):
        esz = 4 if dtype == F32 else 2
        n = 1
        for s in shape:
            n *= s
        nb = (n * esz + 63) // 64 * 64
        if self.top + nb > self.size:
            raise RuntimeError("arena overflow at %s: top=%d need=%d" % (name, self.top, nb))
        ap = self.ar[0:parts, self.top:self.top + n * esz].bitcast(dtype)
        lo_ = self.top
        self.top += nb
        self.peak = max(self.peak, self.top)
        if len(shape) == 2:
            ap = ap.rearrange("p (a b) -> p a b", a=shape[0])
        elif len(shape) == 3:
            ap = ap.rearrange("p (a b c) -> p a b c", a=shape[0], b=shape[1])
        elif len(shape) == 4:
            ap = ap.rearrange("p (a b c d) -> p a b c d", a=shape[0], b=shape[1], c=shape[2])
        t_ = Tile(ap, name)
        self.register(lo_, lo_ + nb, t_)
        return t_


def _bf(a):
    return np.ascontiguousarray(a.astype(np.float32)).astype(ml_dtypes.bfloat16)


def make_constants():
    c = {}
    c["ident"] = np.eye(128, dtype=np.float32)
    P = np.zeros((128, 128), np.float32)
    for blk in range(4):
        o = blk * 32
        for j in range(16):
            P[o + j + 16, o + j] = -1.0
            P[o + j, o + j + 16] = 1.0
    c["prot"] = _bf(P)
    n_tok = 2048
    rows = np.repeat(np.arange(n_tok // 64), 64).astype(np.float32)
    cols = np.tile(np.arange(64), n_tok // 64).astype(np.float32)
    inv = (10000.0 ** (-np.arange(0, 32, 2, dtype=np.float32) / 32)).astype(np.float32)
    ang_r = rows[:, None] * inv[None, :]
    ang_c = cols[:, None] * inv[None, :]
    ang = np.concatenate([ang_r, ang_r, ang_c, ang_c], axis=-1).astype(np.float32)
    cos = np.cos(ang).astype(np.float32).T
    sin = np.sin(ang).astype(np.float32).T
    c["rcos"] = np.ascontiguousarray(np.concatenate([cos, cos], axis=0))
    c["rsin"] = np.ascontiguousarray(np.concatenate([sin, sin], axis=0))
    wins = (2, 4, 8, 16)
    n = 384
    band = np.zeros((4, 5, 128, 128), np.float32)
    for g, win in enumerate(wins):
        def full(nn):
            M = np.zeros((nn, nn), np.float64)
            for t in range(nn):
                lo = min(max(t - win // 2, 0), nn)
                hi = min(max(t - win // 2 + win, 0), nn)
                M[lo:hi, t] = 1.0 / (hi - lo)
                M[t, t] -= 1.0
            return M
        M = full(n)
        band[g, 0] = M[0:128, 0:128]
        band[g, 1] = M[128:256, 128:256]
        band[g, 2] = M[256:384, 256:384]
        band[g, 3] = M[0:128, 128:256]
        band[g, 4] = M[256:384, 128:256]
    c["band"] = _bf(band.transpose(2, 0, 1, 3).reshape(128, 20 * 128))
    def dft(nn):
        k = np.arange(nn, dtype=np.int64)
        ph = (np.outer(k, k) % nn).astype(np.float64) * (2.0 * np.pi / nn)
        s = 1.0 / math.sqrt(nn)
        return np.cos(ph) * s, np.sin(ph) * s
    cc, sc = dft(128)
    c["dftc"] = _bf(np.concatenate([cc, -sc], axis=1))
    c2, s2 = dft(256)
    c["dft256"] = _bf(np.concatenate([c2.reshape(2, 128, 256).transpose(1, 0, 2),
                                      s2.reshape(2, 128, 256).transpose(1, 0, 2)], axis=1).reshape(128, 4 * 256))
    c3, s3 = dft(2048)
    def tiles(Mx):
        return np.ascontiguousarray(
            Mx.reshape(16, 128, 4, 512).transpose(2, 1, 0, 3).reshape(4, 128, 16 * 512))
    c["dft2kc"] = _bf(tiles(c3))
    c["dft2ks"] = _bf(tiles(s3))
    return c


_CONST = None


def build_program(debug=False, stage=99):
    nc = bass.Bass("TRN2", target_bir_lowering=False)
    dt = {}

    def din(name, shape, dtype=F32):
        dt[name] = nc.dram_tensor(name, list(shape), dtype, kind="ExternalInput").ap()
        return dt[name]

    def dout(name, shape, dtype=F32):
        dt[name] = nc.dram_tensor(name, list(shape), dtype, kind="ExternalOutput").ap()
        return dt[name]

    xp_d = din("xp", [1024, D])
    xs_d = din("xs", [2048, D])
    ck_d = din("ck", [4, 256, 128])
    cv_d = din("cv", [4, 256, 128])
    vpack_d = din("vpack", [128, 128])
    bpack_d = din("bpack", [128, 128])
    lam_d = din("lamp", [1, 256])
    wmod_d = din("w_mod", [2, 12, 128, 8 * 512])
    wine_d = din("w_in_even", [24, 128, 8 * 128])
    woute_d = din("w_out_even", [4, 128, 8 * 256])
    wino_d = din("w_in_odd", [4, 128, 8 * 256])
    wouto_d = din("w_out_odd", [4, 128, 8 * 256])
    wpool_d = din("w_pool", [4, 128, 128])
    wfour_d = din("w_fourier", [4, 128, 128])
    wgate_d = din("w_gate", [2, 11, 128, 8 * 256])
    wup_d = din("w_up", [2, 11, 128, 8 * 256])
    wdown_d = din("w_down", [2, 8, 128, 22 * 128])
    ident_d = din("ident", [128, 128])
    prot_d = din("prot", [128, 128], BF16)
    rcos_d = din("rcos", [128, 2048])
    rsin_d = din("rsin", [128, 2048])
    band_d = din("band", [128, 20 * 128], BF16)
    dftc_d = din("dftc", [128, 256], BF16)
    dft256_d = din("dft256", [128, 4 * 256], BF16)
    dft2kc_d = din("dft2kc", [4, 128, 16 * 512], BF16)
    dft2ks_d = din("dft2ks", [4, 128, 16 * 512], BF16)

    yp_d = dout("yp", [1024, D])
    ys_d = dout("ys", [2048, D])
    nk_d = dout("nk", [4, 4, 256, 128])
    nv_d = dout("nv", [4, 4, 256, 128])
    dbg = {}

    es = ExitStack()
    arena_t = es.enter_context(nc.sbuf_tensor("arena", [128, ARENA_BYTES], U8))
    psum_t = es.enter_context(nc.psum_tensor("psum", [128, 8, 512], F32))
    P = Prog(nc, es)
    A = Arena(arena_t, ARENA_BYTES)
    A.prog = P
    PS = [Tile(psum_t[:, i, :], "ps%d" % i) for i in range(8)]
    for p_ in PS:
        p_.b().excl = True

    def phase_sync():
        if debug or FULLBAR:
            P.barrier()
        else:
            import inspect
            fr = inspect.stack()[1]
            P.marks.append((fr.function, fr.lineno, sum(1 for o in P.streams["pe"] if o[0] is not None)))

    def dbg_dump(name, src_ap, bufs, shape):
        if not debug:
            return
        d = dout("dbg_" + name, shape, src_ap.dtype)
        dbg[name] = shape
        P.op("sp", lambda e: e.dma_start(out=d, in_=src_ap), r=bufs, dma=True)

    rr = {"ev": 0, "dq": 0}

    def ev_eng():
        rr["ev"] ^= 1
        return "act" if rr["ev"] else "dve"

    def copy_op(eng, out, in_, r, w, scale=None):
        if eng == "act":
            if scale is None:
                return P.op("act", lambda e: e.activation(out=out, in_=in_, func=AF.Copy), r=r, w=w)
            return P.op("act", lambda e: e.activation(out=out, in_=in_, func=AF.Copy, scale=scale), r=r, w=w)
        if scale is None:
            return P.op(eng, lambda e: e.tensor_copy(out=out, in_=in_), r=r, w=w)
        return P.op(eng, lambda e: e.tensor_scalar(out=out, in0=in_, scalar1=scale, scalar2=None,
                                                   op0=ALU.mult), r=r, w=w)

    def dma(eng, out, in_, r=(), w=(), mld=None):
        if mld is None:
            return P.op(eng, lambda e: e.dma_start(out=out, in_=in_), r=r, w=w, dma=True)
        return P.op(eng, lambda e: e.dma_start(out=out, in_=in_, max_dma_last_dim=mld), r=r, w=w, dma=True)

    ident = A.t("ident", [128], F32)
    ones_bf = A.t("ones_bf", [128], BF16)
    ones_f = A.t("ones_f", [128], F32)
    vT = A.t("vT", [128], F32)
    bT = A.t("bT", [128], F32)
    sc = A.t("sc", [8, 2], F32)
    scb = A.t("scb", [8, 2], BF16)
    modsb = A.t("modsb", [2, 48, 2], F32)
    mv = A.t("mv", [2, 2, 6, 8], F32)
    neglam = A.t("neglam", [1], F32)
    sublns = A.t("sublns", [1], F32)
    prot = A.t("prot", [128], BF16)
    band = A.t("band", [20, 128], BF16)
    dftc = A.t("dftc", [256], BF16)
    dft256 = A.t("dft256", [4, 256], BF16)
    wpool = A.t("wpool", [4, 128], BF16)
    wfour = A.t("wfour", [4, 128], BF16)
    epsb = A.t("epsb", [1], F32)
    small = A.t("small", [64], F32)

    dma("sp", ident.ap, ident_d, w=[ident.b()])
    dma("sp", prot.ap, prot_d, w=[prot.b()])
    dma("sp", band.ap, band_d.rearrange("p (a b) -> p a b", a=20), w=[band.b()])
    dma("sp", dftc.ap, dftc_d, w=[dftc.b()])
    dma("sp", dft256.ap, dft256_d.rearrange("p (a b) -> p a b", a=4), w=[dft256.b()])
    P.op("dve", lambda e: e.memset(ones_bf.ap, 1.0), w=[ones_bf.b()])
    P.op("dve", lambda e: e.memset(ones_f.ap, 1.0), w=[ones_f.b()])
    P.op("dve", lambda e: e.memset(epsb.ap, EPS), w=[epsb.b()])

    m0 = A.mark()
    stg = A.t("stg", [2, 128], F32)
    dma("sp", stg.ap[:, 0, :], vpack_d, w=[stg.b(0)])
    dma("sp", stg.ap[:, 1, :], bpack_d, w=[stg.b(1)])
    P.op("pe", lambda e: e.transpose(out=PS[0].ap[:, 0:128], in_=stg.ap[:, 0, :], identity=ident.ap),
         r=[stg.b(0), ident.b()], w=[PS[0].b()])
    P.op("pe", lambda e: e.transpose(out=PS[1].ap[:, 0:128], in_=stg.ap[:, 1, :], identity=ident.ap),
         r=[stg.b(1), ident.b()], w=[PS[1].b()])
    copy_op("dve", vT.ap, PS[0].ap[:, 0:128], [PS[0].b()], [vT.b()])
    copy_op("dve", bT.ap, PS[1].ap[:, 0:128], [PS[1].b()], [bT.b()])
    P.op("act", lambda e: e.activation(out=sc.ap[:, :, 0], in_=vT.ap[:, 89:97], func=AF.Silu),
         r=[vT.b()], w=[sc.b()])
    P.op("act", lambda e: e.activation(out=sc.ap[:, :, 1], in_=vT.ap[:, 81:89], func=AF.Silu),
         r=[vT.b(), sc.b()], w=[sc.b()])
    P.op("dve", lambda e: e.tensor_scalar(out=sublns.ap, in0=vT.ap[:, 80:81], scalar1=1.0 - LAM_INIT,
                                          scalar2=None, op0=ALU.mult), r=[vT.b()], w=[sublns.b()])
    wst = A.t("wst8", [8, 128], F32)
    dma("sp", wst.ap[:, 0:4, :], wpool_d.rearrange("g c d -> c g d"), w=[wst.b()])
    dma("sp", wst.ap[:, 4:8, :], wfour_d.rearrange("g c d -> c g d"), w=[wst.b()])
    copy_op("dve", wpool.ap, wst.ap[:, 0:4, :], [wst.b()], [wpool.b()])
    copy_op("dve", wfour.ap, wst.ap[:, 4:8, :], [wst.b()], [wfour.b()])
    lp = A.t("lp", [4, 64], F32, parts=1)
    dma("sp", lp.ap, lam_d.rearrange("o (a b) -> o a b", a=4), w=[lp.b()])
    lprod = A.t("lprod", [2, 64], F32, parts=1)
    lsum = A.t("lsum", [4], F32, parts=1)
    P.op("dve", lambda e: e.tensor_tensor(out=lprod.ap, in0=lp.ap[:, 0:4:2, :], in1=lp.ap[:, 1:4:2, :],
                                          op=ALU.mult), r=[lp.b()], w=[lprod.b()])
    P.op("dve", lambda e: e.reduce_sum(out=lsum.ap[:, 0:2], in_=lprod.ap, axis=AX.X),
         r=[lprod.b()], w=[lsum.b()])
    P.op("act", lambda e: e.activation(out=lsum.ap[:, 2:4], in_=lsum.ap[:, 0:2], func=AF.Exp),
         r=[lsum.b()], w=[lsum.b()])
    P.op("dve", lambda e: e.tensor_tensor(out=lsum.ap[:, 0:1], in0=lsum.ap[:, 3:4], in1=lsum.ap[:, 2:3],
                                          op=ALU.subtract), r=[lsum.b()], w=[lsum.b()])
    P.op("dve", lambda e: e.tensor_scalar(out=lsum.ap[:, 1:2], in0=lsum.ap[:, 0:1],
                                          scalar1=-LAM_INIT, scalar2=None, op0=ALU.add),
         r=[lsum.b()], w=[lsum.b()])
    P.op("pe", lambda e: e.matmul(PS[2].ap[:, 0:1], ones_f.ap[0:1, :], lsum.ap[:, 1:2], start=True, stop=True),
         r=[ones_f.b(), lsum.b()], w=[PS[2].b()])
    copy_op("dve", neglam.ap, PS[2].ap[:, 0:1], [PS[2].b()], [neglam.b()])

    P.op("act", lambda e: e.activation(out=scb.ap, in_=sc.ap, func=AF.Copy), r=[sc.b()], w=[scb.b()])
    bg = []

    def bg_step():
        if bg:
            bg.pop(0)()

    def bg_flush():
        while bg:
            bg.pop(0)()

    def mod_steps(l, wms, rowblk):
        def issue(blk):
            ws = wms[blk % 3]
            dma("pool", ws.ap, wmod_d[l, blk].rearrange("p (k n) -> p k n", k=8), w=[ws.b()], mld=4096)

        def step(blk):
            if blk + 2 < 12:
                issue(blk + 2)
            ws = wms[blk % 3]
            psX, psY = PS[2], PS[3]
            for k in range(8):
                P.op("pe", lambda e, k=k: e.matmul(psX.ap[0:2, :], scb.ap[:, k, :], ws.ap[:, k, :],
                                                  start=(k == 0), stop=(k == 7)),
                     r=[ws.b(), scb.b()], w=[psX.b()], sig=(k == 7))
            copy_op("dve", rowblk.ap, psX.ap[0:2, :], [psX.b()], [rowblk.b()])
            for j in range(4):
                P.op("pe", lambda e, j=j: e.transpose(out=psY.ap[:, j * 2:(j + 1) * 2],
                                                      in_=rowblk.ap[:, j * 128:(j + 1) * 128],
                                                      identity=ident.ap[0:2, 0:2]),
                     r=[rowblk.b(), ident.b()], w=[psY.b()], sig=(j == 3))
            for w_ in range(2):
                P.op("dve", lambda e, w_=w_: e.tensor_tensor(
                    out=modsb.ap[:, l, blk * 4:(blk + 1) * 4, w_],
                    in0=psY.ap[:, 0:8].rearrange("p (c w) -> p c w", w=2)[:, :, w_],
                    in1=bT.ap[:, l * 48 + blk * 4: l * 48 + blk * 4 + 4], op=ALU.add),
                    r=[psY.b(), bT.b()], w=[modsb.b()])

        issue(0)
        issue(1)
        return [lambda blk=blk: step(blk) for blk in range(12)] + [lambda: mod_final(l)]

    def mod_final(l):
        for w_ in range(2):
            g = lambda j, l=l: vT.ap[:, l * 32 + j * 8: l * 32 + j * 8 + 8]
            md = lambda i, l=l, w_=w_: modsb.ap[:, l, i * 8:(i + 1) * 8, w_]
            P.op("dve", lambda e, l=l, w_=w_, g=g, md=md: e.scalar_tensor_tensor(
                out=mv.ap[:, l, w_, 0, :], in0=md(1), scalar=1.0, in1=g(0), op0=ALU.add, op1=ALU.mult),
                r=[modsb.b(), vT.b()], w=[mv.b()])
            P.op("dve", lambda e, l=l, w_=w_, md=md: e.tensor_copy(out=mv.ap[:, l, w_, 1, :], in_=md(0)),
                 r=[modsb.b()], w=[mv.b()])
            P.op("dve", lambda e, l=l, w_=w_, g=g, md=md: e.tensor_tensor(
                out=mv.ap[:, l, w_, 2, :], in0=md(2), in1=g(1), op=ALU.mult),
                r=[modsb.b(), vT.b()], w=[mv.b()])
            P.op("dve", lambda e, l=l, w_=w_, g=g, md=md: e.scalar_tensor_tensor(
                out=mv.ap[:, l, w_, 3, :], in0=md(4), scalar=1.0, in1=g(2), op0=ALU.add, op1=ALU.mult),
                r=[modsb.b(), vT.b()], w=[mv.b()])
            P.op("dve", lambda e, l=l, w_=w_, md=md: e.tensor_copy(out=mv.ap[:, l, w_, 4, :], in_=md(3)),
                 r=[modsb.b()], w=[mv.b()])
            P.op("dve", lambda e, l=l, w_=w_, g=g, md=md: e.tensor_tensor(
                out=mv.ap[:, l, w_, 5, :], in0=md(5), in1=g(3), op=ALU.mult),
                r=[modsb.b(), vT.b()], w=[mv.b()])
    wms0 = [A.t("wmsA%d" % i, [8, 512], BF16) for i in range(3)]
    rowblk0 = A.t("rowblkA", [512], F32, parts=2)
    for st_ in mod_steps(0, wms0, rowblk0):
        st_()
    dbg_dump("mv", mv.ap, [mv.b()], [128, 2, 2, 6, 8])
    dbg_dump("neglam", neglam.ap, [neglam.b()], [128, 1])
    dbg_dump("sc", sc.ap, [sc.b()], [128, 8, 2])
    dbg_dump("vT", vT.ap, [vT.b()], [128, 128])
    dbg_dump("bT", bT.ap, [bT.b()], [128, 128])
    dbg_dump("modsb", modsb.ap, [modsb.b()], [128, 2, 48, 2])
    phase_sync()
    A.release(m0)

    def load_x(xd, x, T):
        m = A.mark()
        xtm = [A.t("xtm%d" % i, [1024], F32) for i in range(6)]
        for ck in range(T // 128):
            xt = xtm[ck % 6]
            dma("sp", xt.ap, xd[ck * 128:(ck + 1) * 128, :], w=[xt.b()])
            for half in range(2):
                ps = PS[(ck * 2 + half) % 8]
                for j in range(4):
                    c = half * 4 + j
                    P.op("pe", lambda e, ps=ps, xt=xt, c=c, j=j: e.transpose(
                        out=ps.ap[:, j * 128:(j + 1) * 128], in_=xt.ap[:, c * 128:(c + 1) * 128],
                        identity=ident.ap), r=[xt.b(), ident.b()], w=[ps.b()], sig=(j == 3))
                copy_op(ev_eng(), x.ap[:, half * 4:half * 4 + 4, ck * 128:(ck + 1) * 128],
                        ps.ap.rearrange("p (a b) -> p a b", a=4), [ps.b()], [x.b(ck // 4)])
        phase_sync()
        A.release(m)

    def store_x(yd, x, T):
        m = A.mark()
        otm = [A.t("otm%d" % i, [1024], F32) for i in range(4)]
        for ck in range(T // 128):
            ot = otm[ck % 4]
            for half in range(2):
                ps = PS[(ck * 2 + half) % 8]
                for j in range(4):
                    c = half * 4 + j
                    P.op("pe", lambda e, ps=ps, c=c, j=j, ck=ck: e.transpose(
                        out=ps.ap[:, j * 128:(j + 1) * 128], in_=x.ap[:, c, ck * 128:(ck + 1) * 128],
                        identity=ident.ap), r=[x.b(ck // 4), ident.b()], w=[ps.b()], sig=(j == 3))
                copy_op(ev_eng(), ot.ap[:, half * 512:(half + 1) * 512], ps.ap, [ps.b()], [ot.b()])
            dma("sp", yd[ck * 128:(ck + 1) * 128, :], ot.ap, r=[ot.b()])
        phase_sync()
        A.release(m)

    def rstd_from_ps(ps, rs, n, width):
        P.op("act", lambda e: e.activation(out=rs.ap[:, 0:width], in_=ps.ap[:, 0:width], func=AF.Ln,
                                           bias=epsb.ap, scale=1.0 / n), r=[ps.b(), epsb.b()], w=[rs.b()])
        P.op("act", lambda e: e.activation(out=rs.ap[:, 0:width], in_=rs.ap[:, 0:width], func=AF.Exp,
                                           scale=-0.5), r=[rs.b()], w=[rs.b()])

    def norm_mod(x, h, tiles, l, w_, kind_gs, kind_sh, tmp):
        sq, rs, t1 = tmp
        for ti, tl in enumerate(tiles):
            cols = slice(tl * 512, (tl + 1) * 512)
            ps = PS[ti % 2]
            for c in range(8):
                P.op("act", lambda e, c=c, cols=cols: e.activation(out=sq.ap[:, c, :], in_=x.ap[:, c, cols],
                                                                  func=AF.Square),
                     r=[x.b(tl)], w=[sq.b(c)])
                P.op("pe", lambda e, c=c, ps=ps: e.matmul(ps.ap, ones_bf.ap, sq.ap[:, c, :],
                                                         start=(c == 0), stop=(c == 7)),
                     r=[sq.b(c), ones_bf.b()], w=[ps.b()], sig=True)
            rstd_from_ps(ps, rs, D, 512)
            hc = slice(ti * 512, (ti + 1) * 512)
            for c in range(8):
                t = t1[c % 2]
                P.op("dve", lambda e, c=c, cols=cols, t=t: e.scalar_tensor_tensor(
                    out=t.ap, in0=x.ap[:, c, cols], scalar=mv.ap[:, l, w_, kind_gs, c:c + 1], in1=rs.ap,
                    op0=ALU.mult, op1=ALU.mult), r=[x.b(tl), mv.b(), rs.b()], w=[t.b()])
                P.op("act", lambda e, c=c, hc=hc, t=t: e.activation(
                    out=h.ap[:, c, hc], in_=t.ap, func=AF.Identity, bias=mv.ap[:, l, w_, kind_sh, c:c + 1],
                    scale=1.0), r=[t.b(), mv.b()], w=[h.b(ti)])

    def gated_residual(x, tl, ysb, ssq_ps, l, w_, kind_gg, tmp):
        sq, rs, t1 = tmp
        cols = slice(tl * 512, (tl + 1) * 512)
        rstd_from_ps(ssq_ps, rs, D, 512)
        for c in range(8):
            t = t1[c % 2]
            P.op("dve", lambda e, c=c, t=t: e.scalar_tensor_tensor(
                out=t.ap, in0=ysb.ap[:, c, :], scalar=mv.ap[:, l, w_, kind_gg, c:c + 1], in1=rs.ap,
                op0=ALU.mult, op1=ALU.mult), r=[ysb.b(c), mv.b(), rs.b()], w=[t.b()])
            P.op("dve", lambda e, c=c, t=t, cols=cols: e.tensor_tensor(
                out=x.ap[:, c, cols], in0=x.ap[:, c, cols], in1=t.ap, op=ALU.add),
                r=[t.b(), x.b(tl)], w=[x.b(tl)])

    def norm_mod_pieces(x, h, tl, ti, l, w_, kind_gs, kind_sh, tmp, ps):
        sq, rs, t1 = tmp
        cols = slice(tl * 512, (tl + 1) * 512)
        hc = slice(ti * 512, (ti + 1) * 512)

        def stats():
            for c in range(8):
                P.op("act", lambda e, c=c: e.activation(out=sq.ap[:, c % 4, :], in_=x.ap[:, c, cols],
                                                        func=AF.Square), r=[x.b(tl)], w=[sq.b(c % 4)])
                P.op("pe", lambda e, c=c: e.matmul(ps.ap, ones_bf.ap, sq.ap[:, c % 4, :],
                                                   start=(c == 0), stop=(c == 7)),
                     r=[sq.b(c % 4), ones_bf.b()], w=[ps.b()], sig=True)

        def half(c0):
            if c0 == 0:
                rstd_from_ps(ps, rs, D, 512)
            for c in range(c0, c0 + 4):
                t = t1[c % 2]
                P.op("dve", lambda e, c=c, t=t: e.scalar_tensor_tensor(
                    out=t.ap, in0=x.ap[:, c, cols], scalar=mv.ap[:, l, w_, kind_gs, c:c + 1], in1=rs.ap,
                    op0=ALU.mult, op1=ALU.mult), r=[x.b(tl), mv.b(), rs.b()], w=[t.b()])
                P.op("act", lambda e, c=c, t=t: e.activation(
                    out=h.ap[:, c, hc], in_=t.ap, func=AF.Identity, bias=mv.ap[:, l, w_, kind_sh, c:c + 1],
                    scale=1.0), r=[t.b(), mv.b()], w=[h.b(ti)])
        return [stats, lambda: half(0), lambda: half(4)]

    def gated_residual_pieces(x, tl, ysb, ssq_ps, l, w_, kind_gg, tmp):
        sq, rs, t1 = tmp
        cols = slice(tl * 512, (tl + 1) * 512)

        def piece(c):
            if c == 0:
                rstd_from_ps(ssq_ps, rs, D, 512)
            t = t1[c % 2]
            P.op("dve", lambda e: e.scalar_tensor_tensor(
                out=t.ap, in0=ysb.ap[:, c, :], scalar=mv.ap[:, l, w_, kind_gg, c:c + 1], in1=rs.ap,
                op0=ALU.mult, op1=ALU.mult), r=[ysb.b(c), mv.b(), rs.b()], w=[t.b()])
            P.op("dve", lambda e: e.tensor_tensor(
                out=x.ap[:, c, cols], in0=x.ap[:, c, cols], in1=t.ap, op=ALU.add),
                r=[t.b(), x.b(tl)], w=[x.b(tl)])
        return [lambda c=c: piece(c) for c in range(8)]

    class WStream:
        def __init__(self, kc, maxcols, nstage=2, name="ws"):
            self.kc = kc
            self.maxcols = maxcols
            self.st = [] if CASTMODE == "dma" else [A.t("%s_st%d" % (name, i), [kc, maxcols], F32) for i in range(nstage)]
            self.i = 0

        def load(self, dst_tile, dst_ap, slab_ap, kc, ncols, dst_buf):
            dma("pool", dst_ap, slab_ap.rearrange("p (k n) -> p k n", k=kc), w=[dst_buf], mld=4096)

    def ffn(x, tiles, l, w_, scr, h2_pre=None, next_tiles=None):
        m = A.mark()
        nt = len(tiles)
        h2 = h2_pre if h2_pre is not None else A.carve(scr, 0, 8 * 512 * nt, 8, "h2")
        a = A.carve(scr, 8 * 512 * nt, 8 * 512 * nt + FC * 512 * nt, FC, "a")
        sq = A.t("f_sq", [8, 512], BF16)
        rs = A.t("f_rs", [512], F32)
        t1 = [A.t("f_t%d" % i, [512], F32) for i in range(2)]
        tmp = (sq, rs, t1)
        if h2_pre is None:
            norm_mod(x, h2, tiles, l, w_, 3, 4, tmp)
        ws = WStream(8, 256, 2, "wgu")
        NR = 3
        wg = [A.t("wg%d" % i, [8, 256], BF16) for i in range(NR)]
        wu = [A.t("wu%d" % i, [8, 256], BF16) for i in range(NR)]
        sg = [A.t("sg%d" % i, [512], F32) for i in range(2)]
        k = 0
        for jp in range(FC // 2):
            bg_step()
            g_ = wg[jp % NR]
            u_ = wu[jp % NR]
            ws.load(g_, g_.ap, wgate_d[l, jp], 8, 256, g_.b())
            ws.load(u_, u_.ap, wup_d[l, jp], 8, 256, u_.b())
            for jj in range(2):
                j = jp * 2 + jj
                for ti in range(nt):
                    hc = slice(ti * 512, (ti + 1) * 512)
                    pg = PS[(k * 2) % 8]
                    pu = PS[(k * 2 + 1) % 8]
                    k += 1
                    for c in range(8):
                        P.op("pe", lambda e, c=c, pg=pg, g_=g_, jj=jj, hc=hc: e.matmul(
                            pg.ap, g_.ap[:, c, jj * 128:(jj + 1) * 128], h2.ap[:, c, hc],
                            start=(c == 0), stop=(c == 7)), r=[g_.b(), h2.b(ti)], w=[pg.b()], sig=(c == 7))
                    for c in range(8):
                        P.op("pe", lambda e, c=c, pu=pu, u_=u_, jj=jj, hc=hc: e.matmul(
                            pu.ap, u_.ap[:, c, jj * 128:(jj + 1) * 128], h2.ap[:, c, hc],
                            start=(c == 0), stop=(c == 7)), r=[u_.b(), h2.b(ti)], w=[pu.b()], sig=(c == 7))
                    s_ = sg[k % 2]
                    P.op("act", lambda e, s_=s_, pg=pg: e.activation(out=s_.ap, in_=pg.ap, func=AF.Silu),
                         r=[pg.b()], w=[s_.b()])
                    P.op("dve", lambda e, s_=s_, pu=pu, j=j, hc=hc: e.tensor_tensor(
                        out=a.ap[:, j, hc], in0=s_.ap, in1=pu.ap, op=ALU.mult),
                        r=[s_.b(), pu.b()], w=[a.b((j, ti))])
        phase_sync()
        A.release(m)
        rs = A.t("f_rs", [512], F32)
        t1 = [A.t("f_t%d" % i, [512], F32) for i in range(2)]
        tmp = (None, rs, t1)
        wsd = WStream(FC // 2, 128, 2, "wd")
        NWD = 2 if next_tiles else 3
        wd = [A.t("wdn%d" % i, [FC, 128], BF16) for i in range(NWD)]
        ysb = [A.t("ysb%d" % i, [8, 512], F32) for i in range(nt)]
        sqd = [A.t("sqd%d" % i, [512], BF16) for i in range(3)]
        ssq = [PS[6 + i] for i in range(nt)]
        k = 0
        pend = []
        todo = []
        h2n = None
        if next_tiles:
            nsq = A.t("f_nsq", [4, 512], BF16)
            nrs = A.t("f_nrs", [512], F32)
            nt1 = [A.t("f_nt%d" % i, [512], F32) for i in range(2)]
            h2n = A.carve(scr, 0, 8 * 512 * len(next_tiles), 8, "h2n")
            for ti2, tl2 in enumerate(next_tiles):
                todo.extend(norm_mod_pieces(x, h2n, tl2, ti2, l, w_, 3, 4, (nsq, nrs, nt1), PS[5]))
        for d in range(8):
            w_d = wd[d % NWD]
            for hf in range(2):
                wsd.load(w_d, w_d.ap[:, hf * 11:(hf + 1) * 11, :], wdown_d[l, d][:, hf * 1408:(hf + 1) * 1408],
                         11, 128, w_d.b(hf))
            for ti in range(nt):
                hc = slice(ti * 512, (ti + 1) * 512)
                ps = PS[k % 5]
                k += 1
                for j in range(FC):
                    P.op("pe", lambda e, j=j, ps=ps, w_d=w_d, hc=hc: e.matmul(
                        ps.ap, w_d.ap[:, j, :], a.ap[:, j, hc], start=(j == 0), stop=(j == FC - 1)),
                        r=[w_d.b(j // 11), a.b((j, ti))], w=[ps.b()], sig=(j == FC - 1))
                if pend:
                    pend.pop()()
                s_ = sqd[k % 3]
                P.op("act", lambda e, ps=ps, s_=s_: e.activation(out=s_.ap, in_=ps.ap, func=AF.Square),
                     r=[ps.b()], w=[s_.b()])
                copy_op("dve", ysb[ti].ap[:, d, :], ps.ap, [ps.b()], [ysb[ti].b(d)])
                if todo and k >= 2:
                    todo.pop(0)()

                def fin(ti=ti, s_=s_, d=d):
                    P.op("pe", lambda e: e.matmul(ssq[ti].ap, ones_bf.ap, s_.ap, start=(d == 0), stop=(d == 7)),
                         r=[s_.b(), ones_bf.b()], w=[ssq[ti].b()], sig=True)
                pend.append(fin)
        if pend:
            pend.pop()()
        while todo:
            todo.pop(0)()
        for ti, tl in enumerate(tiles):
            gated_residual(x, tl, ysb[ti], ssq[ti], l, w_, 5, tmp)
        phase_sync()
        A.release(m)
        return h2n

    def even_mixer(x, scr, NT, nseq, L, w_, is_sample, nk_out=None, nv_out=None):
        T = NT * 512
        nchunk = T // 128
        nctx = 2 if is_sample else 0
        m = A.mark()
        h = A.carve(scr, 0, 8 * T, 8, "h")
        cat = A.carve(scr, 8 * T, 16 * T, 8, "cat")
        sq = A.t("m_sq", [8, 512], BF16)
        rs = A.t("m_rs", [512], F32)
        t1 = [A.t("m_t%d" % i, [512], F32) for i in range(2)]
        tmp = (sq, rs, t1)
        norm_mod(x, h, list(range(NT)), 0, w_, 0, 1, tmp)
        phase_sync()
        A.release(m)
        if debug and not is_sample:
            dbg_dump("h0p", h.ap, [h.b(i) for i in range(NT)], [128, 8, T])
        if stage <= 1:
            phase_sync(); A.release(m); return
        ws = WStream(8, 128, 2, "wie")
        wsl = [A.t("wsl%d" % i, [8, 128], BF16) for i in range(5)]
        wcnt = [0]

        def slab(col0):
            s_ = wsl[wcnt[0] % 5]
            wcnt[0] += 1
            ws.load(s_, s_.ap, wine_d[col0 // 128], 8, 128, s_.b())
            return s_

        mc = A.mark()
        Lp = L + 2
        zl = [A.t("z%d" % i, [nseq, Lp], F32) for i in range(2)]
        accl = [A.t("acc%d" % i, [nseq, L], F32) for i in range(2)]
        gbl = [A.t("gb%d" % i, [T], BF16) for i in range(2)]
        gcs = [A.t("gcs%d" % i, [512], F32) for i in range(2)]
        spt = 512 // L if L < 512 else 1
        for z in zl:
            P.op("dve", lambda e, z=z: e.memset(z.ap, 0.0), w=[z.b()])
        kk = 0
        for c in range(4):
            bg_step()
            z, acc, gb = zl[c % 2], accl[c % 2], gbl[c % 2]
            s_gb = slab(1536 + c * 128)
            s_gc = slab(2048 + c * 128)
            s_xi = slab(2560 + c * 128)
            for tl in range(NT):
                cols = slice(tl * 512, (tl + 1) * 512)
                p_gb, p_gc, p_xi = PS[kk % 8], PS[(kk + 1) % 8], PS[(kk + 2) % 8]
                kk += 3
                for (pp, ss) in ((p_gb, s_gb), (p_gc, s_gc), (p_xi, s_xi)):
                    for k in range(8):
                        P.op("pe", lambda e, pp=pp, ss=ss, k=k, cols=cols: e.matmul(
                            pp.ap, ss.ap[:, k, :], h.ap[:, k, cols], start=(k == 0), stop=(k == 7)),
                            r=[ss.b(), h.b(tl)], w=[pp.b()], sig=(k == 7))
                copy_op("act", gb.ap[:, cols], p_gb.ap, [p_gb.b()], [gb.b()])
                g_ = gcs[tl % 2]
                copy_op("act", g_.ap, p_gc.ap, [p_gc.b()], [g_.b()])
                if L >= 512:
                    zo = z.ap[:, 0, 1 + tl * 512: 1 + (tl + 1) * 512]
                    xi = p_xi.ap
                    gi = g_.ap
                else:
                    zo = z.ap[:, tl * spt:(tl + 1) * spt, 1:1 + L]
                    xi = p_xi.ap.rearrange("p (s t) -> p s t", s=spt)
                    gi = g_.ap.rearrange("p (s t) -> p s t", s=spt)
                P.op("dve", lambda e, zo=zo, xi=xi, gi=gi: e.tensor_tensor(out=zo, in0=xi, in1=gi, op=ALU.mult),
                     r=[p_xi.b(), g_.b()], w=[z.b()])
            cw = lambda tap, c=c: vT.ap[:, 64 + tap * 4 + c: 64 + tap * 4 + c + 1]
            P.op("act", lambda e, cw=cw, z=z, acc=acc: e.activation(out=acc.ap, in_=z.ap[:, :, 1:1 + L], func=AF.Copy,
                                                      scale=cw(1)), r=[z.b(), vT.b()], w=[acc.b()])
            P.op("dve", lambda e, cw=cw, z=z, acc=acc: e.scalar_tensor_tensor(
                out=acc.ap, in0=z.ap[:, :, 0:L], scalar=cw(0), in1=acc.ap, op0=ALU.mult, op1=ALU.add),
                r=[z.b(), vT.b(), acc.b()], w=[acc.b()])
            P.op("dve", lambda e, cw=cw, z=z, acc=acc: e.scalar_tensor_tensor(
                out=acc.ap, in0=z.ap[:, :, 2:2 + L], scalar=cw(2), in1=acc.ap, op0=ALU.mult, op1=ALU.add),
                r=[z.b(), vT.b(), acc.b()], w=[acc.b()])
            P.op("dve", lambda e, c=c, acc=acc, gb=gb: e.tensor_tensor(
                out=cat.ap[:, 4 + c, :], in0=acc.ap.rearrange("p s t -> p (s t)"), in1=gb.ap, op=ALU.mult),
                r=[acc.b(), gb.b()], w=[cat.b(("cv", c))])
        phase_sync()
        A.release(mc)
        if debug and not is_sample:
            dbg_dump("catconv_p", cat.ap[:, 4:8, :], [cat.b(("cv", c)) for c in range(4)], [128, 4, T])
        if stage <= 2:
            phase_sync(); A.release(m); return

        nkeys = nctx * 128 + (T if is_sample else L)
        qh = A.t("qh", [T], BF16)
        kTh = A.t("kTh", [nctx * 128 + T], BF16)
        vh = A.t("vh", [nctx + nchunk, 128], BF16)
        if is_sample:
            rcos = A.t("rcos", [2048], F32)
            rsin = A.t("rsin", [2048], F32)
            dma("sp", rcos.ap, rcos_d, w=[rcos.b()])
            dma("sp", rsin.ap, rsin_d, w=[rsin.b()])
            qb = [A.t("qb%d" % i, [512], BF16) for i in range(1)] * 2
            rt1 = [A.t("rt1%d" % i, [512], F32) for i in range(1)] * 2
            rt2 = [A.t("rt2%d" % i, [512], F32) for i in range(1)] * 2
            ckst = A.t("ckst", [2, 128], F32)
            cvst = A.t("cvst", [2, 128], F32)
        else:
            kvo = [A.t("kvo%d" % i, [512], F32) for i in range(4)]
            kvc = [0]
        E = [[A.t("E%d%d" % (i, j), [512], BF16) for j in range(2)] for i in range(2)]
        r01 = [A.t("r01%d" % i, [512], F32) for i in range(2)]
        osb = A.t("osb", [512], F32)
        o2 = A.t("o2", [512], F32)
        osq = A.t("osq", [512], BF16)
        scale = 64 ** -0.5
        kq = [0]
        deferred = []

        def proj_fm(s_, dst, dst_off, rope, nm):
            for tl in range(NT):
                cols = slice(tl * 512, (tl + 1) * 512)
                dcols = slice(dst_off + tl * 512, dst_off + (tl + 1) * 512)
                ps = PS[kq[0] % 4]
                kq[0] += 1
                for k in range(8):
                    P.op("pe", lambda e, ps=ps, s_=s_, k=k, cols=cols: e.matmul(
                        ps.ap, s_.ap[:, k, :], h.ap[:, k, cols], start=(k == 0), stop=(k == 7)),
                        r=[s_.b(), h.b(tl)], w=[ps.b()], sig=(k == 7))
                if not rope:
                    copy_op(ev_eng(), dst.ap[:, dcols], ps.ap, [ps.b()], [dst.b(tl)])
                else:
                    q_ = qb[tl % 2]
                    a_ = rt1[tl % 2]
                    b_ = rt2[tl % 2]
                    ps2 = PS[4 + (kq[0] % 2)]
                    copy_op("act", q_.ap, ps.ap, [ps.b()], [q_.b()])
                    P.op("dve", lambda e, a_=a_, ps=ps, cols=cols: e.tensor_tensor(
                        out=a_.ap, in0=ps.ap, in1=rcos.ap[:, cols], op=ALU.mult),
                        r=[ps.b(), rcos.b()], w=[a_.b()])
                    P.op("pe", lambda e, ps2=ps2, q_=q_: e.matmul(ps2.ap, prot.ap, q_.ap, start=True, stop=True),
                         r=[prot.b(), q_.b()], w=[ps2.b()])
                    P.op("dve", lambda e, b_=b_, ps2=ps2, cols=cols: e.tensor_tensor(
                        out=b_.ap, in0=ps2.ap, in1=rsin.ap[:, cols], op=ALU.mult),
                        r=[ps2.b(), rsin.b()], w=[b_.b()])
                    P.op("dve", lambda e, a_=a_, b_=b_, dcols=dcols: e.tensor_tensor(
                        out=dst.ap[:, dcols], in0=a_.ap, in1=b_.ap, op=ALU.add),
                        r=[a_.b(), b_.b()], w=[dst.b(tl)])

        def proj_tm(s_, hd, to_v, out_d):
            for tl in range(NT):
                ps = PS[kq[0] % 4]
                kq[0] += 1
                for cq in range(4):
                    ck = tl * 4 + cq
                    for k in range(8):
                        P.op("pe", lambda e, ps=ps, s_=s_, k=k, ck=ck, cq=cq: e.matmul(
                            ps.ap[:, cq * 128:(cq + 1) * 128], h.ap[:, k, ck * 128:(ck + 1) * 128], s_.ap[:, k, :],
                            start=(k == 0), stop=(k == 7)),
                            r=[s_.b(), h.b(tl)], w=[ps.b()], sig=(k == 7 and cq == 3))
                if to_v:
                    copy_op("act", vh.ap[:, nctx + tl * 4: nctx + tl * 4 + 4, :],
                            ps.ap.rearrange("p (a b) -> p a b", a=4), [ps.b()], [vh.b(tl)])
                if out_d is not None:
                    ko = kvo[kvc[0] % 4]
                    kvc[0] += 1
                    copy_op("dve", ko.ap, ps.ap, [ps.b()], [ko.b()])
                    for sq_ in range(2):
                        for c2 in range(2):
                            dma("sp", out_d[2 * tl + sq_, hd, c2 * 128:(c2 + 1) * 128, :],
                                ko.ap[:, (sq_ * 2 + c2) * 128:(sq_ * 2 + c2 + 1) * 128], r=[ko.b()])

        for hd in range(4):
            bg_step()
            s_q = slab(hd * 128)
            proj_fm(s_q, qh, 0, is_sample, "q")
            s_k = slab(512 + hd * 128)
            proj_fm(s_k, kTh, nctx * 128, is_sample, "k")
            if not is_sample and 'tmk' not in SKIP:
                proj_tm(s_k, hd, False, nk_out if 'kvout' not in SKIP else None)
            s_v = slab(1024 + hd * 128)
            if 'tmv' not in SKIP:
                proj_tm(s_v, hd, True, (nv_out if 'kvout' not in SKIP else None) if not is_sample else None)
            if is_sample:
                dma("sp", ckst.ap, ck_d[hd].rearrange("(c t) d -> t c d", c=2), w=[ckst.b()])
                dma("sp", cvst.ap, cv_d[hd].rearrange("(c t) d -> t c d", c=2), w=[cvst.b()])
                ps = PS[4]
                for cq in range(2):
                    P.op("pe", lambda e, ps=ps, cq=cq: e.transpose(
                        out=ps.ap[:, cq * 128:(cq + 1) * 128], in_=ckst.ap[:, cq, :], identity=ident.ap),
                        r=[ckst.b(), ident.b()], w=[ps.b()], sig=(cq == 1))
                copy_op("dve", kTh.ap[:, 0:256], ps.ap[:, 0:256], [ps.b()], [kTh.b("ctx")])
                copy_op("act", vh.ap[:, 0:2, :], cvst.ap, [cvst.b()], [vh.b("ctx")])
            if debug and hd == 0 and 'qdump' not in SKIP:
                tag = "s" if is_sample else "p"
                dbg_dump("qh_" + tag, qh.ap, [qh.b(i) for i in range(NT)], [128, T])
                dbg_dump("kTh_" + tag, kTh.ap, [kTh.b(i) for i in range(NT)] + ([kTh.b("ctx")] if is_sample else []),
                         [128, nctx * 128 + T])
                dbg_dump("vh_" + tag, vh.ap, [vh.b(i) for i in range(NT)] + ([vh.b("ctx")] if is_sample else []),
                         [128, nctx + nchunk, 128])
            if is_sample:
                jobs = [(tl * 512, 512, list(range(nctx + nchunk)), tl) for tl in range(NT)]
            else:
                jobs = [(s * L, L, [s * (L // 128) + i for i in range(L // 128)], (s * L) // 512)
                        for s in range(nseq)]
            if stage < 3:
                jobs = []
            for (q0, nq, kcs, tl) in jobs:
                qc = slice(q0, q0 + nq)
                pO = [PS[4], PS[5]]
                pZ = [PS[6], PS[7]]
                kbufs = [kTh.b(i) for i in range(NT)] + ([kTh.b("ctx")] if is_sample else [])
                vbufs = [vh.b(i) for i in range(NT)] + ([vh.b("ctx")] if is_sample else [])
                nk_ = len(kcs)

                def emit_S(ki):
                    kc = kcs[ki]
                    kcol = slice(kc * 128, (kc + 1) * 128)
                    for comp in range(2):
                        pS = PS[(ki % 2) * 2 + comp]
                        pr = slice(comp * 64, (comp + 1) * 64)
                        P.op("pe", lambda e, pS=pS, pr=pr, kcol=kcol, qc=qc, nq=nq: e.matmul(
                            pS.ap[:, 0:nq], kTh.ap[pr, kcol], qh.ap[pr, qc], start=True, stop=True),
                            r=kbufs + [qh.b(tl)], w=[pS.b()])
                        E_ = E[ki % 2][comp]
                        P.op("act", lambda e, pS=pS, E_=E_, nq=nq: e.activation(
                            out=E_.ap[:, 0:nq], in_=pS.ap[:, 0:nq], func=AF.Exp, scale=scale),
                            r=[pS.b()], w=[E_.b()])

                def emit_PV(ki):
                    kc = kcs[ki]
                    first = (ki == 0)
                    last = (ki == nk_ - 1)
                    for comp in range(2):
                        E_ = E[ki % 2][comp]
                        P.op("pe", lambda e, comp=comp, E_=E_, kc=kc, nq=nq, first=first, last=last: e.matmul(
                            pO[comp].ap[:, 0:nq], vh.ap[:, kc, :], E_.ap[:, 0:nq], start=first, stop=last),
                            r=vbufs + [E_.b()], w=[pO[comp].b()], sig=False)
                        P.op("pe", lambda e, comp=comp, E_=E_, nq=nq, first=first, last=last: e.matmul(
                            pZ[comp].ap[:, 0:nq], ones_bf.ap, E_.ap[:, 0:nq], start=first, stop=last),
                            r=[ones_bf.b(), E_.b()], w=[pZ[comp].b()], sig=True)

                emit_S(0)
                if nk_ > 1:
                    emit_S(1)
                for ki in range(nk_):
                    emit_PV(ki)
                    if ki + 2 < nk_:
                        emit_S(ki + 2)
                if deferred:
                    deferred.pop()()
                P.op("act", lambda e, nq=nq: e.activation(out=osb.ap[:, 0:nq], in_=pO[0].ap[:, 0:nq], func=AF.Copy),
                     r=[pO[0].b()], w=[osb.b()])
                P.op("act", lambda e, nq=nq: e.activation(out=o2.ap[:, 0:nq], in_=pO[1].ap[:, 0:nq], func=AF.Copy),
                     r=[pO[1].b()], w=[o2.b()])
                for comp in range(2):
                    P.op("dve", lambda e, comp=comp, nq=nq: e.tensor_copy(out=r01[comp].ap[:, 0:nq],
                                                                          in_=pZ[comp].ap[:, 0:nq]),
                         r=[pZ[comp].b()], w=[r01[comp].b()])
                for comp in range(2):
                    P.op("dve", lambda e, comp=comp, nq=nq: e.reciprocal(out=r01[comp].ap[:, 0:nq],
                                                                        in_=r01[comp].ap[:, 0:nq]),
                         r=[r01[comp].b()], w=[r01[comp].b()])
                P.op("dve", lambda e, nq=nq: e.tensor_tensor(out=osb.ap[:, 0:nq], in0=osb.ap[:, 0:nq],
                                                             in1=r01[0].ap[:, 0:nq], op=ALU.mult),
                     r=[osb.b(), r01[0].b()], w=[osb.b()])
                P.op("dve", lambda e, nq=nq: e.tensor_tensor(out=o2.ap[:, 0:nq], in0=o2.ap[:, 0:nq],
                                                             in1=r01[1].ap[:, 0:nq], op=ALU.mult),
                     r=[o2.b(), r01[1].b()], w=[o2.b()])
                P.op("dve", lambda e, nq=nq: e.scalar_tensor_tensor(
                    out=osb.ap[:, 0:nq], in0=o2.ap[:, 0:nq], scalar=neglam.ap[:, 0:1], in1=osb.ap[:, 0:nq],
                    op0=ALU.mult, op1=ALU.add), r=[o2.b(), neglam.b(), osb.b()], w=[osb.b()])
                def tail(nq=nq, hd=hd, qc=qc, q0=q0):
                    pN = PS[0]
                    P.op("act", lambda e: e.activation(out=osq.ap[:, 0:nq], in_=osb.ap[:, 0:nq], func=AF.Square),
                         r=[osb.b()], w=[osq.b()])
                    P.op("pe", lambda e: e.matmul(pN.ap[:, 0:nq], ones_bf.ap, osq.ap[:, 0:nq],
                                                  start=True, stop=True),
                         r=[ones_bf.b(), osq.b()], w=[pN.b()])
                    rs_o = r01[0]
                    P.op("act", lambda e: e.activation(
                        out=rs_o.ap[:, 0:nq], in_=pN.ap[:, 0:nq], func=AF.Ln, bias=epsb.ap, scale=1.0 / 128),
                        r=[pN.b(), epsb.b()], w=[rs_o.b()])
                    P.op("act", lambda e: e.activation(out=rs_o.ap[:, 0:nq], in_=rs_o.ap[:, 0:nq], func=AF.Exp,
                                                       scale=-0.5), r=[rs_o.b()], w=[rs_o.b()])
                    P.op("dve", lambda e: e.scalar_tensor_tensor(
                        out=cat.ap[:, hd, qc], in0=osb.ap[:, 0:nq], scalar=sublns.ap[:, 0:1], in1=rs_o.ap[:, 0:nq],
                        op0=ALU.mult, op1=ALU.mult), r=[osb.b(), sublns.b(), rs_o.b()], w=[cat.b(("o", hd, q0))])
                deferred.append(tail)
        if deferred:
            deferred.pop()()
        phase_sync()
        A.release(m)
        if debug:
            tag = "s" if is_sample else "p"
            dbg_dump("cat0_" + tag, cat.ap, [], [128, 8, T])
            phase_sync()
        if stage <= 3:
            return
        out_proj_residual(x, cat, NT, woute_d, 0, w_)

    def out_proj_residual(x, cat, NT, wd, l, w_):
        m = A.mark()
        wo = A.t("wo", [8, 1024], BF16)
        ws = WStream(8, 256, 2, "wos")
        for i in range(4):
            ws.load(wo, wo.ap[:, :, i * 256:(i + 1) * 256], wd[i], 8, 256, wo.b(i))
        rsl = [A.t("o_rs%d" % i, [512], F32) for i in range(2)]
        t1 = [A.t("o_t%d" % i, [512], F32) for i in range(2)]
        ysbl = [A.t("o_ysb%d" % i, [8, 512], F32) for i in range(2)]
        sqd = [A.t("o_sqd%d" % i, [512], BF16) for i in range(3)]
        k = 0
        pend = []
        todo = []
        for tl in range(NT):
            cols = slice(tl * 512, (tl + 1) * 512)
            ssq = PS[6 + tl % 2]
            ysb = ysbl[tl % 2]
            for d in range(8):
                ps = PS[k % 6]
                k += 1
                for c in range(8):
                    P.op("pe", lambda e, ps=ps, c=c, d=d, cols=cols: e.matmul(
                        ps.ap, wo.ap[:, c, d * 128:(d + 1) * 128], cat.ap[:, c, cols],
                        start=(c == 0), stop=(c == 7)), r=[wo.b(d // 2)], w=[ps.b()], sig=(c == 7))
                if pend:
                    pend.pop()()
                s_ = sqd[k % 3]
                P.op("act", lambda e, ps=ps, s_=s_: e.activation(out=s_.ap, in_=ps.ap, func=AF.Square),
                     r=[ps.b()], w=[s_.b()])
                copy_op("dve", ysb.ap[:, d, :], ps.ap, [ps.b()], [ysb.b(d)])
                if todo:
                    todo.pop(0)()

                def fin(ssq=ssq, s_=s_, d=d, tl=tl, ysb=ysb):
                    P.op("pe", lambda e: e.matmul(ssq.ap, ones_bf.ap, s_.ap, start=(d == 0), stop=(d == 7)),
                         r=[s_.b(), ones_bf.b()], w=[ssq.b()], sig=True)
                    if d == 7:
                        while todo:
                            todo.pop(0)()
                        todo.extend(gated_residual_pieces(x, tl, ysb, ssq, l, w_, 2, (None, rsl[tl % 2], t1)))
                pend.append(fin)
        if pend:
            pend.pop()()
        while todo:
            todo.pop(0)()
        phase_sync()
        A.release(m)

    def odd_mixer(x, scr, NT, nseq, L, w_, is_sample):
        T = NT * 512
        nchunk = T // 128
        cps = L // 128
        m = A.mark()
        hA = A.carve(scr, 0, 8 * 512, 8, "h1")
        cat = A.carve(scr, 8 * T, 16 * T, 8, "cat1")
        u = A.t("u", [nchunk, 1024], BF16)
        m1 = A.mark()
        if is_sample:
            wi = A.carve(scr, 4096, 4096 + 8192, 8, "wi")
        else:
            wi = A.t("wi", [8, 1024], BF16)
        ws = WStream(8, 128 if is_sample else 256, 2, "wio")
        for i in range(4):
            ws.load(wi, wi.ap[:, :, i * 256:(i + 1) * 256], wino_d[i], 8, 256, wi.b(i))
        sq = A.t("m_sq", [8, 512], BF16)
        rs = A.t("m_rs", [512], F32)
        t1 = [A.t("m_t%d" % i, [512], F32) for i in range(2)]
        tmp = (sq, rs, t1)
        k = 0
        if is_sample:
            hB = A.carve(scr, 12288, 16384, 8, "h1b")
        else:
            hB = A.carve(scr, 4096, 8192, 8, "h1b")
        hAB = [hA, hB]
        norm_mod(x, hAB[0], [0], 1, w_, 0, 1, tmp)
        for tl in range(NT):
            hcur = hAB[tl % 2]
            if tl + 1 < NT:
                norm_mod(x, hAB[(tl + 1) % 2], [tl + 1], 1, w_, 0, 1, tmp)
            for cq in range(4):
                ck = tl * 4 + cq
                for half in range(2):
                    ps = PS[2 + (k % 6)]
                    k += 1
                    for c in range(8):
                        P.op("pe", lambda e, ps=ps, c=c, cq=cq, half=half, hcur=hcur: e.matmul(
                            ps.ap, hcur.ap[:, c, cq * 128:(cq + 1) * 128], wi.ap[:, c, half * 512:(half + 1) * 512],
                            start=(c == 0), stop=(c == 7)),
                            r=[hcur.b(0), wi.b(half * 2), wi.b(half * 2 + 1)], w=[ps.b()], sig=(c == 7))
                    copy_op(ev_eng(), u.ap[:, ck, half * 512:(half + 1) * 512], ps.ap, [ps.b()], [u.b(ck)])
        phase_sync()
        A.release(m1)
        if debug and not is_sample:
            dbg_dump("u_p", u.ap, [], [128, nchunk, 1024])
            phase_sync()
        if is_sample:
            tabc = A.carve(scr, 0, 8192, 16, "tabc")
            tabs = A.carve(scr, 8192, 16384, 16, "tabs")
        pd = [A.t("pd%d" % i, [512], BF16) for i in range(2)]
        cu = [A.t("cu%d" % i, [512], BF16) for i in range(2)]
        su = [A.t("su%d" % i, [512], BF16) for i in range(2)]
        fr = [A.t("fr%d" % i, [512], BF16) for i in range(2)]
        ubufs = [u.b(i) for i in range(nchunk)]
        items = [(tl, g) for tl in range(NT) for g in range(4)]
        nit = len(items)

        def stage_A(i):
            tl, g = items[i]
            if g == 0 and is_sample:
                dma("sp", tabc.ap, dft2kc_d[tl].rearrange("p (a b) -> p a b", a=16), w=[tabc.b()])
                dma("sp", tabs.ap, dft2ks_d[tl].rearrange("p (a b) -> p a b", a=16), w=[tabs.b()])
            ps = PS[0 + i % 2]
            pc = PS[2 + i % 2]
            pz = PS[4 + i % 2]
            for bq in range(4):
                ck = tl * 4 + bq
                pos = ck % cps
                kind = 0 if pos == 0 else (2 if pos == cps - 1 else 1)
                terms = [(ck, kind)]
                if pos > 0:
                    terms.append((ck - 1, 3))
                if pos < cps - 1:
                    terms.append((ck + 1, 4))
                for j_, (cs, kd) in enumerate(terms):
                    P.op("pe", lambda e, ps=ps, bq=bq, cs=cs, kd=kd, g=g, j_=j_, n=len(terms): e.matmul(
                        ps.ap[:, bq * 128:(bq + 1) * 128], u.ap[:, cs, g * 128:(g + 1) * 128],
                        band.ap[:, g * 5 + kd, :], start=(j_ == 0), stop=(j_ == n - 1)),
                        r=ubufs + [band.b()], w=[ps.b()], sig=(bq == 3 and j_ == len(terms) - 1))
            pd_ = pd[i % 2]
            copy_op("act", pd_.ap, ps.ap, [ps.b()], [pd_.b()])
            if is_sample:
                for a_ in range(16):
                    P.op("pe", lambda e, pc=pc, a_=a_, g=g: e.matmul(
                        pc.ap, u.ap[:, a_, 512 + g * 128: 512 + (g + 1) * 128], tabc.ap[:, a_, :],
                        start=(a_ == 0), stop=(a_ == 15)), r=ubufs + [tabc.b()], w=[pc.b()], sig=(a_ == 15))
                for a_ in range(16):
                    P.op("pe", lambda e, pz=pz, a_=a_, g=g: e.matmul(
                        pz.ap, u.ap[:, a_, 512 + g * 128: 512 + (g + 1) * 128], tabs.ap[:, a_, :],
                        start=(a_ == 0), stop=(a_ == 15)), r=ubufs + [tabs.b()], w=[pz.b()], sig=(a_ == 15))
            else:
                for s_i in range(2):
                    for a_ in range(2):
                        ck = tl * 4 + s_i * 2 + a_
                        P.op("pe", lambda e, pc=pc, a_=a_, g=g, s_i=s_i, ck=ck: e.matmul(
                            pc.ap[:, s_i * 256:(s_i + 1) * 256], u.ap[:, ck, 512 + g * 128: 512 + (g + 1) * 128],
                            dft256.ap[:, a_, :], start=(a_ == 0), stop=(a_ == 1)),
                            r=ubufs + [dft256.b()], w=[pc.b()], sig=(a_ == 1 and s_i == 1))
                for s_i in range(2):
                    for a_ in range(2):
                        ck = tl * 4 + s_i * 2 + a_
                        P.op("pe", lambda e, pz=pz, a_=a_, g=g, s_i=s_i, ck=ck: e.matmul(
                            pz.ap[:, s_i * 256:(s_i + 1) * 256], u.ap[:, ck, 512 + g * 128: 512 + (g + 1) * 128],
                            dft256.ap[:, 2 + a_, :], start=(a_ == 0), stop=(a_ == 1)),
                            r=ubufs + [dft256.b()], w=[pz.b()], sig=(a_ == 1 and s_i == 1))
            copy_op("act", cu[i % 2].ap, pc.ap, [pc.b()], [cu[i % 2].b()])
            copy_op("dve", su[i % 2].ap, pz.ap, [pz.b()], [su[i % 2].b()])

        def stage_B(i):
            tl, g = items[i]
            cols = slice(tl * 512, (tl + 1) * 512)
            cu_, su_, pd_, fr_ = cu[i % 2], su[i % 2], pd[i % 2], fr[i % 2]
            pf = PS[7]
            P.op("pe", lambda e: e.matmul(pf.ap, dftc.ap[:, 0:128], cu_.ap, start=True, stop=False),
                 r=[dftc.b(), cu_.b()], w=[pf.b()], sig=False)
            P.op("pe", lambda e: e.matmul(pf.ap, dftc.ap[:, 128:256], su_.ap, start=False, stop=True),
                 r=[dftc.b(), su_.b()], w=[pf.b()])
            copy_op("act", fr_.ap, pf.ap, [pf.b()], [fr_.b()])
            ps2 = PS[6]
            P.op("pe", lambda e: e.matmul(ps2.ap, wpool.ap[:, g, :], pd_.ap, start=True, stop=True),
                 r=[wpool.b(), pd_.b()], w=[ps2.b()])
            copy_op("dve", cat.ap[:, g, cols], ps2.ap, [ps2.b()], [cat.b((tl, g))],
                    scale=vT.ap[:, 76 + g:77 + g])

        def stage_C(i):
            tl, g = items[i]
            cols = slice(tl * 512, (tl + 1) * 512)
            fr_ = fr[i % 2]
            pw = PS[6]
            P.op("pe", lambda e: e.matmul(pw.ap, wfour.ap[:, g, :], fr_.ap, start=True, stop=True),
                 r=[wfour.b(), fr_.b()], w=[pw.b()])
            copy_op("dve", cat.ap[:, 4 + g, cols], pw.ap, [pw.b()], [cat.b((tl, 4 + g))])

        for idx in range(nit + 2):
            if idx < nit:
                stage_A(idx)
            if idx >= 2:
                stage_C(idx - 2)
            if 1 <= idx <= nit:
                stage_B(idx - 1)
        phase_sync()
        A.release(m)
        if debug and not is_sample:
            dbg_dump("cat1_p", cat.ap, [], [128, 8, T])
            phase_sync()
        out_proj_residual(x, cat, NT, wouto_d, 1, w_)

    def run_group(xd, yd, NT, nseq, L, w_, is_sample):
        T = NT * 512
        mg = A.mark()
        if not is_sample:
            wms1 = [A.t("wmsB%d" % i, [8, 512], BF16) for i in range(3)]
            rowblk1 = A.t("rowblkB", [512], F32, parts=2)
            bg.extend(mod_steps(1, wms1, rowblk1))
        scr = A.t("scr", [max(16 * T, 30 * 1024)], BF16)
        x = A.t("x", [8, T], F32)
        load_x(xd, x, T)
        if debug and not is_sample:
            dbg_dump("x0p", x.ap, [x.b(i) for i in range(NT)], [128, 8, T])
        for l in range(2):
            if l == 0:
                even_mixer(x, scr, NT, nseq, L, w_, is_sample,
                           nk_out=None if is_sample else nk_d, nv_out=None if is_sample else nv_d)
            else:
                bg_flush()
                odd_mixer(x, scr, NT, nseq, L, w_, is_sample)
            if stage <= 3 + 3 * l:
                break
            if debug and not is_sample:
                dbg_dump("xmid%d_p" % l, x.ap, [x.b(i) for i in range(NT)], [128, 8, T])
            h2p = None
            for h0 in range(0, NT, 2):
                tiles = list(range(h0, min(h0 + 2, NT)))
                nxt = list(range(h0 + 2, min(h0 + 4, NT)))
                h2p = ffn(x, tiles, l, w_, scr, h2_pre=h2p, next_tiles=nxt or None)
            if debug and not is_sample:
                dbg_dump("xl%d_p" % l, x.ap, [x.b(i) for i in range(NT)], [128, 8, T])
            if stage <= 4 + 3 * l:
                break
        store_x(yd, x, T)
        phase_sync()
        A.release(mg)

    if stage >= 1:
        run_group(xp_d, yp_d, 2, 4, 256, 0, False)
    if stage >= 50:
        run_group(xs_d, ys_d, 4, 1, 2048, 1, True)

    P.barrier()

    with nc.Block() as block:
        @block.tensor
        def _(e):
            P.replay("pe", e)

        @block.scalar
        def _(e):
            P.replay("act", e)

        @block.vector
        def _(e):
            P.replay("dve", e)

        @block.gpsimd
        def _(e):
            P.replay("pool", e)

        @block.sync
        def _(e):
            P.replay("sp", e)
    es.close()
    return nc, dbg, {"nops": P.nops, "peak": A.peak, "marks": P.marks}


_PROG = {}


def _get_prog(debug, stage):
    key = (debug, stage)
    if key not in _PROG:
        _PROG[key] = build_program(debug, stage)
    return _PROG[key]


def kernel(x_prompt, x_sample, cache_k, cache_v, c, c_ctx, w_mod, b_mod, norm_g,
           w_in_even, lam_params, subln_g, conv_w, w_out_even,
           w_in_odd, w_pool, pool_scale, w_fourier, w_out_odd,
           w_gate, w_up, w_down, _debug=False, _stage=99):
    global _CONST
    if _CONST is None:
        _CONST = make_constants()
    f = lambda a: np.ascontiguousarray(np.asarray(a, dtype=np.float32))
    x_prompt, x_sample, cache_k, cache_v = f(x_prompt), f(x_sample), f(cache_k), f(cache_v)
    c, c_ctx, b_mod, norm_g = f(c), f(c_ctx), f(b_mod), f(norm_g)
    nc, dbg, info = _get_prog(_debug, _stage)
    def slabs(W, ncols):
        K, N = W.shape
        return np.ascontiguousarray(
            W.reshape(K // 128, 128, N // ncols, ncols).transpose(2, 1, 0, 3).reshape(N // ncols, 128, (K // 128) * ncols))

    shared = {
        "w_mod": np.stack([slabs(f(w_mod)[l], 512) for l in range(2)]),
        "w_in_even": slabs(f(w_in_even)[0], 128), "w_out_even": slabs(f(w_out_even)[0], 256),
        "w_in_odd": slabs(f(w_in_odd)[0], 256), "w_out_odd": slabs(f(w_out_odd)[0], 256),
        "w_pool": f(w_pool)[0], "w_fourier": f(w_fourier)[0],
        "w_gate": np.stack([slabs(f(w_gate)[l], 256) for l in range(2)]),
        "w_up": np.stack([slabs(f(w_up)[l], 256) for l in range(2)]),
        "w_down": np.stack([slabs(f(w_down)[l], 128) for l in range(2)]),
        "lamp": f(lam_params).reshape(1, 256),
    }
    shared.update(_CONST)
    bpack = np.zeros((128, 128), np.float32)
    bpack[0:96] = b_mod.reshape(96, 128)
    in_maps = []
    for i in range(NCORES):
        vpack = np.zeros((128, 128), np.float32)
        vpack[0:64] = norm_g.reshape(64, 128)
        vpack[64:76] = f(conv_w)[0].reshape(12, 128)
        vpack[76:80] = f(pool_scale)[0].reshape(4, 128)
        vpack[80] = f(subln_g)[0]
        vpack[81:89] = c[i].reshape(8, 128)
        vpack[89:97] = c_ctx.reshape(8, 128)
        mp = dict(shared)
        mp.update({
            "xp": x_prompt[4 * i:4 * i + 4].reshape(1024, D),
            "xs": x_sample[i],
            "ck": cache_k[i, 0], "cv": cache_v[i, 0],
            "vpack": vpack, "bpack": bpack,
        })
        in_maps.append(mp)
    res = run_bass_kernel_spmd(nc, in_maps, core_ids=list(range(NCORES)))
    rs = res.results
    yp = np.concatenate([r["yp"].reshape(4, 256, D) for r in rs], axis=0)
    ys = np.stack([r["ys"] for r in rs], axis=0)
    nk = np.concatenate([r["nk"] for r in rs], axis=0)[:, None]
    nv = np.concatenate([r["nv"] for r in rs], axis=0)[:, None]
    if _debug:
        return (yp, ys, nk, nv), [{k: r["dbg_" + k] for k in dbg} for r in rs]
    return (yp.astype(np.float32), ys.astype(np.float32), nk.astype(np.float32), nv.astype(np.float32))
```

```python
import math
from contextlib import ExitStack

import numpy as np
import ml_dtypes

import concourse.bass as bass
import concourse.mybir as mybir
from concourse.bass_utils import run_bass_kernel_spmd

F32 = mybir.dt.float32
BF16 = mybir.dt.bfloat16
U8 = mybir.dt.uint8
AF = mybir.ActivationFunctionType
ALU = mybir.AluOpType
AX = mybir.AxisListType

D = 1024
KC = 8
DFF = 2816
FC = 22
NCORES = 8
EPS = 1e-6
LAM_INIT = 0.8 - 0.6 * math.exp(-0.3 * 0)
ARENA_BYTES = 212000

import os
SKIP = set(os.environ.get('K_SKIP', '').split(','))
DEBUG = False
CASTMODE = os.environ.get('K_CAST', 'dma')
FULLBAR = os.environ.get('K_FULLBAR', '') == '1'
SAMEENG = False
STAGE = 99


class Buf:
    __slots__ = ("name", "w", "w_eng", "r", "pend", "excl")

    def __init__(self, name=""):
        self.name = name
        self.excl = False
        self.w = None
        self.w_eng = None
        self.r = []
        self.pend = None


class Tile:
    def __init__(self, ap, name):
        self.ap = ap
        self.name = name
        self.subs = {}
        self.inh = []
        self.abs_lo = None

    def b(self, key=None):
        if key not in self.subs:
            bf = Buf("%s/%s" % (self.name, key))
            bf.r = list(self.inh)
            self.subs[key] = bf
        return self.subs[key]


class Prog:
    ENGS = ("pe", "act", "dve", "pool", "sp")
    NDMA = 16

    def __init__(self, nc, es):
        self.nc = nc
        self.streams = {e: [] for e in self.ENGS}
        self.sem = {}
        self.cnt = {}
        for e in ("pe", "act", "dve", "pool"):
            self.sem[e] = es.enter_context(nc.semaphore("s_" + e))
            self.cnt[e] = 0
        for i in range(self.NDMA):
            for pre in ("dmaH", "dmaS"):
                n = "%s%d" % (pre, i)
                self.sem[n] = es.enter_context(nc.semaphore("s_" + n))
                self.cnt[n] = 0
        self.dma_rr = {"dmaH": 0, "dmaS": 0}
        self.waited = {e: {} for e in self.ENGS}
        self.pending = {e: [] for e in self.ENGS}
        self.nops = 0
        self.marks = []

    def _need(self, eng, need, tok):
        s, v = tok
        if self.waited[eng].get(s, 0) >= v:
            return
        if need.get(s, 0) < v:
            need[s] = v

    def op(self, eng, fn, r=(), w=(), sig=True, dma=False):
        need = {}
        for b in r:
            if b.pend is not None and b.pend != eng:
                raise RuntimeError("read of buffer with pending unsignalled write: " + b.name)
            if b.w is not None:
                if not (b.pend == eng):
                    self._need(eng, need, b.w)
            if b.excl:
                for (t, e2) in b.r:
                    if e2 != eng:
                        self._need(eng, need, t)
        for b in w:
            if b.pend is not None and (b.pend != eng or dma):
                raise RuntimeError("write to buffer with pending unsignalled access: " + b.name)
            if b.w is not None and (b.w_eng != eng or dma or b.w_eng == "dma" or (SAMEENG and eng != "pe")):
                self._need(eng, need, b.w)
            for (t, e2) in b.r:
                if e2 != eng or dma or (SAMEENG and eng != "pe"):
                    self._need(eng, need, t)
        signal = None
        if dma:
            pre = "dmaS" if eng == "pool" else "dmaH"
            sname = "%s%d" % (pre, self.dma_rr[pre])
            self.dma_rr[pre] = (self.dma_rr[pre] + 1) % self.NDMA
            if self.cnt[sname] > 0:
                self._need(eng, need, (sname, self.cnt[sname]))
            self.cnt[sname] += 16
            tok = (sname, self.cnt[sname])
            signal = (sname, 16)
            peng = "dma"
        elif sig:
            self.cnt[eng] += 1
            tok = (eng, self.cnt[eng])
            signal = (eng, 1)
            peng = eng
        else:
            tok = None
            peng = eng
        waits = sorted(need.items())
        for s, v in waits:
            self.waited[eng][s] = v
        self.streams[eng].append((fn, waits, signal))
        self.nops += 1
        if tok is None:
            for b in r:
                self.pending[eng].append((b, "r"))
            for b in w:
                self.pending[eng].append((b, "w"))
                b.pend = eng
            return None
        allr = [(b, "r") for b in r]
        allw = [(b, "w") for b in w]
        if not dma:
            allr += [(b, k) for (b, k) in self.pending[eng] if k == "r"]
            allw += [(b, k) for (b, k) in self.pending[eng] if k == "w"]
            self.pending[eng] = []
        for b, _ in allw:
            b.w = tok
            b.w_eng = peng
            b.r = []
            b.pend = None
        for b, _ in allr:
            b.r.append((tok, peng))
        return tok

    def barrier(self):
        import inspect
        fr = inspect.stack()[1]
        self.marks.append((fr.function, fr.lineno, sum(1 for o in self.streams["pe"] if o[0] is not None)))
        toks = [(s, c) for s, c in self.cnt.items() if c > 0]
        for e in self.ENGS:
            need = {}
            for t in toks:
                self._need(e, need, t)
            waits = sorted(need.items())
            if waits:
                for s, v in waits:
                    self.waited[e][s] = v
                self.streams[e].append((None, waits, None))

    def replay(self, eng, e):
        for fn, waits, signal in self.streams[eng]:
            for s, v in waits:
                e.wait_ge(self.sem[s], v)
            if fn is None:
                continue
            ins = fn(e)
            if signal is not None:
                ins.then_inc(self.sem[signal[0]], signal[1])


class Arena:
    def __init__(self, ar, nbytes):
        self.ar = ar
        self.top = 0
        self.size = nbytes
        self.peak = 0
        self.reg = []
        self.prog = None

    def register(self, lo, hi, tile):
        if self.prog is not None:
            for e_, lst in self.prog.pending.items():
                if lst:
                    raise RuntimeError("tile allocation with unsignalled ops pending on " + e_)
        inh = {}
        for (l2, h2, t2) in self.reg:
            if l2 < hi and lo < h2:
                toks = list(t2.inh)
                for b in t2.subs.values():
                    if b.w is not None:
                        toks.append((b.w, b.w_eng))
                    toks.extend(b.r)
                for (tk, e2) in toks:
                    sname, v = tk
                    if sname not in inh or inh[sname][0][1] < v:
                        inh[sname] = ((sname, v), e2)
        tile.inh = list(inh.values())
        tile.abs_lo = lo
        self.reg = [(l2, h2, t2) for (l2, h2, t2) in self.reg if not (lo <= l2 and h2 <= hi)]
        self.reg.append((lo, hi, tile))

    def carve(self, parent, lo_el, hi_el, c, name, esz=2):
        ap = parent.ap[:, lo_el:hi_el].rearrange("p (c t) -> p c t", c=c)
        t_ = Tile(ap, name)
        self.register(parent.abs_lo + esz * lo_el, parent.abs_lo + esz * hi_el, t_)
        return t_

    def mark(self):
        return self.top

    def release(self, m):
        self.top = m

    def t(self, name, shape, dtype, parts=128):
        esz = 4 if dtype == F32 else 2
        n = 1
        for s in shape:
            n *= s
        nb = (n * esz + 63) // 64 * 64
        if self.top + nb > self.size:
            raise RuntimeError("arena overflow at %s: top=%d need=%d" % (name, self.top, nb))
        ap = self.ar[0:parts, self.top:self.top + n * esz].bitcast(dtype)
        lo_ = self.top
        self.top += nb
        self.peak = max(self.peak, self.top)
        if len(shape) == 2:
            ap = ap.rearrange("p (a b) -> p a b", a=shape[0])
        elif len(shape) == 3:
            ap = ap.rearrange("p (a b c) -> p a b c", a=shape[0], b=shape[1])
        elif len(shape) == 4:
            ap = ap.rearrange("p (a b c d) -> p a b c d", a=shape[0], b=shape[1], c=shape[2])
        t_ = Tile(ap, name)
        self.register(lo_, lo_ + nb, t_)
        return t_


def _bf(a):
    return np.ascontiguousarray(a.astype(np.float32)).astype(ml_dtypes.bfloat16)


def make_constants():
    c = {}
    c["ident"] = np.eye(128, dtype=np.float32)
    P = np.zeros((128, 128), np.float32)
    for blk in range(4):
        o = blk * 32
        for j in range(16):
            P[o + j + 16, o + j] = -1.0
            P[o + j, o + j + 16] = 1.0
    c["prot"] = _bf(P)
    n_tok = 2048
    rows = np.repeat(np.arange(n_tok // 64), 64).astype(np.float32)
    cols = np.tile(np.arange(64), n_tok // 64).astype(np.float32)
    inv = (10000.0 ** (-np.arange(0, 32, 2, dtype=np.float32) / 32)).astype(np.float32)
    ang_r = rows[:, None] * inv[None, :]
    ang_c = cols[:, None] * inv[None, :]
    ang = np.concatenate([ang_r, ang_r, ang_c, ang_c], axis=-1).astype(np.float32)
    cos = np.cos(ang).astype(np.float32).T
    sin = np.sin(ang).astype(np.float32).T
    c["rcos"] = np.ascontiguousarray(np.concatenate([cos, cos], axis=0))
    c["rsin"] = np.ascontiguousarray(np.concatenate([sin, sin], axis=0))
    wins = (2, 4, 8, 16)
    n = 384
    band = np.zeros((4, 5, 128, 128), np.float32)
    for g, win in enumerate(wins):
        def full(nn):
            M = np.zeros((nn, nn), np.float64)
            for t in range(nn):
                lo = min(max(t - win // 2, 0), nn)
                hi = min(max(t - win // 2 + win, 0), nn)
                M[lo:hi, t] = 1.0 / (hi - lo)
                M[t, t] -= 1.0
            return M
        M = full(n)
        band[g, 0] = M[0:128, 0:128]
        band[g, 1] = M[128:256, 128:256]
        band[g, 2] = M[256:384, 256:384]
        band[g, 3] = M[0:128, 128:256]
        band[g, 4] = M[256:384, 128:256]
    c["band"] = _bf(band.transpose(2, 0, 1, 3).reshape(128, 20 * 128))
    def dft(nn):
        k = np.arange(nn, dtype=np.int64)
        ph = (np.outer(k, k) % nn).astype(np.float64) * (2.0 * np.pi / nn)
        s = 1.0 / math.sqrt(nn)
        return np.cos(ph) * s, np.sin(ph) * s
    cc, sc = dft(128)
    c["dftc"] = _bf(np.concatenate([cc, -sc], axis=1))
    c2, s2 = dft(256)
    c["dft256"] = _bf(np.concatenate([c2.reshape(2, 128, 256).transpose(1, 0, 2),
                                      s2.reshape(2, 128, 256).transpose(1, 0, 2)], axis=1).reshape(128, 4 * 256))
    c3, s3 = dft(2048)
    def tiles(Mx):
        return np.ascontiguousarray(
            Mx.reshape(16, 128, 4, 512).transpose(2, 1, 0, 3).reshape(4, 128, 16 * 512))
    c["dft2kc"] = _bf(tiles(c3))
    c["dft2ks"] = _bf(tiles(s3))
    return c


_CONST = None


def build_program(debug=False, stage=99):
    nc = bass.Bass("TRN2", target_bir_lowering=False)
    dt = {}

    def din(name, shape, dtype=F32):
        dt[name] = nc.dram_tensor(name, list(shape), dtype, kind="ExternalInput").ap()
        return dt[name]

    def dout(name, shape, dtype=F32):
        dt[name] = nc.dram_tensor(name, list(shape), dtype, kind="ExternalOutput").ap()
        return dt[name]

    xp_d = din("xp", [1024, D])
    xs_d = din("xs", [2048, D])
    ck_d = din("ck", [4, 256, 128])
    cv_d = din("cv", [4, 256, 128])
    vpack_d = din("vpack", [128, 128])
    bpack_d = din("bpack", [128, 128])
    lam_d = din("lamp", [1, 256])
    wmod_d = din("w_mod", [2, 12, 128, 8 * 512])
    wine_d = din("w_in_even", [24, 128, 8 * 128])
    woute_d = din("w_out_even", [4, 128, 8 * 256])
    wino_d = din("w_in_odd", [4, 128, 8 * 256])
    wouto_d = din("w_out_odd", [4, 128, 8 * 256])
    wpool_d = din("w_pool", [4, 128, 128])
    wfour_d = din("w_fourier", [4, 128, 128])
    wgate_d = din("w_gate", [2, 11, 128, 8 * 256])
    wup_d = din("w_up", [2, 11, 128, 8 * 256])
    wdown_d = din("w_down", [2, 8, 128, 22 * 128])
    ident_d = din("ident", [128, 128])
    prot_d = din("prot", [128, 128], BF16)
    rcos_d = din("rcos", [128, 2048])
    rsin_d = din("rsin", [128, 2048])
    band_d = din("band", [128, 20 * 128], BF16)
    dftc_d = din("dftc", [128, 256], BF16)
    dft256_d = din("dft256", [128, 4 * 256], BF16)
    dft2kc_d = din("dft2kc", [4, 128, 16 * 512], BF16)
    dft2ks_d = din("dft2ks", [4, 128, 16 * 512], BF16)

    yp_d = dout("yp", [1024, D])
    ys_d = dout("ys", [2048, D])
    nk_d = dout("nk", [4, 4, 256, 128])
    nv_d = dout("nv", [4, 4, 256, 128])
    dbg = {}

    es = ExitStack()
    arena_t = es.enter_context(nc.sbuf_tensor("arena", [128, ARENA_BYTES], U8))
    psum_t = es.enter_context(nc.psum_tensor("psum", [128, 8, 512], F32))
    P = Prog(nc, es)
    A = Arena(arena_t, ARENA_BYTES)
    A.prog = P
    PS = [Tile(psum_t[:, i, :], "ps%d" % i) for i in range(8)]
    for p_ in PS:
        p_.b().excl = True

    def phase_sync():
        if debug or FULLBAR:
            P.barrier()
        else:
            import inspect
            fr = inspect.stack()[1]
            P.marks.append((fr.function, fr.lineno, sum(1 for o in P.streams["pe"] if o[0] is not None)))

    def dbg_dump(name, src_ap, bufs, shape):
        if not debug:
            return
        d = dout("dbg_" + name, shape, src_ap.dtype)
        dbg[name] = shape
        P.op("sp", lambda e: e.dma_start(out=d, in_=src_ap), r=bufs, dma=True)

    rr = {"ev": 0, "dq": 0}

    def ev_eng():
        rr["ev"] ^= 1
        return "act" if rr["ev"] else "dve"

    def copy_op(eng, out, in_, r, w, scale=None):
        if eng == "act":
            if scale is None:
                return P.op("act", lambda e: e.activation(out=out, in_=in_, func=AF.Copy), r=r, w=w)
            return P.op("act", lambda e: e.activation(out=out, in_=in_, func=AF.Copy, scale=scale), r=r, w=w)
        if scale is None:
            return P.op(eng, lambda e: e.tensor_copy(out=out, in_=in_), r=r, w=w)
        return P.op(eng, lambda e: e.tensor_scalar(out=out, in0=in_, scalar1=scale, scalar2=None,
                                                   op0=ALU.mult), r=r, w=w)

    def dma(eng, out, in_, r=(), w=(), mld=None):
        if mld is None:
            return P.op(eng, lambda e: e.dma_start(out=out, in_=in_), r=r, w=w, dma=True)
        return P.op(eng, lambda e: e.dma_start(out=out, in_=in_, max_dma_last_dim=mld), r=r, w=w, dma=True)

    ident = A.t("ident", [128], F32)
    ones_bf = A.t("ones_bf", [128], BF16)
    ones_f = A.t("ones_f", [128], F32)
    vT = A.t("vT", [128], F32)
    bT = A.t("bT", [128], F32)
    sc = A.t("sc", [8, 2], F32)
    scb = A.t("scb", [8, 2], BF16)
    modsb = A.t("modsb", [2, 48, 2], F32)
    mv = A.t("mv", [2, 2, 6, 8], F32)
    neglam = A.t("neglam", [1], F32)
    sublns = A.t("sublns", [1], F32)
    prot = A.t("prot", [128], BF16)
    band = A.t("band", [20, 128], BF16)
    dftc = A.t("dftc", [256], BF16)
    dft256 = A.t("dft256", [4, 256], BF16)
    wpool = A.t("wpool", [4, 128], BF16)
    wfour = A.t("wfour", [4, 128], BF16)
    epsb = A.t("epsb", [1], F32)
    small = A.t("small", [64], F32)

    dma("sp", ident.ap, ident_d, w=[ident.b()])
    dma("sp", prot.ap, prot_d, w=[prot.b()])
    dma("sp", band.ap, band_d.rearrange("p (a b) -> p a b", a=20), w=[band.b()])
    dma("sp", dftc.ap, dftc_d, w=[dftc.b()])
    dma("sp", dft256.ap, dft256_d.rearrange("p (a b) -> p a b", a=4), w=[dft256.b()])
    P.op("dve", lambda e: e.memset(ones_bf.ap, 1.0), w=[ones_bf.b()])
    P.op("dve", lambda e: e.memset(ones_f.ap, 1.0), w=[ones_f.b()])
    P.op("dve", lambda e: e.memset(epsb.ap, EPS), w=[epsb.b()])

    m0 = A.mark()
    stg = A.t("stg", [2, 128], F32)
    dma("sp", stg.ap[:, 0, :], vpack_d, w=[stg.b(0)])
    dma("sp", stg.ap[:, 1, :], bpack_d, w=[stg.b(1)])
    P.op("pe", lambda e: e.transpose(out=PS[0].ap[:, 0:128], in_=stg.ap[:, 0, :], identity=ident.ap),
         r=[stg.b(0), ident.b()], w=[PS[0].b()])
    P.op("pe", lambda e: e.transpose(out=PS[1].ap[:, 0:128], in_=stg.ap[:, 1, :], identity=ident.ap),
         r=[stg.b(1), ident.b()], w=[PS[1].b()])
    copy_op("dve", vT.ap, PS[0].ap[:, 0:128], [PS[0].b()], [vT.b()])
    copy_op("dve", bT.ap, PS[1].ap[:, 0:128], [PS[1].b()], [bT.b()])
    P.op("act", lambda e: e.activation(out=sc.ap[:, :, 0], in_=vT.ap[:, 89:97], func=AF.Silu),
         r=[vT.b()], w=[sc.b()])
    P.op("act", lambda e: e.activation(out=sc.ap[:, :, 1], in_=vT.ap[:, 81:89], func=AF.Silu),
         r=[vT.b(), sc.b()], w=[sc.b()])
    P.op("dve", lambda e: e.tensor_scalar(out=sublns.ap, in0=vT.ap[:, 80:81], scalar1=1.0 - LAM_INIT,
                                          scalar2=None, op0=ALU.mult), r=[vT.b()], w=[sublns.b()])
    wst = A.t("wst8", [8, 128], F32)
    dma("sp", wst.ap[:, 0:4, :], wpool_d.rearrange("g c d -> c g d"), w=[wst.b()])
    dma("sp", wst.ap[:, 4:8, :], wfour_d.rearrange("g c d -> c g d"), w=[wst.b()])
    copy_op("dve", wpool.ap, wst.ap[:, 0:4, :], [wst.b()], [wpool.b()])
    copy_op("dve", wfour.ap, wst.ap[:, 4:8, :], [wst.b()], [wfour.b()])
    lp = A.t("lp", [4, 64], F32, parts=1)
    dma("sp", lp.ap, lam_d.rearrange("o (a b) -> o a b", a=4), w=[lp.b()])
    lprod = A.t("lprod", [2, 64], F32, parts=1)
    lsum = A.t("lsum", [4], F32, parts=1)
    P.op("dve", lambda e: e.tensor_tensor(out=lprod.ap, in0=lp.ap[:, 0:4:2, :], in1=lp.ap[:, 1:4:2, :],
                                          op=ALU.mult), r=[lp.b()], w=[lprod.b()])
    P.op("dve", lambda e: e.reduce_sum(out=lsum.ap[:, 0:2], in_=lprod.ap, axis=AX.X),
         r=[lprod.b()], w=[lsum.b()])
    P.op("act", lambda e: e.activation(out=lsum.ap[:, 2:4], in_=lsum.ap[:, 0:2], func=AF.Exp),
         r=[lsum.b()], w=[lsum.b()])
    P.op("dve", lambda e: e.tensor_tensor(out=lsum.ap[:, 0:1], in0=lsum.ap[:, 3:4], in1=lsum.ap[:, 2:3],
                                          op=ALU.subtract), r=[lsum.b()], w=[lsum.b()])
    P.op("dve", lambda e: e.tensor_scalar(out=lsum.ap[:, 1:2], in0=lsum.ap[:, 0:1],
                                          scalar1=-LAM_INIT, scalar2=None, op0=ALU.add),
         r=[lsum.b()], w=[lsum.b()])
    P.op("pe", lambda e: e.matmul(PS[2].ap[:, 0:1], ones_f.ap[0:1, :], lsum.ap[:, 1:2], start=True, stop=True),
         r=[ones_f.b(), lsum.b()], w=[PS[2].b()])
    copy_op("dve", neglam.ap, PS[2].ap[:, 0:1], [PS[2].b()], [neglam.b()])

    P.op("act", lambda e: e.activation(out=scb.ap, in_=sc.ap, func=AF.Copy), r=[sc.b()], w=[scb.b()])
    bg = []

    def bg_step():
        if bg:
            bg.pop(0)()

    def bg_flush():
        while bg:
            bg.pop(0)()

    def mod_steps(l, wms, rowblk):
        def issue(blk):
            ws = wms[blk % 3]
            dma("pool", ws.ap, wmod_d[l, blk].rearrange("p (k n) -> p k n", k=8), w=[ws.b()], mld=4096)

        def step(blk):
            if blk + 2 < 12:
                issue(blk + 2)
            ws = wms[blk % 3]
            psX, psY = PS[2], PS[3]
            for k in range(8):
                P.op("pe", lambda e, k=k: e.matmul(psX.ap[0:2, :], scb.ap[:, k, :], ws.ap[:, k, :],
                                                  start=(k == 0), stop=(k == 7)),
                     r=[ws.b(), scb.b()], w=[psX.b()], sig=(k == 7))
            copy_op("dve", rowblk.ap, psX.ap[0:2, :], [psX.b()], [rowblk.b()])
            for j in range(4):
                P.op("pe", lambda e, j=j: e.transpose(out=psY.ap[:, j * 2:(j + 1) * 2],
                                                      in_=rowblk.ap[:, j * 128:(j + 1) * 128],
                                                      identity=ident.ap[0:2, 0:2]),
                     r=[rowblk.b(), ident.b()], w=[psY.b()], sig=(j == 3))
            for w_ in range(2):
                P.op("dve", lambda e, w_=w_: e.tensor_tensor(
                    out=modsb.ap[:, l, blk * 4:(blk + 1) * 4, w_],
                    in0=psY.ap[:, 0:8].rearrange("p (c w) -> p c w", w=2)[:, :, w_],
                    in1=bT.ap[:, l * 48 + blk * 4: l * 48 + blk * 4 + 4], op=ALU.add),
                    r=[psY.b(), bT.b()], w=[modsb.b()])

        issue(0)
        issue(1)
        return [lambda blk=blk: step(blk) for blk in range(12)] + [lambda: mod_final(l)]

    def mod_final(l):
        for w_ in range(2):
            g = lambda j, l=l: vT.ap[:, l * 32 + j * 8: l * 32 + j * 8 + 8]
            md = lambda i, l=l, w_=w_: modsb.ap[:, l, i * 8:(i + 1) * 8, w_]
            P.op("dve", lambda e, l=l, w_=w_, g=g, md=md: e.scalar_tensor_tensor(
                out=mv.ap[:, l, w_, 0, :], in0=md(1), scalar=1.0, in1=g(0), op0=ALU.add, op1=ALU.mult),
                r=[modsb.b(), vT.b()], w=[mv.b()])
            P.op("dve", lambda e, l=l, w_=w_, md=md: e.tensor_copy(out=mv.ap[:, l, w_, 1, :], in_=md(0)),
                 r=[modsb.b()], w=[mv.b()])
            P.op("dve", lambda e, l=l, w_=w_, g=g, md=md: e.tensor_tensor(
                out=mv.ap[:, l, w_, 2, :], in0=md(2), in1=g(1), op=ALU.mult),
                r=[modsb.b(), vT.b()], w=[mv.b()])
            P.op("dve", lambda e, l=l, w_=w_, g=g, md=md: e.scalar_tensor_tensor(
                out=mv.ap[:, l, w_, 3, :], in0=md(4), scalar=1.0, in1=g(2), op0=ALU.add, op1=ALU.mult),
                r=[modsb.b(), vT.b()], w=[mv.b()])
            P.op("dve", lambda e, l=l, w_=w_, md=md: e.tensor_copy(out=mv.ap[:, l, w_, 4, :], in_=md(3)),
                 r=[modsb.b()], w=[mv.b()])
            P.op("dve", lambda e, l=l, w_=w_, g=g, md=md: e.tensor_tensor(
                out=mv.ap[:, l, w_, 5, :], in0=md(5), in1=g(3), op=ALU.mult),
                r=[modsb.b(), vT.b()], w=[mv.b()])
    wms0 = [A.t("wmsA%d" % i, [8, 512], BF16) for i in range(3)]
    rowblk0 = A.t("rowblkA", [512], F32, parts=2)
    for st_ in mod_steps(0, wms0, rowblk0):
        st_()
    dbg_dump("mv", mv.ap, [mv.b()], [128, 2, 2, 6, 8])
    dbg_dump("neglam", neglam.ap, [neglam.b()], [128, 1])
    dbg_dump("sc", sc.ap, [sc.b()], [128, 8, 2])
    dbg_dump("vT", vT.ap, [vT.b()], [128, 128])
    dbg_dump("bT", bT.ap, [bT.b()], [128, 128])
    dbg_dump("modsb", modsb.ap, [modsb.b()], [128, 2, 48, 2])
    phase_sync()
    A.release(m0)

    def load_x(xd, x, T):
        m = A.mark()
        xtm = [A.t("xtm%d" % i, [1024], F32) for i in range(6)]
        for ck in range(T // 128):
            xt = xtm[ck % 6]
            dma("sp", xt.ap, xd[ck * 128:(ck + 1) * 128, :], w=[xt.b()])
            for half in range(2):
                ps = PS[(ck * 2 + half) % 8]
                for j in range(4):
                    c = half * 4 + j
                    P.op("pe", lambda e, ps=ps, xt=xt, c=c, j=j: e.transpose(
                        out=ps.ap[:, j * 128:(j + 1) * 128], in_=xt.ap[:, c * 128:(c + 1) * 128],
                        identity=ident.ap), r=[xt.b(), ident.b()], w=[ps.b()], sig=(j == 3))
                copy_op(ev_eng(), x.ap[:, half * 4:half * 4 + 4, ck * 128:(ck + 1) * 128],
                        ps.ap.rearrange("p (a b) -> p a b", a=4), [ps.b()], [x.b(ck // 4)])
        phase_sync()
        A.release(m)

    def store_x(yd, x, T):
        m = A.mark()
        otm = [A.t("otm%d" % i, [1024], F32) for i in range(4)]
        for ck in range(T // 128):
            ot = otm[ck % 4]
            for half in range(2):
                ps = PS[(ck * 2 + half) % 8]
                for j in range(4):
                    c = half * 4 + j
                    P.op("pe", lambda e, ps=ps, c=c, j=j, ck=ck: e.transpose(
                        out=ps.ap[:, j * 128:(j + 1) * 128], in_=x.ap[:, c, ck * 128:(ck + 1) * 128],
                        identity=ident.ap), r=[x.b(ck // 4), ident.b()], w=[ps.b()], sig=(j == 3))
                copy_op(ev_eng(), ot.ap[:, half * 512:(half + 1) * 512], ps.ap, [ps.b()], [ot.b()])
            dma("sp", yd[ck * 128:(ck + 1) * 128, :], ot.ap, r=[ot.b()])
        phase_sync()
        A.release(m)

    def rstd_from_ps(ps, rs, n, width):
        P.op("act", lambda e: e.activation(out=rs.ap[:, 0:width], in_=ps.ap[:, 0:width], func=AF.Ln,
                                           bias=epsb.ap, scale=1.0 / n), r=[ps.b(), epsb.b()], w=[rs.b()])
        P.op("act", lambda e: e.activation(out=rs.ap[:, 0:width], in_=rs.ap[:, 0:width], func=AF.Exp,
                                           scale=-0.5), r=[rs.b()], w=[rs.b()])

    def norm_mod(x, h, tiles, l, w_, kind_gs, kind_sh, tmp):
        sq, rs, t1 = tmp
        for ti, tl in enumerate(tiles):
            cols = slice(tl * 512, (tl + 1) * 512)
            ps = PS[ti % 2]
            for c in range(8):
                P.op("act", lambda e, c=c, cols=cols: e.activation(out=sq.ap[:, c, :], in_=x.ap[:, c, cols],
                                                                  func=AF.Square),
                     r=[x.b(tl)], w=[sq.b(c)])
                P.op("pe", lambda e, c=c, ps=ps: e.matmul(ps.ap, ones_bf.ap, sq.ap[:, c, :],
                                                         start=(c == 0), stop=(c == 7)),
                     r=[sq.b(c), ones_bf.b()], w=[ps.b()], sig=True)
            rstd_from_ps(ps, rs, D, 512)
            hc = slice(ti * 512, (ti + 1) * 512)
            for c in range(8):
                t = t1[c % 2]
                P.op("dve", lambda e, c=c, cols=cols, t=t: e.scalar_tensor_tensor(
                    out=t.ap, in0=x.ap[:, c, cols], scalar=mv.ap[:, l, w_, kind_gs, c:c + 1], in1=rs.ap,
                    op0=ALU.mult, op1=ALU.mult), r=[x.b(tl), mv.b(), rs.b()], w=[t.b()])
                P.op("act", lambda e, c=c, hc=hc, t=t: e.activation(
                    out=h.ap[:, c, hc], in_=t.ap, func=AF.Identity, bias=mv.ap[:, l, w_, kind_sh, c:c + 1],
                    scale=1.0), r=[t.b(), mv.b()], w=[h.b(ti)])

    def gated_residual(x, tl, ysb, ssq_ps, l, w_, kind_gg, tmp):
        sq, rs, t1 = tmp
        cols = slice(tl * 512, (tl + 1) * 512)
        rstd_from_ps(ssq_ps, rs, D, 512)
        for c in range(8):
            t = t1[c % 2]
            P.op("dve", lambda e, c=c, t=t: e.scalar_tensor_tensor(
                out=t.ap, in0=ysb.ap[:, c, :], scalar=mv.ap[:, l, w_, kind_gg, c:c + 1], in1=rs.ap,
                op0=ALU.mult, op1=ALU.mult), r=[ysb.b(c), mv.b(), rs.b()], w=[t.b()])
            P.op("dve", lambda e, c=c, t=t, cols=cols: e.tensor_tensor(
                out=x.ap[:, c, cols], in0=x.ap[:, c, cols], in1=t.ap, op=ALU.add),
                r=[t.b(), x.b(tl)], w=[x.b(tl)])

    def norm_mod_pieces(x, h, tl, ti, l, w_, kind_gs, kind_sh, tmp, ps):
        sq, rs, t1 = tmp
        cols = slice(tl * 512, (tl + 1) * 512)
        hc = slice(ti * 512, (ti + 1) * 512)

        def stats():
            for c in range(8):
                P.op("act", lambda e, c=c: e.activation(out=sq.ap[:, c % 4, :], in_=x.ap[:, c, cols],
                                                        func=AF.Square), r=[x.b(tl)], w=[sq.b(c % 4)])
                P.op("pe", lambda e, c=c: e.matmul(ps.ap, ones_bf.ap, sq.ap[:, c % 4, :],
                                                   start=(c == 0), stop=(c == 7)),
                     r=[sq.b(c % 4), ones_bf.b()], w=[ps.b()], sig=True)

        def half(c0):
            if c0 == 0:
                rstd_from_ps(ps, rs, D, 512)
            for c in range(c0, c0 + 4):
                t = t1[c % 2]
                P.op("dve", lambda e, c=c, t=t: e.scalar_tensor_tensor(
                    out=t.ap, in0=x.ap[:, c, cols], scalar=mv.ap[:, l, w_, kind_gs, c:c + 1], in1=rs.ap,
                    op0=ALU.mult, op1=ALU.mult), r=[x.b(tl), mv.b(), rs.b()], w=[t.b()])
                P.op("act", lambda e, c=c, t=t: e.activation(
                    out=h.ap[:, c, hc], in_=t.ap, func=AF.Identity, bias=mv.ap[:, l, w_, kind_sh, c:c + 1],
                    scale=1.0), r=[t.b(), mv.b()], w=[h.b(ti)])
        return [stats, lambda: half(0), lambda: half(4)]

    def gated_residual_pieces(x, tl, ysb, ssq_ps, l, w_, kind_gg, tmp):
        sq, rs, t1 = tmp
        cols = slice(tl * 512, (tl + 1) * 512)

        def piece(c):
            if c == 0:
                rstd_from_ps(ssq_ps, rs, D, 512)
            t = t1[c % 2]
            P.op("dve", lambda e: e.scalar_tensor_tensor(
                out=t.ap, in0=ysb.ap[:, c, :], scalar=mv.ap[:, l, w_, kind_gg, c:c + 1], in1=rs.ap,
                op0=ALU.mult, op1=ALU.mult), r=[ysb.b(c), mv.b(), rs.b()], w=[t.b()])
            P.op("dve", lambda e: e.tensor_tensor(
                out=x.ap[:, c, cols], in0=x.ap[:, c, cols], in1=t.ap, op=ALU.add),
                r=[t.b(), x.b(tl)], w=[x.b(tl)])
        return [lambda c=c: piece(c) for c in range(8)]

    class WStream:
        def __init__(self, kc, maxcols, nstage=2, name="ws"):
            self.kc = kc
            self.maxcols = maxcols
            self.st = [] if CASTMODE == "dma" else [A.t("%s_st%d" % (name, i), [kc, maxcols], F32) for i in range(nstage)]
            self.i = 0

        def load(self, dst_tile, dst_ap, slab_ap, kc, ncols, dst_buf):
            dma("pool", dst_ap, slab_ap.rearrange("p (k n) -> p k n", k=kc), w=[dst_buf], mld=4096)

    def ffn(x, tiles, l, w_, scr, h2_pre=None, next_tiles=None):
        m = A.mark()
        nt = len(tiles)
        h2 = h2_pre if h2_pre is not None else A.carve(scr, 0, 8 * 512 * nt, 8, "h2")
        a = A.carve(scr, 8 * 512 * nt, 8 * 512 * nt + FC * 512 * nt, FC, "a")
        sq = A.t("f_sq", [8, 512], BF16)
        rs = A.t("f_rs", [512], F32)
        t1 = [A.t("f_t%d" % i, [512], F32) for i in range(2)]
        tmp = (sq, rs, t1)
        if h2_pre is None:
            norm_mod(x, h2, tiles, l, w_, 3, 4, tmp)
        ws = WStream(8, 256, 2, "wgu")
        NR = 3
        wg = [A.t("wg%d" % i, [8, 256], BF16) for i in range(NR)]
        wu = [A.t("wu%d" % i, [8, 256], BF16) for i in range(NR)]
        sg = [A.t("sg%d" % i, [512], F32) for i in range(2)]
        k = 0
        for jp in range(FC // 2):
            bg_step()
            g_ = wg[jp % NR]
            u_ = wu[jp % NR]
            ws.load(g_, g_.ap, wgate_d[l, jp], 8, 256, g_.b())
            ws.load(u_, u_.ap, wup_d[l, jp], 8, 256, u_.b())
            for jj in range(2):
                j = jp * 2 + jj
                for ti in range(nt):
                    hc = slice(ti * 512, (ti + 1) * 512)
                    pg = PS[(k * 2) % 8]
                    pu = PS[(k * 2 + 1) % 8]
                    k += 1
                    for c in range(8):
                        P.op("pe", lambda e, c=c, pg=pg, g_=g_, jj=jj, hc=hc: e.matmul(
                            pg.ap, g_.ap[:, c, jj * 128:(jj + 1) * 128], h2.ap[:, c, hc],
                            start=(c == 0), stop=(c == 7)), r=[g_.b(), h2.b(ti)], w=[pg.b()], sig=(c == 7))
                    for c in range(8):
                        P.op("pe", lambda e, c=c, pu=pu, u_=u_, jj=jj, hc=hc: e.matmul(
                            pu.ap, u_.ap[:, c, jj * 128:(jj + 1) * 128], h2.ap[:, c, hc],
                            start=(c == 0), stop=(c == 7)), r=[u_.b(), h2.b(ti)], w=[pu.b()], sig=(c == 7))
                    s_ = sg[k % 2]
                    P.op("act", lambda e, s_=s_, pg=pg: e.activation(out=s_.ap, in_=pg.ap, func=AF.Silu),
                         r=[pg.b()], w=[s_.b()])
                    P.op("dve", lambda e, s_=s_, pu=pu, j=j, hc=hc: e.tensor_tensor(
                        out=a.ap[:, j, hc], in0=s_.ap, in1=pu.ap, op=ALU.mult),
                        r=[s_.b(), pu.b()], w=[a.b((j, ti))])
        phase_sync()
        A.release(m)
        rs = A.t("f_rs", [512], F32)
        t1 = [A.t("f_t%d" % i, [512], F32) for i in range(2)]
        tmp = (None, rs, t1)
        wsd = WStream(FC // 2, 128, 2, "wd")
        NWD = 2 if next_tiles else 3
        wd = [A.t("wdn%d" % i, [FC, 128], BF16) for i in range(NWD)]
        ysb = [A.t("ysb%d" % i, [8, 512], F32) for i in range(nt)]
        sqd = [A.t("sqd%d" % i, [512], BF16) for i in range(3)]
        ssq = [PS[6 + i] for i in range(nt)]
        k = 0
        pend = []
        todo = []
        h2n = None
        if next_tiles:
            nsq = A.t("f_nsq", [4, 512], BF16)
            nrs = A.t("f_nrs", [512], F32)
            nt1 = [A.t("f_nt%d" % i, [512], F32) for i in range(2)]
            h2n = A.carve(scr, 0, 8 * 512 * len(next_tiles), 8, "h2n")
            for ti2, tl2 in enumerate(next_tiles):
                todo.extend(norm_mod_pieces(x, h2n, tl2, ti2, l, w_, 3, 4, (nsq, nrs, nt1), PS[5]))
        for d in range(8):
            w_d = wd[d % NWD]
            for hf in range(2):
                wsd.load(w_d, w_d.ap[:, hf * 11:(hf + 1) * 11, :], wdown_d[l, d][:, hf * 1408:(hf + 1) * 1408],
                         11, 128, w_d.b(hf))
            for ti in range(nt):
                hc = slice(ti * 512, (ti + 1) * 512)
                ps = PS[k % 5]
                k += 1
                for j in range(FC):
                    P.op("pe", lambda e, j=j, ps=ps, w_d=w_d, hc=hc: e.matmul(
                        ps.ap, w_d.ap[:, j, :], a.ap[:, j, hc], start=(j == 0), stop=(j == FC - 1)),
                        r=[w_d.b(j // 11), a.b((j, ti))], w=[ps.b()], sig=(j == FC - 1))
                if pend:
                    pend.pop()()
                s_ = sqd[k % 3]
                P.op("act", lambda e, ps=ps, s_=s_: e.activation(out=s_.ap, in_=ps.ap, func=AF.Square),
                     r=[ps.b()], w=[s_.b()])
                copy_op("dve", ysb[ti].ap[:, d, :], ps.ap, [ps.b()], [ysb[ti].b(d)])
                if todo and k >= 2:
                    todo.pop(0)()

                def fin(ti=ti, s_=s_, d=d):
                    P.op("pe", lambda e: e.matmul(ssq[ti].ap, ones_bf.ap, s_.ap, start=(d == 0), stop=(d == 7)),
                         r=[s_.b(), ones_bf.b()], w=[ssq[ti].b()], sig=True)
                pend.append(fin)
        if pend:
            pend.pop()()
        while todo:
            todo.pop(0)()
        for ti, tl in enumerate(tiles):
            gated_residual(x, tl, ysb[ti], ssq[ti], l, w_, 5, tmp)
        phase_sync()
        A.release(m)
        return h2n

    def even_mixer(x, scr, NT, nseq, L, w_, is_sample, nk_out=None, nv_out=None):
        T = NT * 512
        nchunk = T // 128
        nctx = 2 if is_sample else 0
        m = A.mark()
        h = A.carve(scr, 0, 8 * T, 8, "h")
        cat = A.carve(scr, 8 * T, 16 * T, 8, "cat")
        sq = A.t("m_sq", [8, 512], BF16)
        rs = A.t("m_rs", [512], F32)
        t1 = [A.t("m_t%d" % i, [512], F32) for i in range(2)]
        tmp = (sq, rs, t1)
        norm_mod(x, h, list(range(NT)), 0, w_, 0, 1, tmp)
        phase_sync()
        A.release(m)
        if debug and not is_sample:
            dbg_dump("h0p", h.ap, [h.b(i) for i in range(NT)], [128, 8, T])
        if stage <= 1:
            phase_sync(); A.release(m); return
        ws = WStream(8, 128, 2, "wie")
        wsl = [A.t("wsl%d" % i, [8, 128], BF16) for i in range(5)]
        wcnt = [0]

        def slab(col0):
            s_ = wsl[wcnt[0] % 5]
            wcnt[0] += 1
            ws.load(s_, s_.ap, wine_d[col0 // 128], 8, 128, s_.b())
            return s_

        mc = A.mark()
        Lp = L + 2
        zl = [A.t("z%d" % i, [nseq, Lp], F32) for i in range(2)]
        accl = [A.t("acc%d" % i, [nseq, L], F32) for i in range(2)]
        gbl = [A.t("gb%d" % i, [T], BF16) for i in range(2)]
        gcs = [A.t("gcs%d" % i, [512], F32) for i in range(2)]
        spt = 512 // L if L < 512 else 1
        for z in zl:
            P.op("dve", lambda e, z=z: e.memset(z.ap, 0.0), w=[z.b()])
        kk = 0
        for c in range(4):
            bg_step()
            z, acc, gb = zl[c % 2], accl[c % 2], gbl[c % 2]
            s_gb = slab(1536 + c * 128)
            s_gc = slab(2048 + c * 128)
            s_xi = slab(2560 + c * 128)
            for tl in range(NT):
                cols = slice(tl * 512, (tl + 1) * 512)
                p_gb, p_gc, p_xi = PS[kk % 8], PS[(kk + 1) % 8], PS[(kk + 2) % 8]
                kk += 3
                for (pp, ss) in ((p_gb, s_gb), (p_gc, s_gc), (p_xi, s_xi)):
                    for k in range(8):
                        P.op("pe", lambda e, pp=pp, ss=ss, k=k, cols=cols: e.matmul(
                            pp.ap, ss.ap[:, k, :], h.ap[:, k, cols], start=(k == 0), stop=(k == 7)),
                            r=[ss.b(), h.b(tl)], w=[pp.b()], sig=(k == 7))
                copy_op("act", gb.ap[:, cols], p_gb.ap, [p_gb.b()], [gb.b()])
                g_ = gcs[tl % 2]
                copy_op("act", g_.ap, p_gc.ap, [p_gc.b()], [g_.b()])
                if L >= 512:
                    zo = z.ap[:, 0, 1 + tl * 512: 1 + (tl + 1) * 512]
                    xi = p_xi.ap
                    gi = g_.ap
                else:
                    zo = z.ap[:, tl * spt:(tl + 1) * spt, 1:1 + L]
                    xi = p_xi.ap.rearrange("p (s t) -> p s t", s=spt)
                    gi = g_.ap.rearrange("p (s t) -> p s t", s=spt)
                P.op("dve", lambda e, zo=zo, xi=xi, gi=gi: e.tensor_tensor(out=zo, in0=xi, in1=gi, op=ALU.mult),
                     r=[p_xi.b(), g_.b()], w=[z.b()])
            cw = lambda tap, c=c: vT.ap[:, 64 + tap * 4 + c: 64 + tap * 4 + c + 1]
            P.op("act", lambda e, cw=cw, z=z, acc=acc: e.activation(out=acc.ap, in_=z.ap[:, :, 1:1 + L], func=AF.Copy,
                                                      scale=cw(1)), r=[z.b(), vT.b()], w=[acc.b()])
            P.op("dve", lambda e, cw=cw, z=z, acc=acc: e.scalar_tensor_tensor(
                out=acc.ap, in0=z.ap[:, :, 0:L], scalar=cw(0), in1=acc.ap, op0=ALU.mult, op1=ALU.add),
                r=[z.b(), vT.b(), acc.b()], w=[acc.b()])
            P.op("dve", lambda e, cw=cw, z=z, acc=acc: e.scalar_tensor_tensor(
                out=acc.ap, in0=z.ap[:, :, 2:2 + L], scalar=cw(2), in1=acc.ap, op0=ALU.mult, op1=ALU.add),
                r=[z.b(), vT.b(), acc.b()], w=[acc.b()])
            P.op("dve", lambda e, c=c, acc=acc, gb=gb: e.tensor_tensor(
                out=cat.ap[:, 4 + c, :], in0=acc.ap.rearrange("p s t -> p (s t)"), in1=gb.ap, op=ALU.mult),
                r=[acc.b(), gb.b()], w=[cat.b(("cv", c))])
        phase_sync()
        A.release(mc)
        if debug and not is_sample:
            dbg_dump("catconv_p", cat.ap[:, 4:8, :], [cat.b(("cv", c)) for c in range(4)], [128, 4, T])
        if stage <= 2:
            phase_sync(); A.release(m); return

        nkeys = nctx * 128 + (T if is_sample else L)
        qh = A.t("qh", [T], BF16)
        kTh = A.t("kTh", [nctx * 128 + T], BF16)
        vh = A.t("vh", [nctx + nchunk, 128], BF16)
        if is_sample:
            rcos = A.t("rcos", [2048], F32)
            rsin = A.t("rsin", [2048], F32)
            dma("sp", rcos.ap, rcos_d, w=[rcos.b()])
            dma("sp", rsin.ap, rsin_d, w=[rsin.b()])
            qb = [A.t("qb%d" % i, [512], BF16) for i in range(1)] * 2
            rt1 = [A.t("rt1%d" % i, [512], F32) for i in range(1)] * 2
            rt2 = [A.t("rt2%d" % i, [512], F32) for i in range(1)] * 2
            ckst_l = [A.t("ckst%d" % i, [2, 128], F32) for i in range(2)]
            cvst_l = [A.t("cvst%d" % i, [2, 128], F32) for i in range(2)]
        else:
            kvo = [A.t("kvo%d" % i, [512], F32) for i in range(4)]
            kvc = [0]
        E = [[A.t("E%d%d" % (i, j), [512], BF16) for j in range(2)] for i in range(2)]
        r01 = [A.t("r01%d" % i, [512], F32) for i in range(2)]
        osb = A.t("osb", [512], F32)
        o2 = A.t("o2", [512], F32)
        osq = A.t("osq", [512], BF16)
        scale = 64 ** -0.5
        kq = [0]
        deferred = []

        def proj_fm(s_, dst, dst_off, rope, nm):
            for tl in range(NT):
                cols = slice(tl * 512, (tl + 1) * 512)
                dcols = slice(dst_off + tl * 512, dst_off + (tl + 1) * 512)
                ps = PS[kq[0] % 4]
                kq[0] += 1
                for k in range(8):
                    P.op("pe", lambda e, ps=ps, s_=s_, k=k, cols=cols: e.matmul(
                        ps.ap, s_.ap[:, k, :], h.ap[:, k, cols], start=(k == 0), stop=(k == 7)),
                        r=[s_.b(), h.b(tl)], w=[ps.b()], sig=(k == 7))
                if not rope:
                    copy_op(ev_eng(), dst.ap[:, dcols], ps.ap, [ps.b()], [dst.b(tl)])
                else:
                    q_ = qb[tl % 2]
                    a_ = rt1[tl % 2]
                    b_ = rt2[tl % 2]
                    ps2 = PS[4 + (kq[0] % 2)]
                    copy_op("act", q_.ap, ps.ap, [ps.b()], [q_.b()])
                    P.op("dve", lambda e, a_=a_, ps=ps, cols=cols: e.tensor_tensor(
                        out=a_.ap, in0=ps.ap, in1=rcos.ap[:, cols], op=ALU.mult),
                        r=[ps.b(), rcos.b()], w=[a_.b()])
                    P.op("pe", lambda e, ps2=ps2, q_=q_: e.matmul(ps2.ap, prot.ap, q_.ap, start=True, stop=True),
                         r=[prot.b(), q_.b()], w=[ps2.b()])
                    P.op("dve", lambda e, b_=b_, ps2=ps2, cols=cols: e.tensor_tensor(
                        out=b_.ap, in0=ps2.ap, in1=rsin.ap[:, cols], op=ALU.mult),
                        r=[ps2.b(), rsin.b()], w=[b_.b()])
                    P.op("dve", lambda e, a_=a_, b_=b_, dcols=dcols: e.tensor_tensor(
                        out=dst.ap[:, dcols], in0=a_.ap, in1=b_.ap, op=ALU.add),
                        r=[a_.b(), b_.b()], w=[dst.b(tl)])

        def proj_tm(s_, hd, to_v, out_d):
            for tl in range(NT):
                ps = PS[kq[0] % 4]
                kq[0] += 1
                for cq in range(4):
                    ck = tl * 4 + cq
                    for k in range(8):
                        P.op("pe", lambda e, ps=ps, s_=s_, k=k, ck=ck, cq=cq: e.matmul(
                            ps.ap[:, cq * 128:(cq + 1) * 128], h.ap[:, k, ck * 128:(ck + 1) * 128], s_.ap[:, k, :],
                            start=(k == 0), stop=(k == 7)),
                            r=[s_.b(), h.b(tl)], w=[ps.b()], sig=(k == 7 and cq == 3))
                if to_v:
                    copy_op("act", vh.ap[:, nctx + tl * 4: nctx + tl * 4 + 4, :],
                            ps.ap.rearrange("p (a b) -> p a b", a=4), [ps.b()], [vh.b(tl)])
                if out_d is not None:
                    ko = kvo[kvc[0] % 4]
                    kvc[0] += 1
                    copy_op("dve", ko.ap, ps.ap, [ps.b()], [ko.b()])
                    for sq_ in range(2):
                        for c2 in range(2):
                            dma("sp", out_d[2 * tl + sq_, hd, c2 * 128:(c2 + 1) * 128, :],
                                ko.ap[:, (sq_ * 2 + c2) * 128:(sq_ * 2 + c2 + 1) * 128], r=[ko.b()])

        for hd in range(4):
            bg_step()
            s_q = slab(hd * 128)
            proj_fm(s_q, qh, 0, is_sample, "q")
            s_k = slab(512 + hd * 128)
            proj_fm(s_k, kTh, nctx * 128, is_sample, "k")
            if not is_sample and 'tmk' not in SKIP:
                proj_tm(s_k, hd, False, nk_out if 'kvout' not in SKIP else None)
            s_v = slab(1024 + hd * 128)
            if 'tmv' not in SKIP:
                proj_tm(s_v, hd, True, (nv_out if 'kvout' not in SKIP else None) if not is_sample else None)
            if is_sample:
                ckst, cvst = ckst_l[hd % 2], cvst_l[hd % 2]
                dma("sp", ckst.ap, ck_d[hd].rearrange("(c t) d -> t c d", c=2), w=[ckst.b()])
                dma("sp", cvst.ap, cv_d[hd].rearrange("(c t) d -> t c d", c=2), w=[cvst.b()])
                ps = PS[4]
                for cq in range(2):
                    P.op("pe", lambda e, ps=ps, cq=cq, ckst=ckst: e.transpose(
                        out=ps.ap[:, cq * 128:(cq + 1) * 128], in_=ckst.ap[:, cq, :], identity=ident.ap),
                        r=[ckst.b(), ident.b()], w=[ps.b()], sig=(cq == 1))
                copy_op("dve", kTh.ap[:, 0:256], ps.ap[:, 0:256], [ps.b()], [kTh.b("ctx")])
                copy_op("act", vh.ap[:, 0:2, :], cvst.ap, [cvst.b()], [vh.b("ctx")])
            if debug and hd == 0 and 'qdump' not in SKIP:
                tag = "s" if is_sample else "p"
                dbg_dump("qh_" + tag, qh.ap, [qh.b(i) for i in range(NT)], [128, T])
                dbg_dump("kTh_" + tag, kTh.ap, [kTh.b(i) for i in range(NT)] + ([kTh.b("ctx")] if is_sample else []),
                         [128, nctx * 128 + T])
                dbg_dump("vh_" + tag, vh.ap, [vh.b(i) for i in range(NT)] + ([vh.b("ctx")] if is_sample else []),
                         [128, nctx + nchunk, 128])
            if is_sample:
                jobs = [(tl * 512, 512, list(range(nctx + nchunk)), tl) for tl in range(NT)]
            else:
                jobs = [(s * L, L, [s * (L // 128) + i for i in range(L // 128)], (s * L) // 512)
                        for s in range(nseq)]
            if stage < 3:
                jobs = []
            for (q0, nq, kcs, tl) in jobs:
                qc = slice(q0, q0 + nq)
                pO = [PS[4], PS[5]]
                pZ = [PS[6], PS[7]]
                kbufs = [kTh.b(i) for i in range(NT)] + ([kTh.b("ctx")] if is_sample else [])
                vbufs = [vh.b(i) for i in range(NT)] + ([vh.b("ctx")] if is_sample else [])
                nk_ = len(kcs)

                def emit_S(ki):
                    kc = kcs[ki]
                    kcol = slice(kc * 128, (kc + 1) * 128)
                    for comp in range(2):
                        pS = PS[(ki % 2) * 2 + comp]
                        pr = slice(comp * 64, (comp + 1) * 64)
                        P.op("pe", lambda e, pS=pS, pr=pr, kcol=kcol, qc=qc, nq=nq: e.matmul(
                            pS.ap[:, 0:nq], kTh.ap[pr, kcol], qh.ap[pr, qc], start=True, stop=True),
                            r=kbufs + [qh.b(tl)], w=[pS.b()])
                        E_ = E[ki % 2][comp]
                        P.op("act", lambda e, pS=pS, E_=E_, nq=nq: e.activation(
                            out=E_.ap[:, 0:nq], in_=pS.ap[:, 0:nq], func=AF.Exp, scale=scale),
                            r=[pS.b()], w=[E_.b()])

                def emit_PV(ki):
                    kc = kcs[ki]
                    first = (ki == 0)
                    last = (ki == nk_ - 1)
                    for comp in range(2):
                        E_ = E[ki % 2][comp]
                        P.op("pe", lambda e, comp=comp, E_=E_, kc=kc, nq=nq, first=first, last=last: e.matmul(
                            pO[comp].ap[:, 0:nq], vh.ap[:, kc, :], E_.ap[:, 0:nq], start=first, stop=last),
                            r=vbufs + [E_.b()], w=[pO[comp].b()], sig=False)
                        P.op("pe", lambda e, comp=comp, E_=E_, nq=nq, first=first, last=last: e.matmul(
                            pZ[comp].ap[:, 0:nq], ones_bf.ap, E_.ap[:, 0:nq], start=first, stop=last),
                            r=[ones_bf.b(), E_.b()], w=[pZ[comp].b()], sig=True)

                emit_S(0)
                if nk_ > 1:
                    emit_S(1)
                for ki in range(nk_):
                    emit_PV(ki)
                    if ki + 2 < nk_:
                        emit_S(ki + 2)
                if deferred:
                    deferred.pop()()
                P.op("act", lambda e, nq=nq: e.activation(out=osb.ap[:, 0:nq], in_=pO[0].ap[:, 0:nq], func=AF.Copy),
                     r=[pO[0].b()], w=[osb.b()])
                P.op("act", lambda e, nq=nq: e.activation(out=o2.ap[:, 0:nq], in_=pO[1].ap[:, 0:nq], func=AF.Copy),
                     r=[pO[1].b()], w=[o2.b()])
                for comp in range(2):
                    P.op("dve", lambda e, comp=comp, nq=nq: e.tensor_copy(out=r01[comp].ap[:, 0:nq],
                                                                          in_=pZ[comp].ap[:, 0:nq]),
                         r=[pZ[comp].b()], w=[r01[comp].b()])
                for comp in range(2):
                    P.op("dve", lambda e, comp=comp, nq=nq: e.reciprocal(out=r01[comp].ap[:, 0:nq],
                                                                        in_=r01[comp].ap[:, 0:nq]),
                         r=[r01[comp].b()], w=[r01[comp].b()])
                P.op("dve", lambda e, nq=nq: e.tensor_tensor(out=osb.ap[:, 0:nq], in0=osb.ap[:, 0:nq],
                                                             in1=r01[0].ap[:, 0:nq], op=ALU.mult),
                     r=[osb.b(), r01[0].b()], w=[osb.b()])
                P.op("dve", lambda e, nq=nq: e.tensor_tensor(out=o2.ap[:, 0:nq], in0=o2.ap[:, 0:nq],
                                                             in1=r01[1].ap[:, 0:nq], op=ALU.mult),
                     r=[o2.b(), r01[1].b()], w=[o2.b()])
                P.op("dve", lambda e, nq=nq: e.scalar_tensor_tensor(
                    out=osb.ap[:, 0:nq], in0=o2.ap[:, 0:nq], scalar=neglam.ap[:, 0:1], in1=osb.ap[:, 0:nq],
                    op0=ALU.mult, op1=ALU.add), r=[o2.b(), neglam.b(), osb.b()], w=[osb.b()])
                def tail(nq=nq, hd=hd, qc=qc, q0=q0):
                    pN = PS[0]
                    P.op("act", lambda e: e.activation(out=osq.ap[:, 0:nq], in_=osb.ap[:, 0:nq], func=AF.Square),
                         r=[osb.b()], w=[osq.b()])
                    P.op("pe", lambda e: e.matmul(pN.ap[:, 0:nq], ones_bf.ap, osq.ap[:, 0:nq],
                                                  start=True, stop=True),
                         r=[ones_bf.b(), osq.b()], w=[pN.b()])
                    rs_o = r01[0]
                    P.op("act", lambda e: e.activation(
                        out=rs_o.ap[:, 0:nq], in_=pN.ap[:, 0:nq], func=AF.Ln, bias=epsb.ap, scale=1.0 / 128),
                        r=[pN.b(), epsb.b()], w=[rs_o.b()])
                    P.op("act", lambda e: e.activation(out=rs_o.ap[:, 0:nq], in_=rs_o.ap[:, 0:nq], func=AF.Exp,
                                                       scale=-0.5), r=[rs_o.b()], w=[rs_o.b()])
                    P.op("dve", lambda e: e.scalar_tensor_tensor(
                        out=cat.ap[:, hd, qc], in0=osb.ap[:, 0:nq], scalar=sublns.ap[:, 0:1], in1=rs_o.ap[:, 0:nq],
                        op0=ALU.mult, op1=ALU.mult), r=[osb.b(), sublns.b(), rs_o.b()], w=[cat.b(("o", hd, q0))])
                deferred.append(tail)
        if deferred:
            deferred.pop()()
        phase_sync()
        A.release(m)
        if debug:
            tag = "s" if is_sample else "p"
            dbg_dump("cat0_" + tag, cat.ap, [], [128, 8, T])
            phase_sync()
        if stage <= 3:
            return
        out_proj_residual(x, cat, NT, woute_d, 0, w_)

    def out_proj_residual(x, cat, NT, wd, l, w_):
        m = A.mark()
        wo = A.t("wo", [8, 1024], BF16)
        ws = WStream(8, 256, 2, "wos")
        for i in range(4):
            ws.load(wo, wo.ap[:, :, i * 256:(i + 1) * 256], wd[i], 8, 256, wo.b(i))
        rsl = [A.t("o_rs%d" % i, [512], F32) for i in range(2)]
        t1 = [A.t("o_t%d" % i, [512], F32) for i in range(2)]
        ysbl = [A.t("o_ysb%d" % i, [8, 512], F32) for i in range(2)]
        sqd = [A.t("o_sqd%d" % i, [512], BF16) for i in range(3)]
        k = 0
        pend = []
        todo = []
        for tl in range(NT):
            cols = slice(tl * 512, (tl + 1) * 512)
            ssq = PS[6 + tl % 2]
            ysb = ysbl[tl % 2]
            for d in range(8):
                ps = PS[k % 6]
                k += 1
                for c in range(8):
                    P.op("pe", lambda e, ps=ps, c=c, d=d, cols=cols: e.matmul(
                        ps.ap, wo.ap[:, c, d * 128:(d + 1) * 128], cat.ap[:, c, cols],
                        start=(c == 0), stop=(c == 7)), r=[wo.b(d // 2)], w=[ps.b()], sig=(c == 7))
                if pend:
                    pend.pop()()
                s_ = sqd[k % 3]
                P.op("act", lambda e, ps=ps, s_=s_: e.activation(out=s_.ap, in_=ps.ap, func=AF.Square),
                     r=[ps.b()], w=[s_.b()])
                copy_op("dve", ysb.ap[:, d, :], ps.ap, [ps.b()], [ysb.b(d)])
                if todo:
                    todo.pop(0)()

                def fin(ssq=ssq, s_=s_, d=d, tl=tl, ysb=ysb):
                    P.op("pe", lambda e: e.matmul(ssq.ap, ones_bf.ap, s_.ap, start=(d == 0), stop=(d == 7)),
                         r=[s_.b(), ones_bf.b()], w=[ssq.b()], sig=True)
                    if d == 7:
                        while todo:
                            todo.pop(0)()
                        todo.extend(gated_residual_pieces(x, tl, ysb, ssq, l, w_, 2, (None, rsl[tl % 2], t1)))
                pend.append(fin)
        if pend:
            pend.pop()()
        while todo:
            todo.pop(0)()
        phase_sync()
        A.release(m)

    def odd_mixer(x, scr, NT, nseq, L, w_, is_sample):
        T = NT * 512
        nchunk = T // 128
        cps = L // 128
        m = A.mark()
        hA = A.carve(scr, 0, 8 * 512, 8, "h1")
        cat = A.carve(scr, 8 * T, 16 * T, 8, "cat1")
        u = A.t("u", [nchunk, 1024], BF16)
        m1 = A.mark()
        if is_sample:
            wi = A.carve(scr, 4096, 4096 + 8192, 8, "wi")
        else:
            wi = A.t("wi", [8, 1024], BF16)
        ws = WStream(8, 128 if is_sample else 256, 2, "wio")
        for i in range(4):
            ws.load(wi, wi.ap[:, :, i * 256:(i + 1) * 256], wino_d[i], 8, 256, wi.b(i))
        sq = A.t("m_sq", [8, 512], BF16)
        rs = A.t("m_rs", [512], F32)
        t1 = [A.t("m_t%d" % i, [512], F32) for i in range(2)]
        tmp = (sq, rs, t1)
        k = 0
        if is_sample:
            hB = A.carve(scr, 12288, 16384, 8, "h1b")
        else:
            hB = A.carve(scr, 4096, 8192, 8, "h1b")
        hAB = [hA, hB]
        norm_mod(x, hAB[0], [0], 1, w_, 0, 1, tmp)
        for tl in range(NT):
            hcur = hAB[tl % 2]
            if tl + 1 < NT:
                norm_mod(x, hAB[(tl + 1) % 2], [tl + 1], 1, w_, 0, 1, tmp)
            for cq in range(4):
                ck = tl * 4 + cq
                for half in range(2):
                    ps = PS[2 + (k % 6)]
                    k += 1
                    for c in range(8):
                        P.op("pe", lambda e, ps=ps, c=c, cq=cq, half=half, hcur=hcur: e.matmul(
                            ps.ap, hcur.ap[:, c, cq * 128:(cq + 1) * 128], wi.ap[:, c, half * 512:(half + 1) * 512],
                            start=(c == 0), stop=(c == 7)),
                            r=[hcur.b(0), wi.b(half * 2), wi.b(half * 2 + 1)], w=[ps.b()], sig=(c == 7))
                    copy_op(ev_eng(), u.ap[:, ck, half * 512:(half + 1) * 512], ps.ap, [ps.b()], [u.b(ck)])
        phase_sync()
        A.release(m1)
        if debug and not is_sample:
            dbg_dump("u_p", u.ap, [], [128, nchunk, 1024])
            phase_sync()
        if is_sample:
            tabc = A.carve(scr, 0, 8192, 16, "tabc")
            tabs = A.carve(scr, 8192, 16384, 16, "tabs")
        pd = [A.t("pd%d" % i, [512], BF16) for i in range(2)]
        cu = [A.t("cu%d" % i, [512], BF16) for i in range(2)]
        su = [A.t("su%d" % i, [512], BF16) for i in range(2)]
        fr = [A.t("fr%d" % i, [512], BF16) for i in range(2)]
        ubufs = [u.b(i) for i in range(nchunk)]
        items = [(tl, g) for tl in range(NT) for g in range(4)]
        nit = len(items)

        def stage_A(i):
            tl, g = items[i]
            if g == 0 and is_sample:
                dma("sp", tabc.ap, dft2kc_d[tl].rearrange("p (a b) -> p a b", a=16), w=[tabc.b()])
                dma("sp", tabs.ap, dft2ks_d[tl].rearrange("p (a b) -> p a b", a=16), w=[tabs.b()])
            ps = PS[0 + i % 2]
            pc = PS[2 + i % 2]
            pz = PS[4 + i % 2]
            for bq in range(4):
                ck = tl * 4 + bq
                pos = ck % cps
                kind = 0 if pos == 0 else (2 if pos == cps - 1 else 1)
                terms = [(ck, kind)]
                if pos > 0:
                    terms.append((ck - 1, 3))
                if pos < cps - 1:
                    terms.append((ck + 1, 4))
                for j_, (cs, kd) in enumerate(terms):
                    P.op("pe", lambda e, ps=ps, bq=bq, cs=cs, kd=kd, g=g, j_=j_, n=len(terms): e.matmul(
                        ps.ap[:, bq * 128:(bq + 1) * 128], u.ap[:, cs, g * 128:(g + 1) * 128],
                        band.ap[:, g * 5 + kd, :], start=(j_ == 0), stop=(j_ == n - 1)),
                        r=ubufs + [band.b()], w=[ps.b()], sig=(bq == 3 and j_ == len(terms) - 1))
            pd_ = pd[i % 2]
            copy_op("act", pd_.ap, ps.ap, [ps.b()], [pd_.b()])
            if is_sample:
                for a_ in range(16):
                    P.op("pe", lambda e, pc=pc, a_=a_, g=g: e.matmul(
                        pc.ap, u.ap[:, a_, 512 + g * 128: 512 + (g + 1) * 128], tabc.ap[:, a_, :],
                        start=(a_ == 0), stop=(a_ == 15)), r=ubufs + [tabc.b()], w=[pc.b()], sig=(a_ == 15))
                for a_ in range(16):
                    P.op("pe", lambda e, pz=pz, a_=a_, g=g: e.matmul(
                        pz.ap, u.ap[:, a_, 512 + g * 128: 512 + (g + 1) * 128], tabs.ap[:, a_, :],
                        start=(a_ == 0), stop=(a_ == 15)), r=ubufs + [tabs.b()], w=[pz.b()], sig=(a_ == 15))
            else:
                for s_i in range(2):
                    for a_ in range(2):
                        ck = tl * 4 + s_i * 2 + a_
                        P.op("pe", lambda e, pc=pc, a_=a_, g=g, s_i=s_i, ck=ck: e.matmul(
                            pc.ap[:, s_i * 256:(s_i + 1) * 256], u.ap[:, ck, 512 + g * 128: 512 + (g + 1) * 128],
                            dft256.ap[:, a_, :], start=(a_ == 0), stop=(a_ == 1)),
                            r=ubufs + [dft256.b()], w=[pc.b()], sig=(a_ == 1 and s_i == 1))
                for s_i in range(2):
                    for a_ in range(2):
                        ck = tl * 4 + s_i * 2 + a_
                        P.op("pe", lambda e, pz=pz, a_=a_, g=g, s_i=s_i, ck=ck: e.matmul(
                            pz.ap[:, s_i * 256:(s_i + 1) * 256], u.ap[:, ck, 512 + g * 128: 512 + (g + 1) * 128],
                            dft256.ap[:, 2 + a_, :], start=(a_ == 0), stop=(a_ == 1)),
                            r=ubufs + [dft256.b()], w=[pz.b()], sig=(a_ == 1 and s_i == 1))
            copy_op("act", cu[i % 2].ap, pc.ap, [pc.b()], [cu[i % 2].b()])
            copy_op("dve", su[i % 2].ap, pz.ap, [pz.b()], [su[i % 2].b()])

        def stage_B(i):
            tl, g = items[i]
            cols = slice(tl * 512, (tl + 1) * 512)
            cu_, su_, pd_, fr_ = cu[i % 2], su[i % 2], pd[i % 2], fr[i % 2]
            pf = PS[7]
            P.op("pe", lambda e: e.matmul(pf.ap, dftc.ap[:, 0:128], cu_.ap, start=True, stop=False),
                 r=[dftc.b(), cu_.b()], w=[pf.b()], sig=False)
            P.op("pe", lambda e: e.matmul(pf.ap, dftc.ap[:, 128:256], su_.ap, start=False, stop=True),
                 r=[dftc.b(), su_.b()], w=[pf.b()])
            copy_op("act", fr_.ap, pf.ap, [pf.b()], [fr_.b()])
            ps2 = PS[6]
            P.op("pe", lambda e: e.matmul(ps2.ap, wpool.ap[:, g, :], pd_.ap, start=True, stop=True),
                 r=[wpool.b(), pd_.b()], w=[ps2.b()])
            copy_op("dve", cat.ap[:, g, cols], ps2.ap, [ps2.b()], [cat.b((tl, g))],
                    scale=vT.ap[:, 76 + g:77 + g])

        def stage_C(i):
            tl, g = items[i]
            cols = slice(tl * 512, (tl + 1) * 512)
            fr_ = fr[i % 2]
            pw = PS[6]
            P.op("pe", lambda e: e.matmul(pw.ap, wfour.ap[:, g, :], fr_.ap, start=True, stop=True),
                 r=[wfour.b(), fr_.b()], w=[pw.b()])
            copy_op("dve", cat.ap[:, 4 + g, cols], pw.ap, [pw.b()], [cat.b((tl, 4 + g))])

        for idx in range(nit + 2):
            if idx < nit:
                stage_A(idx)
            if idx >= 2:
                stage_C(idx - 2)
            if 1 <= idx <= nit:
                stage_B(idx - 1)
        phase_sync()
        A.release(m)
        if debug and not is_sample:
            dbg_dump("cat1_p", cat.ap, [], [128, 8, T])
            phase_sync()
        out_proj_residual(x, cat, NT, wouto_d, 1, w_)

    def run_group(xd, yd, NT, nseq, L, w_, is_sample):
        T = NT * 512
        mg = A.mark()
        if not is_sample:
            wms1 = [A.t("wmsB%d" % i, [8, 512], BF16) for i in range(3)]
            rowblk1 = A.t("rowblkB", [512], F32, parts=2)
            bg.extend(mod_steps(1, wms1, rowblk1))
        scr = A.t("scr", [max(16 * T, 30 * 1024)], BF16)
        x = A.t("x", [8, T], F32)
        load_x(xd, x, T)
        if debug and not is_sample:
            dbg_dump("x0p", x.ap, [x.b(i) for i in range(NT)], [128, 8, T])
        for l in range(2):
            if l == 0:
                even_mixer(x, scr, NT, nseq, L, w_, is_sample,
                           nk_out=None if is_sample else nk_d, nv_out=None if is_sample else nv_d)
            else:
                bg_flush()
                odd_mixer(x, scr, NT, nseq, L, w_, is_sample)
            if stage <= 3 + 3 * l:
                break
            if debug and not is_sample:
                dbg_dump("xmid%d_p" % l, x.ap, [x.b(i) for i in range(NT)], [128, 8, T])
            h2p = None
            for h0 in range(0, NT, 2):
                tiles = list(range(h0, min(h0 + 2, NT)))
                nxt = list(range(h0 + 2, min(h0 + 4, NT)))
                h2p = ffn(x, tiles, l, w_, scr, h2_pre=h2p, next_tiles=nxt or None)
            if debug and not is_sample:
                dbg_dump("xl%d_p" % l, x.ap, [x.b(i) for i in range(NT)], [128, 8, T])
            if stage <= 4 + 3 * l:
                break
        store_x(yd, x, T)
        phase_sync()
        A.release(mg)

    if stage >= 1:
        run_group(xp_d, yp_d, 2, 4, 256, 0, False)
    if stage >= 50:
        run_group(xs_d, ys_d, 4, 1, 2048, 1, True)

    P.barrier()

    with nc.Block() as block:
        @block.tensor
        def _(e):
            P.replay("pe", e)

        @block.scalar
        def _(e):
            P.replay("act", e)

        @block.vector
        def _(e):
            P.replay("dve", e)

        @block.gpsimd
        def _(e):
            P.replay("pool", e)

        @block.sync
        def _(e):
            P.replay("sp", e)
    es.close()
    return nc, dbg, {"nops": P.nops, "peak": A.peak, "marks": P.marks}


_PROG = {}


def _get_prog(debug, stage):
    key = (debug, stage)
    if key not in _PROG:
        _PROG[key] = build_program(debug, stage)
    return _PROG[key]


def kernel(x_prompt, x_sample, cache_k, cache_v, c, c_ctx, w_mod, b_mod, norm_g,
           w_in_even, lam_params, subln_g, conv_w, w_out_even,
           w_in_odd, w_pool, pool_scale, w_fourier, w_out_odd,
           w_gate, w_up, w_down, _debug=False, _stage=99):
    global _CONST
    if _CONST is None:
        _CONST = make_constants()
    f = lambda a: np.ascontiguousarray(np.asarray(a, dtype=np.float32))
    x_prompt, x_sample, cache_k, cache_v = f(x_prompt), f(x_sample), f(cache_k), f(cache_v)
    c, c_ctx, b_mod, norm_g = f(c), f(c_ctx), f(b_mod), f(norm_g)
    nc, dbg, info = _get_prog(_debug, _stage)
    def slabs(W, ncols):
        K, N = W.shape
        return np.ascontiguousarray(
            W.reshape(K // 128, 128, N // ncols, ncols).transpose(2, 1, 0, 3).reshape(N // ncols, 128, (K // 128) * ncols))

    shared = {
        "w_mod": np.stack([slabs(f(w_mod)[l], 512) for l in range(2)]),
        "w_in_even": slabs(f(w_in_even)[0], 128), "w_out_even": slabs(f(w_out_even)[0], 256),
        "w_in_odd": slabs(f(w_in_odd)[0], 256), "w_out_odd": slabs(f(w_out_odd)[0], 256),
        "w_pool": f(w_pool)[0], "w_fourier": f(w_fourier)[0],
        "w_gate": np.stack([slabs(f(w_gate)[l], 256) for l in range(2)]),
        "w_up": np.stack([slabs(f(w_up)[l], 256) for l in range(2)]),
        "w_down": np.stack([slabs(f(w_down)[l], 128) for l in range(2)]),
        "lamp": f(lam_params).reshape(1, 256),
    }
    shared.update(_CONST)
    bpack = np.zeros((128, 128), np.float32)
    bpack[0:96] = b_mod.reshape(96, 128)
    in_maps = []
    for i in range(NCORES):
        vpack = np.zeros((128, 128), np.float32)
        vpack[0:64] = norm_g.reshape(64, 128)
        vpack[64:76] = f(conv_w)[0].reshape(12, 128)
        vpack[76:80] = f(pool_scale)[0].reshape(4, 128)
        vpack[80] = f(subln_g)[0]
        vpack[81:89] = c[i].reshape(8, 128)
        vpack[89:97] = c_ctx.reshape(8, 128)
        mp = dict(shared)
        mp.update({
            "xp": x_prompt[4 * i:4 * i + 4].reshape(1024, D),
            "xs": x_sample[i],
            "ck": cache_k[i, 0], "cv": cache_v[i, 0],
            "vpack": vpack, "bpack": bpack,
        })
        in_maps.append(mp)
    res = run_bass_kernel_spmd(nc, in_maps, core_ids=list(range(NCORES)))
    rs = res.results
    yp = np.concatenate([r["yp"].reshape(4, 256, D) for r in rs], axis=0)
    ys = np.stack([r["ys"] for r in rs], axis=0)
    nk = np.concatenate([r["nk"] for r in rs], axis=0)[:, None]
    nv = np.concatenate([r["nv"] for r in rs], axis=0)[:, None]
    if _debug:
        return (yp, ys, nk, nv), [{k: r["dbg_" + k] for k in dbg} for r in rs]
    return (yp.astype(np.float32), ys.astype(np.float32), nk.astype(np.float32), nv.astype(np.float32))
```
